# Optimizing a Trainium2 kernel written in Bass

```python
import math
import jax
import jax.numpy as jnp
from jax import lax
import numpy as np

D_MODEL = 2048
BATCH = 4
SEQ = 2048
DEPTH = 4
DEC_BATCH = 128
DEC_SEQ = 4
PAST_LEN = 16384
PAGE_SIZE = 128

RET_HEADS = 4
RET_DK = 128
RET_DV = 256
RET_CHUNK = 64
ROPE_BASE = 10000.0
GDN_HEADS = 8
GDN_DK = 128
GDN_DV = 128
GDN_CONV = 4
GDN_CHUNK = 64
GDN_QKV = GDN_HEADS * (2 * GDN_DK + GDN_DV)
GLA_HEADS = 4
GLA_DK = 128
GLA_DV = 256
GLA_LOWRANK = 16
GLA_GATE_NORM = 16.0
GLA_CHUNK = 16
N_BRANCH = 3
BRANCH_WIDTH = RET_HEADS * RET_DV
D_FF = -(-8 * D_MODEL // (3 * 256)) * 256
EPS = 1e-6
STATE_SCALE = 0.5

IN_SIZES = [RET_HEADS * RET_DK, RET_HEADS * RET_DK, RET_HEADS * RET_DV, RET_HEADS * RET_DV,
            GDN_QKV, GDN_HEADS, GDN_HEADS, GDN_HEADS * GDN_DV,
            GLA_HEADS * GLA_DK, GLA_HEADS * GLA_DK, GLA_HEADS * GLA_DV, GLA_LOWRANK, GLA_HEADS * GLA_DV,
            N_BRANCH * D_MODEL]
IN_OFFSETS = [sum(IN_SIZES[:i]) for i in range(1, len(IN_SIZES))]
D_IN = sum(IN_SIZES)

kernel_name = "hybrid_retention_gdn_gla_gated_merge_step"


def rmsnorm(x, g=None):
    xf = x.astype(jnp.float32)
    y = xf * lax.rsqrt(jnp.mean(xf * xf, axis=-1, keepdims=True) + EPS)
    if g is not None:
        y = y * g.astype(jnp.float32)
    return y.astype(x.dtype)


def l2norm(t):
    return t * lax.rsqrt(jnp.sum(t * t, axis=-1, keepdims=True) + EPS)


def heads(t, n):
    return t.reshape(t.shape[:2] + (n, t.shape[-1] // n))


def to_chunks(t, c):
    b, l = t.shape[:2]
    t = t.reshape((b, l // c, c) + t.shape[2:])
    return jnp.transpose(t, (1, 0, 3, 2) + tuple(range(4, t.ndim)))


def from_chunks(o):
    nc, b, h, c, d = o.shape
    return jnp.transpose(o, (1, 0, 3, 2, 4)).reshape(b, nc * c, h, d)


def rotary(t, pos0):
    l, d = t.shape[1], t.shape[-1]
    inv = 1.0 / (ROPE_BASE ** jnp.linspace(0.0, 1.0, d // 2, dtype=jnp.float32))
    ang = (jnp.arange(l, dtype=jnp.float32) + pos0)[:, None] * inv[None, :]
    cos = jnp.cos(ang)[None, :, None, :]
    sin = jnp.sin(ang)[None, :, None, :]
    t1, t2 = t[..., 0::2], t[..., 1::2]
    return jnp.stack([t1 * cos - t2 * sin, t1 * sin + t2 * cos], axis=-1).reshape(t.shape)


def retention(q, k, v, state0, pos0):
    l = q.shape[1]
    c = math.gcd(l, RET_CHUNK)
    q = rotary(q.astype(jnp.float32), pos0)
    k = rotary(k.astype(jnp.float32), pos0) * (RET_DK ** -0.5)
    v = v.astype(jnp.float32)
    lg = jnp.log1p(-(2.0 ** (-5.0 - jnp.arange(RET_HEADS, dtype=jnp.float32))))
    idx = jnp.arange(c, dtype=jnp.float32)
    diff = idx[:, None] - idx[None, :]
    d_intra = jnp.where(diff >= 0, jnp.exp(lg[:, None, None] * jnp.maximum(diff, 0.0)), 0.0)
    d_q = jnp.exp(lg[:, None] * (idx + 1.0))[..., None]
    d_k = jnp.exp(lg[:, None] * (c - 1.0 - idx))[..., None]
    d_c = jnp.exp(lg * c)[:, None, None]

    def step(s, xs):
        qc, kc, vc = xs
        scores = jnp.einsum('bhid,bhjd->bhij', qc, kc) * d_intra
        o = (jnp.einsum('bhij,bhje->bhie', scores, vc)
             + jnp.einsum('bhid,bhde->bhie', qc * d_q, s))
        s = d_c * s + jnp.einsum('bhjd,bhje->bhde', kc * d_k, vc)
        return s, o

    s, o = lax.scan(step, state0.astype(jnp.float32), (to_chunks(q, c), to_chunks(k, c), to_chunks(v, c)))
    return from_chunks(o), s


def gated_delta(q, k, v, g, beta, state0):
    l = q.shape[1]
    c = math.gcd(l, GDN_CHUNK)
    incl = jnp.tril(jnp.ones((c, c), dtype=bool))
    strict = jnp.tril(jnp.ones((c, c), dtype=bool), -1)
    eye = jnp.eye(c, dtype=jnp.float32)

    def step(s, xs):
        qc, kc, vc, gc, bc = xs
        cum = jnp.cumsum(gc, axis=-1)
        dec = jnp.exp(jnp.where(incl, cum[..., :, None] - cum[..., None, :], -jnp.inf))
        kb = kc * bc[..., None]
        lower = jnp.where(strict, jnp.einsum('bhid,bhjd->bhij', kb, kc) * dec, 0.0)
        rhs = jnp.concatenate([vc * bc[..., None], kb * jnp.exp(cum)[..., None]], axis=-1)
        sol = lax.linalg.triangular_solve(lower + eye, rhs, left_side=True, lower=True, unit_diagonal=True)
        u, w = sol[..., :GDN_DV], sol[..., GDN_DV:]
        v_new = u - jnp.einsum('bhid,bhde->bhie', w, s)
        o = (jnp.einsum('bhid,bhde->bhie', qc * jnp.exp(cum)[..., None], s)
             + jnp.einsum('bhij,bhje->bhie', jnp.einsum('bhid,bhjd->bhij', qc, kc) * dec, v_new))
        s = (jnp.exp(cum[..., -1])[..., None, None] * s
             + jnp.einsum('bhjd,bhje->bhde', kc * jnp.exp(cum[..., -1:] - cum)[..., None], v_new))
        return s, o

    xs = (to_chunks(q, c), to_chunks(k, c), to_chunks(v, c), to_chunks(g, c), to_chunks(beta, c))
    s, o = lax.scan(step, state0.astype(jnp.float32), xs)
    return from_chunks(o), s


def gla(q, k, v, gk, state0):
    l = q.shape[1]
    c = math.gcd(l, GLA_CHUNK)
    mask = jnp.tril(jnp.ones((c, c), dtype=bool))[:, :, None]

    def step(s, xs):
        qc, kc, vc, gc = xs
        b = jnp.cumsum(gc, axis=2)
        rel = jnp.exp(jnp.where(mask, b[:, :, :, None, :] - b[:, :, None, :, :], -jnp.inf))
        scores = jnp.einsum('bhid,bhijd,bhjd->bhij', qc, rel, kc)
        o = (jnp.einsum('bhij,bhje->bhie', scores, vc)
             + jnp.einsum('bhid,bhde->bhie', qc * jnp.exp(b), s))
        b_last = b[:, :, -1:, :]
        s = (jnp.exp(b_last[:, :, 0, :, None]) * s
             + jnp.einsum('bhjd,bhje->bhde', kc * jnp.exp(b_last - b), vc))
        return s, o

    xs = (to_chunks(q, c), to_chunks(k, c), to_chunks(v, c), to_chunks(gk, c))
    s, o = lax.scan(step, state0.astype(jnp.float32), xs)
    return from_chunks(o), s


def mixer_block(h, pos0, s_ret, s_gdn, s_conv, s_gla, w_in, b_merge, conv_w, a_log, dt_bias,
                gdn_norm, gla_w_up, gla_b_up, gla_norm, w_branch, w_o):
    b, l, _ = h.shape
    f32 = jnp.float32
    (rq, rk, rv, rg, dqkv, da, db, dz, lq, lk, lv, llr, lgt, mg) = jnp.split(h @ w_in, IN_OFFSETS, axis=-1)

    o, new_ret = retention(heads(rq, RET_HEADS), heads(rk, RET_HEADS), heads(rv, RET_HEADS), s_ret, pos0)
    y_ret = (rmsnorm(o).reshape(b, l, -1) * jax.nn.silu(rg.astype(f32))).astype(h.dtype)

    xcat = jnp.concatenate([s_conv.astype(h.dtype), dqkv], axis=1)
    conv = xcat[:, 0:l] * conv_w[0]
    for w in range(1, GDN_CONV):
        conv = conv + xcat[:, w:w + l] * conv_w[w]
    conv = jax.nn.silu(conv.astype(f32))
    new_conv = xcat[:, -(GDN_CONV - 1):]
    gq, gk, gv = jnp.split(conv, [GDN_HEADS * GDN_DK, 2 * GDN_HEADS * GDN_DK], axis=-1)
    gq = l2norm(heads(gq, GDN_HEADS)) * (GDN_DK ** -0.5)
    gk = l2norm(heads(gk, GDN_HEADS))
    gdec = -jnp.exp(a_log.astype(f32)) * jax.nn.softplus(da.astype(f32) + dt_bias.astype(f32))
    beta = jax.nn.sigmoid(db.astype(f32))
    o, new_gdn = gated_delta(gq, gk, heads(gv, GDN_HEADS), gdec, beta, s_gdn)
    y_gdn = (rmsnorm(o, gdn_norm) * jax.nn.silu(heads(dz, GDN_HEADS).astype(f32))).reshape(b, l, -1).astype(h.dtype)

    gk_log = jax.nn.log_sigmoid(llr.astype(f32) @ gla_w_up.astype(f32) + gla_b_up.astype(f32)) / GLA_GATE_NORM
    o, new_gla = gla(heads(lq, GLA_HEADS).astype(f32) * (GLA_DK ** -0.5), heads(lk, GLA_HEADS).astype(f32),
                     heads(lv, GLA_HEADS).astype(f32), heads(gk_log, GLA_HEADS), s_gla)
    y_gla = (rmsnorm(o, gla_norm) * jax.nn.silu(heads(lgt, GLA_HEADS).astype(f32))).reshape(b, l, -1).astype(h.dtype)

    br = jnp.stack([y_ret, y_gdn, y_gla], axis=2)
    br = jnp.einsum('blnc,ncd->blnd', br, w_branch)
    gates = jax.nn.sigmoid(mg + b_merge).reshape(b, l, N_BRANCH, D_MODEL)
    out = jnp.sum(gates * br, axis=2) @ w_o
    return out, (new_ret, new_gdn, new_conv, new_gla)


def run_trunk(x, pos0, s_ret, s_gdn, s_conv, s_gla, norm_mix, norm_ffn, norm_final, w_in, b_merge,
              gdn_conv_w, gdn_a_log, gdn_dt_bias, gdn_norm, gla_w_up, gla_b_up, gla_norm,
              w_branch, w_o, w_gate_up, w_down):
    b = x.shape[0]
    new = ([], [], [], [])
    for layer in range(DEPTH):
        if s_ret is None:
            st = (jnp.zeros((b, RET_HEADS, RET_DK, RET_DV), x.dtype),
                  jnp.zeros((b, GDN_HEADS, GDN_DK, GDN_DV), x.dtype),
                  jnp.zeros((b, GDN_CONV - 1, GDN_QKV), x.dtype),
                  jnp.zeros((b, GLA_HEADS, GLA_DK, GLA_DV), x.dtype))
        else:
            st = (s_ret[layer], s_gdn[layer], s_conv[layer], s_gla[layer])
        h = rmsnorm(x, norm_mix[layer])
        mix, layer_states = mixer_block(h, pos0, st[0], st[1], st[2], st[3], w_in[layer], b_merge[layer],
                                        gdn_conv_w[layer], gdn_a_log[layer], gdn_dt_bias[layer], gdn_norm[layer],
                                        gla_w_up[layer], gla_b_up[layer], gla_norm[layer], w_branch[layer], w_o[layer])
        x = x + mix.astype(x.dtype)
        h = rmsnorm(x, norm_ffn[layer])
        gate, up = jnp.split(h @ w_gate_up[layer], 2, axis=-1)
        x = x + ((jax.nn.silu(gate) * up) @ w_down[layer]).astype(x.dtype)
        for lst, s in zip(new, layer_states):
            lst.append(s.astype(x.dtype))
    y = rmsnorm(x, norm_final)
    return y, [jnp.stack(lst) for lst in new]


def setup_inputs(seed: int = 0) -> dict:
    key = jax.random.key(seed)
    ks = jax.random.split(key, 24)
    f32 = jnp.float32
    nrm = lambda k, shape, scale: jax.random.normal(k, shape, f32) * scale
    dt = jnp.exp(jax.random.uniform(ks[10], (DEPTH, GDN_HEADS), f32, math.log(1e-3), math.log(0.1)))
    return {
        "x_prompt": nrm(ks[0], (BATCH, SEQ, D_MODEL), 1.0),
        "x_sample": nrm(ks[1], (DEC_BATCH, DEC_SEQ, D_MODEL), 1.0),
        "state_ret": nrm(ks[2], (DEPTH, DEC_BATCH, RET_HEADS, RET_DK, RET_DV), STATE_SCALE),
        "state_gdn": nrm(ks[3], (DEPTH, DEC_BATCH, GDN_HEADS, GDN_DK, GDN_DV), STATE_SCALE),
        "state_gdn_conv": nrm(ks[4], (DEPTH, DEC_BATCH, GDN_CONV - 1, GDN_QKV), 1.0),
        "state_gla": nrm(ks[5], (DEPTH, DEC_BATCH, GLA_HEADS, GLA_DK, GLA_DV), STATE_SCALE),
        "norm_mix": 1.0 + nrm(ks[6], (DEPTH, D_MODEL), 0.02),
        "norm_ffn": 1.0 + nrm(ks[7], (DEPTH, D_MODEL), 0.02),
        "norm_final": 1.0 + nrm(ks[8], (D_MODEL,), 0.02),
        "w_in": nrm(ks[9], (DEPTH, D_MODEL, D_IN), D_MODEL ** -0.5),
        "b_merge": nrm(ks[11], (DEPTH, N_BRANCH * D_MODEL), 0.02),
        "gdn_conv_w": nrm(ks[12], (DEPTH, GDN_CONV, GDN_QKV), GDN_CONV ** -0.5),
        "gdn_a_log": jnp.log(jax.random.uniform(ks[13], (DEPTH, GDN_HEADS), f32, 1.0, 16.0)),
        "gdn_dt_bias": dt + jnp.log(-jnp.expm1(-dt)),
        "gdn_norm": 1.0 + nrm(ks[14], (DEPTH, GDN_DV), 0.02),
        "gla_w_up": nrm(ks[15], (DEPTH, GLA_LOWRANK, GLA_HEADS * GLA_DK), GLA_LOWRANK ** -0.5),
        "gla_b_up": nrm(ks[16], (DEPTH, GLA_HEADS * GLA_DK), 0.02),
        "gla_norm": 1.0 + nrm(ks[17], (DEPTH, GLA_DV), 0.02),
        "w_branch": nrm(ks[18], (DEPTH, N_BRANCH, BRANCH_WIDTH, D_MODEL), BRANCH_WIDTH ** -0.5),
        "w_o": nrm(ks[19], (DEPTH, D_MODEL, D_MODEL), D_MODEL ** -0.5),
        "w_gate_up": nrm(ks[20], (DEPTH, D_MODEL, 2 * D_FF), D_MODEL ** -0.5),
        "w_down": nrm(ks[21], (DEPTH, D_FF, D_MODEL), D_FF ** -0.5),
    }


def reference(x_prompt, x_sample, state_ret, state_gdn, state_gdn_conv, state_gla, norm_mix, norm_ffn,
              norm_final, w_in, b_merge, gdn_conv_w, gdn_a_log, gdn_dt_bias, gdn_norm, gla_w_up, gla_b_up,
              gla_norm, w_branch, w_o, w_gate_up, w_down):
    weights = (norm_mix, norm_ffn, norm_final, w_in, b_merge, gdn_conv_w, gdn_a_log, gdn_dt_bias, gdn_norm,
               gla_w_up, gla_b_up, gla_norm, w_branch, w_o, w_gate_up, w_down)
    y_prompt, (p_ret, p_gdn, p_conv, p_gla) = run_trunk(x_prompt, 0.0, None, None, None, None, *weights)
    y_sample, (s_ret, s_gdn, s_conv, s_gla) = run_trunk(x_sample, float(PAST_LEN), state_ret, state_gdn,
                                                        state_gdn_conv, state_gla, *weights)
    return (y_prompt, y_sample, p_ret, p_gdn, p_conv, p_gla, s_ret, s_gdn, s_conv, s_gla)
```

```python
import contextlib
import numpy as np
import concourse.bass as bass
import concourse.mybir as mybir
from concourse.bass_utils import run_bass_kernel_spmd

F32 = mybir.dt.float32
BF16 = mybir.dt.bfloat16
AF = mybir.ActivationFunctionType
ALU = mybir.AluOpType

SEM_LIMIT = 28000


class Buf:
    __slots__ = ("name", "t", "ap", "w", "r", "psum")

    def __init__(self, name, t=None):
        self.name = name
        self.psum = False
        self.t = t
        self.ap = t[:] if t is not None else None
        self.w = None
        self.r = {}


class Ctx:
    ENG = ("pe", "act", "dve", "pool", "sp")

    def __init__(self, nc):
        self.nc = nc
        self.es = contextlib.ExitStack()
        self.engs = {"pe": nc.tensor, "act": nc.scalar, "dve": nc.vector, "pool": nc.gpsimd, "sp": nc.sync}
        self.cur = {}
        self.waited = {e: {} for e in self.ENG}
        self.semid = 0
        self.dma_pool = []
        self.dma_rr = 0
        self.all_sems = []
        self.drams = {}
        self.psums = []
        self.ps_rr = 0
        self.ninst = 0

    def __enter__(self):
        self.es.__enter__()
        for e in self.ENG:
            self.cur[e] = self._newsem(e)
        for i in range(24):
            self.dma_pool.append(self._newsem())
        for i in range(8):
            t = self.es.enter_context(self.nc.psum_tensor(f"psb{i}", [128, 512], F32))
            self.psums.append(Buf(f"psb{i}", t))
            self.psums[-1].psum = True
        return self

    def __exit__(self, *a):
        return self.es.__exit__(*a)

    def _newsem(self, owner=None):
        s = self.es.enter_context(self.nc.semaphore(f"s{self.semid}"))
        self.semid += 1
        rec = [s, 0, self.semid, owner]
        self.all_sems.append(rec)
        return rec

    def sb(self, name, shape, dt, stack=None):
        self.nsb = getattr(self, "nsb", 0) + 1
        name = f"{name}_u{self.nsb}"
        stk = stack or self.es
        t = stk.enter_context(self.nc.sbuf_tensor(name, shape, dt))
        nb = int(np.prod(shape[1:])) * (2 if dt == BF16 else 4)
        self.live = getattr(self, "live", 0) + nb
        self.peak = max(getattr(self, "peak", 0), self.live)

        def _free(n=nb):
            self.live -= n
        stk.callback(_free)
        return Buf(name, t)

    def dram(self, name):
        if name not in self.drams:
            self.drams[name] = Buf(name)
        return self.drams[name]

    def psum(self):
        b = self.psums[self.ps_rr % 8]
        self.ps_rr += 1
        return b

    def _wait(self, e, tok):
        rec, val = tok
        key = rec[2]
        if e == "pe" and rec[3] == "pe":
            return
        if self.waited[e].get(key, 0) >= val:
            return
        self.waited[e][key] = val
        self.engs[e].wait_ge(rec[0], val)

    def _deps(self, e, reads, writes):
        toks = []
        for b in reads:
            if b.w is not None:
                toks.append(b.w)
        for b in writes:
            if b.w is not None:
                toks.append(b.w)
            toks.extend(b.r.values())
        for t in toks:
            self._wait(e, t)

    def _record(self, tok, reads, writes):
        for b in reads:
            key = tok[0][2]
            old = b.r.get(key)
            if old is None or old[1] < tok[1]:
                b.r[key] = tok
        for b in writes:
            b.w = tok
            b.r = {}

    def op(self, e, fn, reads=(), writes=()):
        pr = [b for b in reads if b.psum]
        if pr:
            reads = [b for b in reads if not b.psum]
            writes = list(writes) + pr
        self._deps(e, reads, writes)
        rec = self.cur[e]
        if rec[1] >= SEM_LIMIT:
            rec = self.cur[e] = self._newsem(e)
        inst = fn(self.engs[e])
        rec[1] += 1
        inst.then_inc(rec[0], 1)
        tok = (rec, rec[1])
        self._record(tok, reads, writes)
        self.ninst += 1
        return tok

    def dma(self, q, out, in_, reads=(), writes=(), **kw):
        self._deps(q, reads, writes)
        i = self.dma_rr % len(self.dma_pool)
        self.dma_rr += 1
        rec = self.dma_pool[i]
        if rec[1] >= SEM_LIMIT:
            rec = self.dma_pool[i] = self._newsem()
        if rec[1] > 0:
            self._wait(q, (rec, rec[1]))
        inst = self.engs[q].dma_start(out=out, in_=in_, **kw)
        rec[1] += 16
        inst.then_inc(rec[0], 16)
        tok = (rec, rec[1])
        self._record(tok, reads, writes)
        self.ninst += 1
        return tok

    def barrier(self):
        for e in self.ENG:
            for rec in self.all_sems:
                if rec[1] > 0:
                    self._wait(e, (rec, rec[1]))

    def finish(self):
        for rec in self.all_sems:
            if rec[1] > 0:
                self._wait("sp", (rec, rec[1]))
                self._wait("act", (rec, rec[1]))


D = 2048
TP = 2048
TS = 64
T = TP + TS
NT = 17
DEPTH = 4
DFF = 5632
EPS = 1e-6
TMW = 7424
FMW = 9216
NPK = 1616
NCORES = 8

_o = [0, 512, 1024, 2048, 3072, 6144, 6152, 6160, 7184, 7696, 8208, 9232, 9248, 10272, 16416]
TM_IDX = np.concatenate([np.arange(0, 3072), np.arange(6160, 7184), np.arange(7184, 9232),
                         np.arange(9248, 10272), np.arange(6144, 6160), np.arange(9232, 9248)])
FM_IDX = np.concatenate([np.arange(3072, 6144), np.arange(10272, 16416)])
C_RQK, C_RV, C_RG, C_DZ, C_LQK, C_LV, C_LGT, C_DAB, C_LLR = 0, 1024, 2048, 3072, 4096, 5120, 6144, 7168, 7184
PK_GMIX, PK_GFFN, PK_CONV, PK_BM, PK_ALOG, PK_DTB, PK_GDNN, PK_GLAN, PK_BUP, PK_WUP, PK_GFIN = \
    0, 16, 32, 128, 176, 184, 192, 320, 576, 1088, 1600


def _build_consts():
    items = {}
    i = np.arange(128)
    ident = np.eye(128, dtype=np.float64)
    items["ident"] = ident
    items["ones"] = np.ones((128, 128))
    mti = (i[:, None] <= i[None, :]).astype(np.float64)
    mts = (i[:, None] < i[None, :]).astype(np.float64)
    items["MTi_p"] = mti
    items["MTs_p"] = mts
    items["Mi_p"] = mti.T.copy()
    items["Ms_p"] = mts.T.copy()
    items["ALL_p"] = np.ones((128, 128))
    same = np.zeros((128, 128))
    same[:64, :64] = (i[:64, None] // 4 == i[None, :64] // 4)
    items["MTi_s"] = mti * same
    items["MTs_s"] = mts * same
    items["Mi_s"] = mti.T * same
    items["Ms_s"] = mts.T * same
    items["ALL_s"] = same
    ssp = np.zeros((128, 16)); ssp[:, 0] = 1.0
    items["SEL_p"] = ssp
    sss = np.zeros((128, 16)); sss[np.arange(64), np.arange(64) // 4] = 1.0
    items["SEL_s"] = sss
    bm = np.zeros((128, 16, 64))
    bm[:, np.arange(64) // 4, np.arange(64)] = 1.0
    items["BM"] = bm.reshape(128, 1024)
    gam = 1.0 - 2.0 ** (-5.0 - np.arange(4, dtype=np.float64))
    rdp = np.zeros((128, 4, 128)); rds = np.zeros((128, 4, 128))
    dqkp = np.zeros((128, 8)); dqks = np.zeros((128, 8))
    for h in range(4):
        rdp[:, h, :] = mti * gam[h] ** (-128.0)
        rds[:, h, :] = mti * same * gam[h] ** (-4.0)
        dqkp[:, h] = gam[h] ** (i + 1.0)
        dqkp[:, 4 + h] = gam[h] ** (127.0 - i) * 128 ** -0.5
        t = (i % 4).astype(np.float64)
        dqks[:, h] = gam[h] ** (t + 1.0)
        dqks[:, 4 + h] = gam[h] ** (3.0 - t) * 128 ** -0.5
    items["RD_p"] = rdp.reshape(128, 512)
    items["RD_s"] = rds.reshape(128, 512)
    items["DQK_p"] = dqkp
    items["DQK_s"] = dqks
    off = {}
    cols = []
    o = 0
    for name, a in items.items():
        off[name] = (o, a.shape[1])
        o += a.shape[1]
        cols.append(a)
    return off, o, np.concatenate(cols, axis=1).astype(np.float32), gam


CST_OFF, NCST, CST_ARR, GAMMA = _build_consts()


def blocks(total, step, t0=0):
    out = []
    t = 0
    while t < total:
        n = min(step, total - t)
        out.append((t0 + t, n))
        t += n
    return out


TOKB = blocks(TP, 512) + [(TP, TS)]


class Model:
    def __init__(self, nc, debug=False, phases=None, nl=DEPTH):
        self.nc = nc
        self.k = Ctx(nc)
        self.debug = debug
        self.phases = phases
        self.nl = nl
        d = self.dr = {}

        def di(name, shape, dt=F32):
            d[name] = nc.dram_tensor(name, list(shape), dt, kind="ExternalInput").ap()

        def do(name, shape, dt=F32):
            d[name] = nc.dram_tensor(name, list(shape), dt, kind="ExternalOutput").ap()

        def ds(name, shape, dt=F32):
            d[name] = nc.dram_tensor(name, list(shape), dt, kind="ExternalOutput" if debug else "Internal").ap()

        NLD = self.nl if debug else DEPTH
        di("xT", [D, T])
        di("wtm", [NLD, D, TMW]); di("wfm", [NLD, D, FMW])
        di("wbr", [NLD, 3, 1024, D]); di("wo", [NLD, D, D])
        di("wgu", [NLD, D, 2 * DFF]); di("wdn", [NLD, DFF, D])
        di("pack", [NLD, 128, NPK])
        di("cst", [128, NCST])
        di("cossin", [T, 128])
        di("sret", [NLD, 16, 4, 128, 256]); di("sgdn", [NLD, 16, 8, 128, 128])
        di("sgla", [NLD, 16, 4, 128, 256]); di("sconv", [NLD, 128, 24 * 16 * 3])
        do("yT", [D, T])
        do("o_pret", [NLD, 4, 128, 256]); do("o_pgdn", [NLD, 8, 128, 128])
        do("o_pconv", [NLD, 3072, 3]); do("o_pgla", [NLD, 4, 128, 256])
        do("o_sret", [NLD, 16, 4, 128, 256]); do("o_sgdn", [NLD, 16, 8, 128, 128])
        do("o_sconv", [NLD, 128, 24 * 16 * 3]); do("o_sgla", [NLD, 16, 4, 128, 256])
        ds("XR", [D, T]); ds("PTM", [T, TMW]); ds("PFM", [FMW, T])
        ds("YBT", [3072, T], BF16); ds("MT", [D, T], BF16); ds("FT", [DFF, T], BF16)

    def build(self):
        k = self.k
        with k:
            self.cst = k.sb("cst", [128, NCST], F32)
            k.dma("sp", self.cst.ap, self.dr["cst"], writes=[self.cst])
            self.pk = k.sb("pk", [128, NPK], F32)
            k.dma("sp", self.dr["XR"], self.dr["xT"])
            k.barrier()
            ph = self.phases or ("win", "ret", "gla", "gdn", "branch", "wo", "ffn", "final")
            for l in range(self.nl):
                k.dma("sp", self.pk.ap, self.dr["pack"][l], writes=[self.pk])
                k.barrier()
                for p in ("win", "ret", "gla", "gdn", "branch", "wo", "ffn"):
                    if p in ph:
                        getattr(self, "phase_" + p)(l)
            if "final" in ph:
                self.phase_final()
            k.finish()

    def C(self, name, rows=128):
        o, n = CST_OFF[name]
        return self.cst.t[0:rows, o:o + n]

    def rmsnorm_fm(self, st, gain_off, hT=None, dst=None):
        k = self.k
        src = self.dr["XR"].rearrange("(kc p) t -> p kc t", p=128)
        xs = [k.sb(f"nx{i}", [128, 16, 256], F32, st) for i in range(2)]
        sq = k.sb("nsq", [128, 16, 256], F32, st)
        r1 = k.sb("nr1", [128, 256], F32, st)
        r2 = k.sb("nr2", [128, 256], F32, st)
        r3 = k.sb("nr3", [128, 256], F32, st)
        oo = [k.sb(f"no{i}", [128, 16, 256], F32, st) for i in range(2)] if dst is not None else None
        ones = self.C("ones")
        for bi, (t0, n) in enumerate(blocks(T, 256)):
            xb = xs[bi % 2]
            k.dma("sp", xb.t[:, :, 0:n], src[:, :, t0:t0 + n], writes=[xb])
            k.op("act", lambda e: e.activation(sq.t[:, :, 0:n], xb.t[:, :, 0:n], AF.Square), reads=[xb], writes=[sq])
            ps = k.psum()
            for kc in range(16):
                k.op("pe", lambda e: e.matmul(ps.t[:, 0:n], ones, sq.t[:, kc, 0:n], start=(kc == 0), stop=(kc == 15)),
                     reads=[sq, self.cst], writes=[ps])
            k.op("dve", lambda e: e.tensor_scalar(r1.t[:, 0:n], ps.t[:, 0:n], 1.0 / D, EPS, ALU.mult, ALU.add),
                 reads=[ps], writes=[r1])
            k.op("act", lambda e: e.activation(r2.t[:, 0:n], r1.t[:, 0:n], AF.Sqrt), reads=[r1], writes=[r2])
            k.op("dve", lambda e: e.reciprocal(r3.t[:, 0:n], r2.t[:, 0:n]), reads=[r2], writes=[r3])
            for kc in range(16):
                g = self.pk.t[:, gain_off + kc:gain_off + kc + 1]
                if dst is None:
                    k.op("dve", lambda e: e.scalar_tensor_tensor(hT.t[:, kc, t0:t0 + n], xb.t[:, kc, 0:n], g,
                                                                 r3.t[:, 0:n], ALU.mult, ALU.mult),
                         reads=[xb, r3, self.pk], writes=[hT])
                else:
                    ob = oo[bi % 2]
                    k.op("dve", lambda e: e.scalar_tensor_tensor(ob.t[:, kc, 0:n], xb.t[:, kc, 0:n], g,
                                                                 r3.t[:, 0:n], ALU.mult, ALU.mult),
                         reads=[xb, r3, self.pk], writes=[ob])
            if dst is not None:
                ob = oo[bi % 2]
                k.dma("pool", dst.rearrange("(kc p) t -> p kc t", p=128)[:, :, t0:t0 + n], ob.t[:, :, 0:n], reads=[ob])

    def load_w_dma(self, wraw, src):
        self.k.dma("sp", wraw.ap, src, writes=[wraw])

    def load_w_cast(self, wraw, wbf, KCi):
        k = self.k
        h = KCi // 2
        k.op("act", lambda e: e.activation(wbf.t[:, 0:h, :], wraw.t[:, 0:h, :], AF.Copy), reads=[wraw], writes=[wbf])
        k.op("pool", lambda e: e.tensor_copy(wbf.t[:, h:KCi, :], wraw.t[:, h:KCi, :]), reads=[wraw], writes=[wbf])

    def load_w(self, wraw, wbf, src, KCi):
        k = self.k
        k.dma("sp", wraw.ap, src, writes=[wraw])
        h = KCi // 2
        k.op("act", lambda e: e.activation(wbf.t[:, 0:h, :], wraw.t[:, 0:h, :], AF.Copy), reads=[wraw], writes=[wbf])
        k.op("pool", lambda e: e.tensor_copy(wbf.t[:, h:KCi, :], wraw.t[:, h:KCi, :]), reads=[wraw], writes=[wbf])

    def dense_fm(self, st, inT, KCi, tokb, wsrc_fn, ncols, WN, epi, pre=None, tag="w", tbase=0):
        k = self.k
        wraw = [k.sb(f"{tag}raw{i}", [128, KCi, WN], F32, st) for i in range(2)]
        wbf = [k.sb(f"{tag}bf{i}", [128, KCi, WN], BF16, st) for i in range(2)]
        nw = ncols // WN
        self.load_w(wraw[0], wbf[0], wsrc_fn(0, WN), KCi)
        for wi in range(nw):
            wr, wb = wraw[wi % 2], wbf[wi % 2]
            if wi + 1 < nw:
                self.load_w_dma(wraw[(wi + 1) % 2], wsrc_fn((wi + 1) * WN, WN))
            for bi_, (t0, n) in enumerate(tokb):
                if wi + 1 < nw and bi_ == max(0, len(tokb) - 2):
                    self.load_w_cast(wraw[(wi + 1) % 2], wbf[(wi + 1) % 2], KCi)
                for sub in range(WN // 128):
                    c0 = wi * WN + sub * 128
                    if pre is not None:
                        pre(c0, t0, n)
                    ps = k.psum()
                    for kc in range(KCi):
                        k.op("pe", lambda e: e.matmul(ps.t[:, 0:n], wb.t[:, kc, sub * 128:(sub + 1) * 128],
                                                      inT.t[:, kc, t0 - tbase:t0 - tbase + n],
                                                      start=(kc == 0), stop=(kc == KCi - 1)),
                             reads=[wb, inT], writes=[ps])
                    epi(c0, t0, n, ps)

    def phase_win(self, l):
        k = self.k
        with contextlib.ExitStack() as st:
            hT = k.sb("hT", [128, 16, T], BF16, st)
            with contextlib.ExitStack() as st2:
                self.rmsnorm_fm(st2, PK_GMIX, hT=hT)
                k.barrier()
            with contextlib.ExitStack() as st2:
                wraw = [k.sb(f"tmraw{i}", [128, 16, 256], F32, st2) for i in range(2)]
                wbf = [k.sb(f"tmbf{i}", [128, 16, 256], BF16, st2) for i in range(2)]
                ost = [k.sb(f"tmo{i}", [128, 256], F32, st2) for i in range(4)]
                wsrc = self.dr["wtm"][l].rearrange("(kc p) n -> p kc n", p=128)
                cnt = 0
                self.load_w(wraw[0], wbf[0], wsrc[:, :, 0:256], 16)
                for ci in range(TMW // 256):
                    c0 = ci * 256
                    ncol = min(256, 7200 - c0)
                    wr, wb = wraw[ci % 2], wbf[ci % 2]
                    if ci + 1 < TMW // 256:
                        self.load_w_dma(wraw[(ci + 1) % 2], wsrc[:, :, c0 + 256:c0 + 512])
                    for tt in range(NT):
                        if ci + 1 < TMW // 256 and tt == 13:
                            self.load_w_cast(wraw[(ci + 1) % 2], wbf[(ci + 1) % 2], 16)
                        R = 128 if tt < 16 else 64
                        ps = k.psum()
                        for kc in range(16):
                            k.op("pe", lambda e: e.matmul(ps.t[0:R, 0:ncol], hT.t[:, kc, tt * 128:tt * 128 + R],
                                                          wb.t[:, kc, 0:ncol], start=(kc == 0), stop=(kc == 15)),
                                 reads=[wb, hT], writes=[ps])
                        ob = ost[cnt % 4]
                        if cnt % 2 == 0:
                            k.op("act", lambda e: e.activation(ob.t[0:R, 0:ncol], ps.t[0:R, 0:ncol], AF.Copy),
                                 reads=[ps], writes=[ob])
                        else:
                            k.op("dve", lambda e: e.tensor_copy(ob.t[0:R, 0:ncol], ps.t[0:R, 0:ncol]),
                                 reads=[ps], writes=[ob])
                        k.dma("pool" if cnt % 2 else "sp", self.dr["PTM"][tt * 128:tt * 128 + R, c0:c0 + ncol],
                              ob.t[0:R, 0:ncol], reads=[ob])
                        cnt += 1
                k.barrier()
            with contextlib.ExitStack() as st2:
                ost = [k.sb(f"fmo{i}", [128, 512], F32, st2) for i in range(4)]
                wsrc = self.dr["wfm"][l].rearrange("(kc p) n -> p kc n", p=128)
                cnt = [0]

                def epi(c0, t0, n, ps):
                    ob = ost[cnt[0] % 4]
                    if c0 < 3072:
                        if cnt[0] % 2 == 0:
                            k.op("act", lambda e: e.activation(ob.t[:, 0:n], ps.t[:, 0:n], AF.Copy), reads=[ps], writes=[ob])
                        else:
                            k.op("dve", lambda e: e.tensor_copy(ob.t[:, 0:n], ps.t[:, 0:n]), reads=[ps], writes=[ob])
                    else:
                        ch = (c0 - 3072) // 128
                        b = self.pk.t[:, PK_BM + ch:PK_BM + ch + 1]
                        k.op("act", lambda e: e.activation(ob.t[:, 0:n], ps.t[:, 0:n], AF.Sigmoid, bias=b),
                             reads=[ps, self.pk], writes=[ob])
                    k.dma("pool" if cnt[0] % 2 else "sp", self.dr["PFM"][c0:c0 + 128, t0:t0 + n], ob.t[:, 0:n], reads=[ob])
                    cnt[0] += 1

                self.dense_fm(st2, hT, 16, TOKB, lambda c, w: wsrc[:, :, c:c + w], FMW, 256, epi, tag="fm")
                k.barrier()

    def load_act_bf(self, dst, src_fm, KCi, t0, n):
        k = self.k
        v = src_fm.rearrange("(kc p) t -> p kc t", p=128)
        for (a, m) in blocks(KCi, 8):
            k.dma("sp", dst.t[:, a:a + m, 0:n], v[:, a:a + m, t0:t0 + n], writes=[dst])

    def phase_branch(self, l):
        k = self.k
        for (s0, sn) in blocks(T, 1056):
            with contextlib.ExitStack() as st:
                ybt = k.sb("ybt", [128, 24, 1056], BF16, st)
                self.load_act_bf(ybt, self.dr["YBT"], 24, s0, sn)
                wraw = [k.sb(f"brraw{i}", [128, 8, 256], F32, st) for i in range(3)]
                wbf = [[k.sb(f"brbf{j}_{i}", [128, 8, 256], BF16, st) for i in range(3)] for j in range(2)]
                gts = [k.sb(f"brg{i}", [128, 512], F32, st) for i in range(6)]
                tmp = [k.sb(f"brt{i}", [128, 512], F32, st) for i in range(6)]
                mo = [k.sb(f"brm{i}", [128, 512], BF16, st) for i in range(2)]
                cnt = 0
                for ci in range(D // 256):
                    for b in range(3):
                        src = self.dr["wbr"][l, b].rearrange("(kc p) n -> p kc n", p=128)[:, :, ci * 256:(ci + 1) * 256]
                        self.load_w(wraw[b], wbf[ci % 2][b], src, 8)
                    for (t0, n) in blocks(sn, 512, s0):
                        for sub in range(2):
                            c0 = ci * 256 + sub * 128
                            tl = []
                            for b in range(3):
                                wb = wbf[ci % 2][b]
                                gt = gts[(cnt * 3 + b) % 6]
                                r0 = 3072 + b * 2048 + c0
                                k.dma("pool", gt.t[:, 0:n], self.dr["PFM"][r0:r0 + 128, t0:t0 + n], writes=[gt])
                                ps = k.psum()
                                for kc in range(8):
                                    k.op("pe", lambda e: e.matmul(ps.t[:, 0:n], wb.t[:, kc, sub * 128:(sub + 1) * 128],
                                                                  ybt.t[:, b * 8 + kc, t0 - s0:t0 - s0 + n],
                                                                  start=(kc == 0), stop=(kc == 7)),
                                         reads=[wb, ybt], writes=[ps])
                                tb = tmp[(cnt * 3 + b) % 6]
                                k.op("dve", lambda e: e.tensor_tensor(tb.t[:, 0:n], ps.t[:, 0:n], gt.t[:, 0:n], ALU.mult),
                                     reads=[ps, gt], writes=[tb])
                                tl.append(tb)
                            k.op("pool", lambda e: e.tensor_tensor(tl[0].t[:, 0:n], tl[0].t[:, 0:n], tl[1].t[:, 0:n], ALU.add),
                                 reads=[tl[1], tl[0]], writes=[tl[0]])
                            m = mo[cnt % 2]
                            k.op("pool", lambda e: e.tensor_tensor(m.t[:, 0:n], tl[0].t[:, 0:n], tl[2].t[:, 0:n], ALU.add),
                                 reads=[tl[0], tl[2]], writes=[m])
                            k.dma("sp", self.dr["MT"][c0:c0 + 128, t0:t0 + n], m.t[:, 0:n], reads=[m])
                            cnt += 1
                k.barrier()

    def resid_epi(self, st, tag):
        k = self.k
        xin = [k.sb(f"{tag}xi{i}", [128, 512], F32, st) for i in range(4)]
        cnt = [0]
        cur = {}

        def pre(c0, t0, n):
            xb = xin[cnt[0] % 4]
            k.dma("pool", xb.t[:, 0:n], self.dr["XR"][c0:c0 + 128, t0:t0 + n], writes=[xb])
            cur["xb"] = xb

        def epi(c0, t0, n, ps):
            xb = cur["xb"]
            k.op("dve", lambda e: e.tensor_tensor(xb.t[:, 0:n], xb.t[:, 0:n], ps.t[:, 0:n], ALU.add), reads=[ps, xb], writes=[xb])
            k.dma("pool", self.dr["XR"][c0:c0 + 128, t0:t0 + n], xb.t[:, 0:n], reads=[xb])
            cnt[0] += 1

        return pre, epi

    def phase_wo(self, l):
        k = self.k
        with contextlib.ExitStack() as st:
            mt = k.sb("mt", [128, 16, T], BF16, st)
            self.load_act_bf(mt, self.dr["MT"], 16, 0, T)
            pre, epi = self.resid_epi(st, "wo")
            wsrc = self.dr["wo"][l].rearrange("(kc p) n -> p kc n", p=128)
            self.dense_fm(st, mt, 16, TOKB, lambda c, w: wsrc[:, :, c:c + w], D, 256, epi, pre=pre, tag="wo")
            k.barrier()

    def phase_ffn(self, l):
        k = self.k
        with contextlib.ExitStack() as st:
            hT = k.sb("h2T", [128, 16, T], BF16, st)
            with contextlib.ExitStack() as st2:
                self.rmsnorm_fm(st2, PK_GFFN, hT=hT)
                k.barrier()
            sg = [k.sb(f"sg{i}", [128, 512], F32, st) for i in range(2)]
            fo = [k.sb(f"fo{i}", [128, 512], BF16, st) for i in range(2)]
            cnt = [0]

            def epi(c0, t0, n, ps):
                ft = c0 // 256
                if (c0 // 128) % 2 == 0:
                    s_ = sg[cnt[0] % 2]
                    k.op("act", lambda e: e.activation(s_.t[:, 0:n], ps.t[:, 0:n], AF.Silu), reads=[ps], writes=[s_])
                else:
                    s_ = sg[cnt[0] % 2]
                    o_ = fo[cnt[0] % 2]
                    k.op("dve", lambda e: e.tensor_tensor(o_.t[:, 0:n], s_.t[:, 0:n], ps.t[:, 0:n], ALU.mult),
                         reads=[ps, s_], writes=[o_])
                    k.dma("pool", self.dr["FT"][ft * 128:(ft + 1) * 128, t0:t0 + n], o_.t[:, 0:n], reads=[o_])
                    cnt[0] += 1

            wsrc = self.dr["wgu"][l].rearrange("(kc p) n -> p kc n", p=128)
            self.dense_fm(st, hT, 16, TOKB, lambda c, w: wsrc[:, :, c:c + w], 2 * DFF, 256, epi, tag="gu")
            k.barrier()
        for (s0, sn) in blocks(T, 704):
            with contextlib.ExitStack() as st:
                ft = k.sb("ftT", [128, 44, 704], BF16, st)
                self.load_act_bf(ft, self.dr["FT"], 44, s0, sn)
                pre, epi = self.resid_epi(st, "dn")
                wsrc = self.dr["wdn"][l].rearrange("(kc p) n -> p kc n", p=128)
                self.dense_fm(st, ft, 44, blocks(sn, 512, s0), lambda c, w: wsrc[:, :, c:c + w], D, 128, epi, pre=pre,
                              tag="dn", tbase=s0)
                k.barrier()

    def phase_final(self):
        k = self.k
        with contextlib.ExitStack() as st:
            self.rmsnorm_fm(st, PK_GFIN, dst=self.dr["yT"])
            k.barrier()

    def transposes(self, src_fn, nch, dst, R, reads, evac_scale=None):
        k = self.k
        ident = self.C("ident")
        for g0 in range(0, nch, 4):
            m = min(4, nch - g0)
            ps = k.psum()
            for c in range(m):
                k.op("pe", lambda e: e.transpose(ps.t[:, c * 128:c * 128 + R], src_fn(g0 + c), ident[0:R, 0:R]),
                     reads=list(reads) + [self.cst], writes=[ps])
            pv = ps.t[:, :].rearrange("p (a b) -> p a b", a=4)[:, 0:m, 0:R]
            if (g0 // 4) % 2 == 0:
                k.op("act", lambda e: e.activation(dst.t[:, g0:g0 + m, 0:R], pv, AF.Copy), reads=[ps], writes=[dst])
            else:
                k.op("dve", lambda e: e.tensor_copy(dst.t[:, g0:g0 + m, 0:R], pv), reads=[ps], writes=[dst])

    def rstd_cols(self, ss, r1, r2, rstd, R, n, inv_n):
        k = self.k
        k.op("dve", lambda e: e.tensor_scalar(r1.t[0:R, 0:n], ss.t[0:R, 0:n], inv_n, EPS, ALU.mult, ALU.add), reads=[ss], writes=[r1])
        k.op("act", lambda e: e.activation(r2.t[0:R, 0:n], r1.t[0:R, 0:n], AF.Sqrt), reads=[r1], writes=[r2])
        k.op("dve", lambda e: e.reciprocal(rstd.t[0:R, 0:n], r2.t[0:R, 0:n]), reads=[r2], writes=[rstd])

    def phase_la(self, l, kind):
        k = self.k
        X = mybir.AxisListType.X
        ret = (kind == "ret")
        cqk, cv, cg = (C_RQK, C_RV, C_RG) if ret else (C_LQK, C_LV, C_LGT)
        s_in = self.dr["sret" if ret else "sgla"]
        s_out = self.dr["o_sret" if ret else "o_sgla"]
        p_out = self.dr["o_pret" if ret else "o_pgla"]
        ybase = 0 if ret else 2048
        PTM = self.dr["PTM"]
        with contextlib.ExitStack() as st:
            sb = lambda n, shp, dt=F32: k.sb(f"{kind}_{n}", shp, dt, st)
            qk = [sb(f"qk{i}", [128, 1024]) for i in range(2)]
            vv = [sb(f"v{i}", [128, 1024]) for i in range(2)]
            gg = [sb(f"g{i}", [128, 1024]) for i in range(2)]
            cs = [sb(f"cs{i}", [128, 128]) for i in range(2)]
            lr = [sb(f"lr{i}", [128, 16]) for i in range(2)]
            t1, t2, t3, t4 = [sb(f"t{i}", [128, 512]) for i in range(4)]
            qkr = sb("qkr", [128, 1024])
            qkd = sb("qkd", [128, 1024])
            qkT = sb("qkT", [128, 8, 128], BF16)
            kdb = sb("kdb", [128, 4, 128], BF16)
            vbf = sb("vbf", [128, 1024], BF16)
            sgt = sb("sgt", [128, 1024])
            scb = [sb(f"sc{i}", [128, 128], BF16) for i in range(2)]
            sqo = sb("sqo", [128, 256])
            y = sb("y", [128, 1024])
            yT = [sb(f"yT{i}", [128, 8, 128], BF16) for i in range(2)]
            ss = sb("ss", [128, 4]); r1 = sb("r1", [128, 4]); r2 = sb("r2", [128, 4]); rstd = sb("rstd", [128, 4])
            S = sb("S", [128, 4, 256]); Sbf = sb("Sbf", [128, 4, 256], BF16)
            Sall = sb("Sall", [128, 16, 256]); Sallbf = sb("Sallbf", [128, 16, 256], BF16)
            Snew = sb("Snew", [128, 16, 256])
            qbig = sb("qbig", [128, 16, 64], BF16); kbig = sb("kbig", [128, 16, 128], BF16)
            if not ret:
                z = sb("z", [128, 512]); ez = sb("ez", [128, 512]); lz = sb("lz", [128, 512])
                Bsb = sb("Bsb", [128, 512]); eb = sb("eb", [128, 512]); enb = sb("enb", [128, 512])
                dfb = sb("dfb", [128, 512]); ekd = sb("ekd", [128, 512])
                llrT = sb("llrT", [16, 128]); ebl = sb("ebl", [128, 4, 16])
            k.op("dve", lambda e: e.memset(S.ap, 0.0), writes=[S])
            k.op("pool", lambda e: e.memset(Sbf.ap, 0.0), writes=[Sbf])

            def load(tt):
                R = 128 if tt < 16 else 64
                r0 = tt * 128
                i = tt % 2
                k.dma("sp", qk[i].t[0:R, :], PTM[r0:r0 + R, cqk:cqk + 1024], writes=[qk[i]])
                k.dma("sp", vv[i].t[0:R, :], PTM[r0:r0 + R, cv:cv + 1024], writes=[vv[i]])
                k.dma("sp", gg[i].t[0:R, :], PTM[r0:r0 + R, cg:cg + 1024], writes=[gg[i]])
                if ret:
                    k.dma("sp", cs[i].t[0:R, :], self.dr["cossin"][r0:r0 + R, :], writes=[cs[i]])
                else:
                    k.dma("sp", lr[i].t[0:R, :], PTM[r0:r0 + R, C_LLR:C_LLR + 16], writes=[lr[i]])

            load(0)
            for tt in range(NT):
                if tt + 1 < NT:
                    load(tt + 1)
                R = 128 if tt < 16 else 64
                smp = tt == 16
                sfx = "_s" if smp else "_p"
                i = tt % 2
                qkb, vb, gb = qk[i], vv[i], gg[i]
                if ret:
                    x4 = qkb.t[0:R, :].rearrange("p (g d two) -> p g d two", g=8, two=2)
                    o4 = qkr.t[0:R, :].rearrange("p (g d two) -> p g d two", g=8, two=2)
                    x1, x2 = x4[:, :, :, 0], x4[:, :, :, 1]
                    cosb = cs[i].t[0:R, 0:64].unsqueeze(1).to_broadcast([R, 8, 64])
                    sinb = cs[i].t[0:R, 64:128].unsqueeze(1).to_broadcast([R, 8, 64])
                    v3 = lambda b: b.t[0:R, :].rearrange("p (g d) -> p g d", g=8)
                    k.op("dve", lambda e: e.tensor_tensor(v3(t1), x1, cosb, ALU.mult), reads=[qkb, cs[i]], writes=[t1])
                    k.op("pool", lambda e: e.tensor_tensor(v3(t2), x2, sinb, ALU.mult), reads=[qkb, cs[i]], writes=[t2])
                    k.op("dve", lambda e: e.tensor_tensor(o4[:, :, :, 0], v3(t1), v3(t2), ALU.subtract), reads=[t1, t2], writes=[qkr])
                    k.op("pool", lambda e: e.tensor_tensor(v3(t3), x1, sinb, ALU.mult), reads=[qkb, cs[i]], writes=[t3])
                    k.op("dve", lambda e: e.tensor_tensor(v3(t4), x2, cosb, ALU.mult), reads=[qkb, cs[i]], writes=[t4])
                    k.op("pool", lambda e: e.tensor_tensor(o4[:, :, :, 1], v3(t3), v3(t4), ALU.add), reads=[t3, t4], writes=[qkr])
                    tab = self.C("DQK" + sfx, R).unsqueeze(2).to_broadcast([R, 8, 128])
                    k.op("dve", lambda e: e.tensor_tensor(qkd.t[0:R, :].rearrange("p (g d) -> p g d", g=8),
                                                          qkr.t[0:R, :].rearrange("p (g d) -> p g d", g=8), tab, ALU.mult),
                         reads=[qkr, self.cst], writes=[qkd])
                else:
                    ps = k.psum()
                    k.op("pe", lambda e: e.transpose(ps.t[0:16, 0:R], lr[i].t[0:R, 0:16], self.C("ident")[0:R, 0:R]),
                         reads=[lr[i], self.cst], writes=[ps])
                    k.op("act", lambda e: e.activation(llrT.t[0:16, 0:R], ps.t[0:16, 0:R], AF.Copy), reads=[ps], writes=[llrT])
                    ps = k.psum()
                    k.op("pe", lambda e: e.matmul(ps.t[0:R, 0:512], llrT.t[0:16, 0:R], self.pk.t[0:16, PK_WUP:PK_WUP + 512],
                                                  start=True, stop=True), reads=[llrT, self.pk], writes=[ps])
                    k.op("dve", lambda e: e.tensor_tensor(z.t[0:R, :], ps.t[0:R, 0:512], self.pk.t[0:R, PK_BUP:PK_BUP + 512], ALU.add),
                         reads=[ps, self.pk], writes=[z])
                    k.op("act", lambda e: e.activation(ez.t[0:R, :], z.t[0:R, :], AF.Exp, scale=-1.0), reads=[z], writes=[ez])
                    k.op("dve", lambda e: e.tensor_scalar(ez.t[0:R, :], ez.t[0:R, :], 1.0, None, ALU.add), reads=[ez], writes=[ez])
                    k.op("act", lambda e: e.activation(lz.t[0:R, :], ez.t[0:R, :], AF.Ln), reads=[ez], writes=[lz])
                    psB = k.psum()
                    k.op("pe", lambda e: e.matmul(psB.t[0:R, 0:512], self.C("MTi" + sfx, R)[:, 0:R], lz.t[0:R, :], start=True, stop=True),
                         reads=[lz, self.cst], writes=[psB])
                    psL = k.psum()
                    k.op("pe", lambda e: e.matmul(psL.t[0:R, 0:512], self.C("ALL" + sfx, R)[:, 0:R], lz.t[0:R, :], start=True, stop=True),
                         reads=[lz, self.cst], writes=[psL])
                    k.op("act", lambda e: e.activation(Bsb.t[0:R, :], psB.t[0:R, 0:512], AF.Copy), reads=[psB], writes=[Bsb])
                    k.op("act", lambda e: e.activation(eb.t[0:R, :], Bsb.t[0:R, :], AF.Exp, scale=-1.0 / 16), reads=[Bsb], writes=[eb])
                    k.op("act", lambda e: e.activation(enb.t[0:R, :], Bsb.t[0:R, :], AF.Exp, scale=1.0 / 16), reads=[Bsb], writes=[enb])
                    k.op("dve", lambda e: e.tensor_tensor(dfb.t[0:R, :], Bsb.t[0:R, :], psL.t[0:R, 0:512], ALU.subtract),
                         reads=[Bsb, psL], writes=[dfb])
                    k.op("act", lambda e: e.activation(ekd.t[0:R, :], dfb.t[0:R, :], AF.Exp, scale=1.0 / 16), reads=[dfb], writes=[ekd])
                    k.op("dve", lambda e: e.scalar_tensor_tensor(qkd.t[0:R, 0:512], qkb.t[0:R, 0:512], 128 ** -0.5, eb.t[0:R, :],
                                                                 ALU.mult, ALU.mult), reads=[qkb, eb], writes=[qkd])
                    k.op("pool", lambda e: e.tensor_tensor(qkd.t[0:R, 512:1024], qkb.t[0:R, 512:1024], enb.t[0:R, :], ALU.mult),
                         reads=[qkb, enb], writes=[qkd])
                    for h in range(4):
                        ps = k.psum()
                        k.op("pe", lambda e: e.matmul(ps.t[:, 0:16], lz.t[0:R, h * 128:(h + 1) * 128], self.C("SEL" + sfx, R),
                                                      start=True, stop=True), reads=[lz, self.cst], writes=[ps])
                        k.op("act", lambda e: e.activation(ebl.t[:, h, :], ps.t[:, 0:16], AF.Exp, scale=-1.0 / 16), reads=[ps], writes=[ebl])
                self.transposes(lambda c: qkd.t[0:R, c * 128:(c + 1) * 128], 8, qkT, R, [qkd])
                if ret:
                    k.op("act", lambda e: e.activation(kdb.t[0:R, :, :], qkd.t[0:R, 512:1024].rearrange("p (g d) -> p g d", g=4), AF.Copy),
                         reads=[qkd], writes=[kdb])
                else:
                    k.op("dve", lambda e: e.tensor_tensor(kdb.t[0:R, :, :], qkb.t[0:R, 512:1024].rearrange("p (g d) -> p g d", g=4),
                                                          ekd.t[0:R, :].rearrange("p (g d) -> p g d", g=4), ALU.mult),
                         reads=[qkb, ekd], writes=[kdb])
                k.op("pool", lambda e: e.tensor_copy(vbf.t[0:R, :], vb.t[0:R, :]), reads=[vb], writes=[vbf])
                k.op("act", lambda e: e.activation(sgt.t[0:R, :], gb.t[0:R, :], AF.Silu), reads=[gb], writes=[sgt])
                if not ret:
                    gn = self.pk.t[0:R, PK_GLAN:PK_GLAN + 256].unsqueeze(1).to_broadcast([R, 4, 256])
                    s3 = sgt.t[0:R, :].rearrange("p (g d) -> p g d", g=4)
                    k.op("pool", lambda e: e.tensor_tensor(s3, s3, gn, ALU.mult), reads=[sgt, self.pk], writes=[sgt])
                for h in range(4):
                    ps_sc = k.psum()
                    k.op("pe", lambda e: e.matmul(ps_sc.t[0:R, 0:R], qkT.t[:, 4 + h, 0:R], qkT.t[:, h, 0:R], start=True, stop=True),
                         reads=[qkT], writes=[ps_sc])
                    sc = scb[h % 2]
                    if ret:
                        mtab = self.C("RD" + sfx, R)[:, h * 128:h * 128 + R]
                    else:
                        mtab = self.C("MTi" + sfx, R)[:, 0:R]
                    k.op("dve", lambda e: e.tensor_tensor(sc.t[0:R, 0:R], ps_sc.t[0:R, 0:R], mtab, ALU.mult),
                         reads=[ps_sc, self.cst], writes=[sc])
                    if smp:
                        k.dma("sp", Sall.ap, s_in[l, :, h].rearrange("s d e -> d s e"), writes=[Sall])
                        k.op("act", lambda e: e.activation(Sallbf.t[:, 0:8, :], Sall.t[:, 0:8, :], AF.Copy), reads=[Sall], writes=[Sallbf])
                        k.op("pool", lambda e: e.tensor_copy(Sallbf.t[:, 8:16, :], Sall.t[:, 8:16, :]), reads=[Sall], writes=[Sallbf])
                        k.op("dve", lambda e: e.tensor_tensor(qbig.ap, qkT.t[:, h, 0:64].unsqueeze(1).to_broadcast([128, 16, 64]),
                                                              self.C("BM").rearrange("p (s i) -> p s i", s=16), ALU.mult),
                             reads=[qkT, self.cst], writes=[qbig])
                    ps_o = k.psum()
                    k.op("pe", lambda e: e.matmul(ps_o.t[0:R, 0:256], sc.t[0:R, 0:R], vbf.t[0:R, h * 256:(h + 1) * 256], start=True, stop=False),
                         reads=[sc, vbf], writes=[ps_o])
                    if not smp:
                        k.op("pe", lambda e: e.matmul(ps_o.t[0:R, 0:256], qkT.t[:, h, 0:R], Sbf.t[:, h, :], start=False, stop=True),
                             reads=[qkT, Sbf], writes=[ps_o])
                    else:
                        for s in range(16):
                            k.op("pe", lambda e: e.matmul(ps_o.t[0:R, 0:256], qbig.t[:, s, :], Sallbf.t[:, s, :], start=False, stop=(s == 15)),
                                 reads=[qbig, Sallbf], writes=[ps_o])
                    k.op("act", lambda e: e.activation(sqo.t[0:R, :], ps_o.t[0:R, 0:256], AF.Square), reads=[ps_o], writes=[sqo])
                    k.op("dve", lambda e: e.tensor_reduce(ss.t[0:R, h:h + 1], sqo.t[0:R, :], X, ALU.add), reads=[sqo], writes=[ss])
                    k.op("dve", lambda e: e.tensor_scalar(r1.t[0:R, h:h + 1], ss.t[0:R, h:h + 1], 1.0 / 256, EPS, ALU.mult, ALU.add), reads=[ss], writes=[r1])
                    k.op("act", lambda e: e.activation(r2.t[0:R, h:h + 1], r1.t[0:R, h:h + 1], AF.Sqrt), reads=[r1], writes=[r2])
                    k.op("dve", lambda e: e.reciprocal(rstd.t[0:R, h:h + 1], r2.t[0:R, h:h + 1]), reads=[r2], writes=[rstd])
                    k.op("dve", lambda e: e.scalar_tensor_tensor(y.t[0:R, h * 256:(h + 1) * 256], ps_o.t[0:R, 0:256], rstd.t[0:R, h:h + 1],
                                                                 sgt.t[0:R, h * 256:(h + 1) * 256], ALU.mult, ALU.mult),
                         reads=[ps_o, rstd, sgt], writes=[y])
                    if not smp:
                        ps_s = k.psum()
                        k.op("pe", lambda e: e.matmul(ps_s.t[:, 0:256], kdb.t[0:R, h, :], vbf.t[0:R, h * 256:(h + 1) * 256], start=True, stop=True),
                             reads=[kdb, vbf], writes=[ps_s])
                        dec = float(GAMMA[h] ** 128.0) if ret else ebl.t[:, h, 0:1]
                        k.op("dve", lambda e: e.scalar_tensor_tensor(S.t[:, h, :], S.t[:, h, :], dec, ps_s.t[:, 0:256], ALU.mult, ALU.add),
                             reads=[ps_s, S] + ([] if ret else [ebl]), writes=[S])
                        k.op("act", lambda e: e.activation(Sbf.t[:, h, :], S.t[:, h, :], AF.Copy), reads=[S], writes=[Sbf])
                        if tt == 15:
                            k.dma("pool", p_out[l, h], S.t[:, h, :], reads=[S])
                    else:
                        k.op("dve", lambda e: e.tensor_tensor(kbig.t[0:64, :, :], kdb.t[0:64, h, :].unsqueeze(1).to_broadcast([64, 16, 128]),
                                                              self.C("SEL_s", 64).unsqueeze(2).to_broadcast([64, 16, 128]), ALU.mult),
                             reads=[kdb, self.cst], writes=[kbig])
                        for s in range(16):
                            ps_s = k.psum()
                            k.op("pe", lambda e: e.matmul(ps_s.t[:, 0:256], kbig.t[0:64, s, :], vbf.t[0:64, h * 256:(h + 1) * 256], start=True, stop=True),
                                 reads=[kbig, vbf], writes=[ps_s])
                            dec = float(GAMMA[h] ** 4.0) if ret else ebl.t[:, h, s:s + 1]
                            k.op("dve", lambda e: e.scalar_tensor_tensor(Snew.t[:, s, :], Sall.t[:, s, :], dec, ps_s.t[:, 0:256], ALU.mult, ALU.add),
                                 reads=[ps_s, Sall] + ([] if ret else [ebl]), writes=[Snew])
                        k.dma("pool", s_out[l, :, h].rearrange("s d e -> d s e"), Snew.ap, reads=[Snew])
                yt = yT[tt % 2]
                self.transposes(lambda c: y.t[0:R, c * 128:(c + 1) * 128], 8, yt, R, [y])
                k.dma("pool", self.dr["YBT"][ybase:ybase + 1024, tt * 128:tt * 128 + R].rearrange("(c p) t -> p c t", p=128),
                      yt.t[:, :, 0:R], reads=[yt])
            k.barrier()

    def phase_ret(self, l):
        self.phase_la(l, "ret")

    def phase_gla(self, l):
        self.phase_la(l, "gla")

    def phase_gdn(self, l):
        k = self.k
        X = mybir.AxisListType.X
        PTM, PFM = self.dr["PTM"], self.dr["PFM"]
        ident = self.C("ident")
        ones = self.C("ones")
        with contextlib.ExitStack() as st:
            sb = lambda n, shp, dt=F32: k.sb(f"gd_{n}", shp, dt, st)
            xc = [sb(f"xc{i}", [128, 24, 131]) for i in range(2)]
            xs = sb("xs", [128, 24, 16, 7])
            xst = sb("xst", [128, 24, 64])
            cst_in = sb("cstin", [128, 24, 16, 3])
            ca = sb("ca", [128, 24, 128]); cb = sb("cb", [128, 24, 128]); cc = cb
            sq = sb("sq", [128, 16, 128]); rr = sb("rr", [128, 16, 128]); rr2 = sq
            qkn = sb("qkn", [128, 16, 128]); qkb = sb("qkb", [128, 16, 128], BF16)
            ktm = sb("ktm", [128, 8, 128]); vtm = sb("vtm", [128, 8, 128])
            dab = [sb(f"dab{i}", [128, 16]) for i in range(2)]
            dzb = [sb(f"dz{i}", [128, 1024]) for i in range(2)]
            sgt = sb("sgt", [128, 1024])
            tg = sb("tg", [128, 8]); eg = sb("eg", [128, 8]); lg = sb("lg", [128, 8]); ea = sb("ea", [128, 8])
            g = sb("g", [128, 8]); beta = sb("beta", [128, 8]); cum = sb("cum", [128, 8]); ecum = sb("ecum", [128, 8])
            dcl = sb("dcl", [128, 8]); ekl = sb("ekl", [128, 8]); becum = sb("becum", [128, 8])
            gsel = sb("gsel", [128, 16, 8]); ecl = sb("ecl", [128, 16, 8])
            Gh = sb("Gh", [128, 128]); n1 = sb("n1", [128, 128]); n2 = sb("n2", [128, 128])
            expd = sb("expd", [128, 128]); expdT = sb("expdT", [128, 128]); ECB = sb("ECB", [128, 128])
            decS = sb("decS", [128, 128]); decTI = sb("decTI", [128, 128])
            qeT = sb("qeT", [128, 128]); Lm = sb("Lm", [128, 128])
            Pa = [sb(f"Pa{i}", [128, 128]) for i in range(2)]; Pt = [sb(f"Pt{i}", [128, 128]) for i in range(2)]
            Rb = [sb(f"Rb{i}", [128, 128]) for i in range(2)]
            AqkT = sb("AqkT", [128, 128]); rhsu = sb("rhsu", [128, 128]); rhsw = sb("rhsw", [128, 128]); kdec = sb("kdec", [128, 128])
            negwT = sb("negwT", [128, 128]); vnew = sb("vnew", [128, 128])
            sqo = sb("sqo", [128, 128]); ss = sb("ss", [128, 8]); r1 = sb("r1", [128, 8]); r2 = sb("r2", [128, 8]); rstd = sb("rstd", [128, 8])
            y = sb("y", [128, 1024]); yT = [sb(f"yT{i}", [128, 8, 128], BF16) for i in range(2)]
            S = sb("S", [128, 8, 128])
            Sall, Snew = sq, rr
            Sall_v = sq.t[:, :, :]; Snew_v = rr.t[:, :, :]
            kbig_v = ca.t[:, 0:16, :]
            wbig_v = ca.t[:, 16:24, :].rearrange("p a b -> p (a b)").rearrange("p (s i) -> p s i", s=16)
            qbig_v = cb.t[:, 0:8, :].rearrange("p a b -> p (a b)").rearrange("p (s i) -> p s i", s=16)
            wbig = kbig = ca
            qbig = cb
            k.op("dve", lambda e: e.memset(S.ap, 0.0), writes=[S])
            k.op("act", lambda e: e.activation(ea.ap, self.pk.t[:, PK_ALOG:PK_ALOG + 8], AF.Exp), reads=[self.pk], writes=[ea])
            cw = lambda w: self.pk.t[:, PK_CONV:PK_CONV + 96].rearrange("p (c w) -> p c w", w=4)[:, :, w:w + 1]
            pfm3 = PFM[0:3072, :].rearrange("(c p) t -> p c t", p=128)

            def load(tt):
                R = 128 if tt < 16 else 64
                r0 = tt * 128
                i = tt % 2
                if tt == 0:
                    k.op("pool", lambda e: e.memset(xc[i].t[:, :, 0:3], 0.0), writes=[xc[i]])
                    k.dma("sp", xc[i].t[:, :, 3:131], pfm3[:, :, 0:128], writes=[xc[i]])
                elif tt < 16:
                    k.dma("sp", xc[i].t[:, :, 0:131], pfm3[:, :, r0 - 3:r0 + 128], writes=[xc[i]])
                else:
                    k.dma("sp", xst.ap, pfm3[:, :, TP:TP + 64], writes=[xst])
                    k.dma("sp", cst_in.ap, self.dr["sconv"][l].rearrange("p (c s w) -> p c s w", c=24, s=16), writes=[cst_in])
                k.dma("sp", dab[i].t[0:R, :], PTM[r0:r0 + R, C_DAB:C_DAB + 16], writes=[dab[i]])
                k.dma("sp", dzb[i].t[0:R, :], PTM[r0:r0 + R, C_DZ:C_DZ + 1024], writes=[dzb[i]])

            import os
            tiles = [int(x) for x in os.environ.get("GDN_TILES", ",".join(map(str, range(NT)))).split(",")]
            lvl = int(os.environ.get("GDN_LEVEL", "9"))
            load(tiles[0])
            for ti, tt in enumerate(tiles):
                if ti + 1 < len(tiles):
                    load(tiles[ti + 1])
                R = 128 if tt < 16 else 64
                smp = tt == 16
                sfx = "_s" if smp else "_p"
                nseq = 16 if smp else 1
                i = tt % 2
                if not smp:
                    src = lambda w: xc[i].t[:, :, w:w + 128]
                    shp = [128, 24, 128]
                    va = lambda b: b.t[:, :, :]
                    rd = [xc[i]]
                else:
                    k.op("pool", lambda e: e.tensor_copy(xs.t[:, :, :, 0:3], cst_in.ap), reads=[cst_in], writes=[xs])
                    k.op("pool", lambda e: e.tensor_copy(xs.t[:, :, :, 3:7], xst.t[:, :, :].rearrange("p c (s t) -> p c s t", s=16)),
                         reads=[xst], writes=[xs])
                    src = lambda w: xs.t[:, :, :, w:w + 4]
                    shp = [128, 24, 16, 4]
                    va = lambda b: b.t[:, :, 0:64].rearrange("p c (s t) -> p c s t", s=16)
                    rd = [xs]
                    k.op("pool", lambda e: e.tensor_copy(cst_in.ap, xs.t[:, :, :, 4:7]), reads=[xs], writes=[cst_in])
                    k.dma("pool", self.dr["o_sconv"][l].rearrange("p (c s w) -> p c s w", c=24, s=16), cst_in.ap, reads=[cst_in])
                cwb = lambda w: (cw(w).to_broadcast(shp) if not smp else cw(w).unsqueeze(3).to_broadcast(shp))
                k.op("dve", lambda e: e.tensor_tensor(va(ca), src(0), cwb(0), ALU.mult), reads=rd + [self.pk], writes=[ca])
                k.op("pool", lambda e: e.tensor_tensor(va(cb), src(1), cwb(1), ALU.mult), reads=rd + [self.pk], writes=[cb])
                k.op("dve", lambda e: e.tensor_tensor(va(ca), va(ca), va(cb), ALU.add), reads=[cb, ca], writes=[ca])
                k.op("pool", lambda e: e.tensor_tensor(va(cb), src(2), cwb(2), ALU.mult), reads=rd + [self.pk], writes=[cb])
                k.op("dve", lambda e: e.tensor_tensor(va(ca), va(ca), va(cb), ALU.add), reads=[cb, ca], writes=[ca])
                k.op("pool", lambda e: e.tensor_tensor(va(cb), src(3), cwb(3), ALU.mult), reads=rd + [self.pk], writes=[cb])
                k.op("dve", lambda e: e.tensor_tensor(va(ca), va(ca), va(cb), ALU.add), reads=[cb, ca], writes=[ca])
                k.op("act", lambda e: e.activation(cc.t[:, :, 0:R], ca.t[:, :, 0:R], AF.Silu), reads=[ca], writes=[cc])
                if lvl < 2:
                    continue
                k.op("act", lambda e: e.activation(sq.t[:, :, 0:R], cc.t[:, 0:16, 0:R], AF.Square), reads=[cc], writes=[sq])
                for g0 in range(0, 16, 4):
                    ps = k.psum()
                    for c in range(4):
                        k.op("pe", lambda e: e.matmul(ps.t[:, c * 128:c * 128 + R], ones, sq.t[:, g0 + c, 0:R], start=True, stop=True),
                             reads=[sq, self.cst], writes=[ps])
                    pv = ps.t[:, :].rearrange("p (a b) -> p a b", a=4)[:, :, 0:R]
                    k.op("dve", lambda e: e.tensor_scalar(rr.t[:, g0:g0 + 4, 0:R], pv, EPS, None, ALU.add), reads=[ps], writes=[rr])
                k.op("act", lambda e: e.activation(rr2.t[:, :, 0:R], rr.t[:, :, 0:R], AF.Sqrt), reads=[rr], writes=[rr2])
                k.op("dve", lambda e: e.reciprocal(rr.t[:, :, 0:R], rr2.t[:, :, 0:R]), reads=[rr2], writes=[rr])
                k.op("dve", lambda e: e.scalar_tensor_tensor(qkn.t[:, 0:8, 0:R], cc.t[:, 0:8, 0:R], 128 ** -0.5, rr.t[:, 0:8, 0:R], ALU.mult, ALU.mult),
                     reads=[cc, rr], writes=[qkn])
                k.op("pool", lambda e: e.tensor_tensor(qkn.t[:, 8:16, 0:R], cc.t[:, 8:16, 0:R], rr.t[:, 8:16, 0:R], ALU.mult),
                     reads=[cc, rr], writes=[qkn])
                k.op("act", lambda e: e.activation(qkb.t[:, :, 0:R], qkn.t[:, :, 0:R], AF.Copy), reads=[qkn], writes=[qkb])
                for h0 in range(0, 8, 4):
                    for (srcb, c0, dst) in ((qkn, 8, ktm), (cc, 16, vtm)):
                        ps = k.psum()
                        for c in range(4):
                            k.op("pe", lambda e: e.transpose(ps.t[0:R, c * 128:(c + 1) * 128], srcb.t[:, c0 + h0 + c, 0:R], ident),
                                 reads=[srcb, self.cst], writes=[ps])
                        k.op("act", lambda e: e.activation(dst.t[0:R, h0:h0 + 4, :], ps.t[0:R, :].rearrange("p (a b) -> p a b", a=4), AF.Copy),
                             reads=[ps], writes=[dst])
                if lvl < 3:
                    continue
                da = dab[i]
                k.op("dve", lambda e: e.tensor_tensor(tg.t[0:R, :], da.t[0:R, 0:8], self.pk.t[0:R, PK_DTB:PK_DTB + 8], ALU.add),
                     reads=[da, self.pk], writes=[tg])
                k.op("act", lambda e: e.activation(eg.t[0:R, :], tg.t[0:R, :], AF.Exp), reads=[tg], writes=[eg])
                k.op("dve", lambda e: e.tensor_scalar(eg.t[0:R, :], eg.t[0:R, :], 1.0, None, ALU.add), reads=[eg], writes=[eg])
                k.op("act", lambda e: e.activation(lg.t[0:R, :], eg.t[0:R, :], AF.Ln), reads=[eg], writes=[lg])
                k.op("dve", lambda e: e.scalar_tensor_tensor(g.t[0:R, :], lg.t[0:R, :], -1.0, ea.t[0:R, :], ALU.mult, ALU.mult),
                     reads=[lg, ea], writes=[g])
                k.op("act", lambda e: e.activation(beta.t[0:R, :], da.t[0:R, 8:16], AF.Sigmoid), reads=[da], writes=[beta])
                ps = k.psum()
                k.op("pe", lambda e: e.matmul(ps.t[0:R, 0:8], self.C("MTi" + sfx, R)[:, 0:R], g.t[0:R, :], start=True, stop=True),
                     reads=[g, self.cst], writes=[ps])
                k.op("pe", lambda e: e.matmul(ps.t[0:R, 8:16], self.C("ALL" + sfx, R)[:, 0:R], g.t[0:R, :], start=True, stop=True),
                     reads=[g, self.cst], writes=[ps])
                k.op("dve", lambda e: e.tensor_copy(cum.t[0:R, :], ps.t[0:R, 0:8]), reads=[ps], writes=[cum])
                k.op("dve", lambda e: e.tensor_tensor(dcl.t[0:R, :], ps.t[0:R, 8:16], cum.t[0:R, :], ALU.subtract), reads=[ps, cum], writes=[dcl])
                k.op("act", lambda e: e.activation(ecum.t[0:R, :], cum.t[0:R, :], AF.Exp), reads=[cum], writes=[ecum])
                k.op("act", lambda e: e.activation(ekl.t[0:R, :], dcl.t[0:R, :], AF.Exp), reads=[dcl], writes=[ekl])
                k.op("dve", lambda e: e.tensor_tensor(becum.t[0:R, :], beta.t[0:R, :], ecum.t[0:R, :], ALU.mult), reads=[beta, ecum], writes=[becum])
                k.op("dve", lambda e: e.tensor_tensor(gsel.t[0:R, :, :], g.t[0:R, :].unsqueeze(1).to_broadcast([R, 16, 8]),
                                                      self.C("SEL" + sfx, R).unsqueeze(2).to_broadcast([R, 16, 8]), ALU.mult),
                     reads=[g, self.cst], writes=[gsel])
                ps = k.psum()
                k.op("pe", lambda e: e.matmul(ps.t[:, 0:128], ones[0:R, :], gsel.t[0:R, :, :].rearrange("p s h -> p (s h)"), start=True, stop=True),
                     reads=[gsel, self.cst], writes=[ps])
                k.op("act", lambda e: e.activation(ecl.t[:, :, :].rearrange("p s h -> p (s h)"), ps.t[:, 0:128], AF.Exp), reads=[ps], writes=[ecl])
                k.op("act", lambda e: e.activation(sgt.t[0:R, :], dzb[i].t[0:R, :], AF.Silu), reads=[dzb[i]], writes=[sgt])
                gn = self.pk.t[0:R, PK_GDNN:PK_GDNN + 128].unsqueeze(1).to_broadcast([R, 8, 128])
                s3 = sgt.t[0:R, :].rearrange("p (g d) -> p g d", g=8)
                k.op("pool", lambda e: e.tensor_tensor(s3, s3, gn, ALU.mult), reads=[sgt, self.pk], writes=[sgt])
                if lvl < 4:
                    continue
                for h in range(int(os.environ.get('GDN_HEADS', '8'))):
                    kT = qkb.t[:, 8 + h, 0:R]
                    qT = qkb.t[:, h, 0:R]
                    k.op("dve", lambda e: e.tensor_scalar(Gh.t[0:R, :], ones[0:R, :], g.t[0:R, h:h + 1], None, ALU.mult),
                         reads=[g, self.cst], writes=[Gh])
                    psc = k.psum()
                    k.op("pe", lambda e: e.matmul(psc.t[:, 0:R], Gh.t[0:R, :], self.C("MTi" + sfx, R)[:, 0:R], start=True, stop=True),
                         reads=[Gh, self.cst], writes=[psc])
                    cumc = cum.t[0:R, h:h + 1]
                    k.op("dve", lambda e: e.tensor_scalar(n1.t[0:R, 0:R], psc.t[0:R, 0:R], cumc, 0.0, ALU.subtract, ALU.max),
                         reads=[psc, cum], writes=[n1])
                    k.op("act", lambda e: e.activation(expd.t[0:R, 0:R], n1.t[0:R, 0:R], AF.Exp, scale=-1.0), reads=[n1], writes=[expd])
                    k.op("dve", lambda e: e.tensor_scalar(n2.t[0:R, 0:R], psc.t[0:R, 0:R], cumc, 0.0, ALU.subtract, ALU.min),
                         reads=[psc, cum], writes=[n2])
                    k.op("act", lambda e: e.activation(expdT.t[0:R, 0:R], n2.t[0:R, 0:R], AF.Exp), reads=[n2], writes=[expdT])
                    k.op("act", lambda e: e.activation(ECB.t[:, 0:R], psc.t[:, 0:R], AF.Exp), reads=[psc], writes=[ECB])
                    k.op("pool", lambda e: e.tensor_tensor(decS.t[0:R, 0:R], expd.t[0:R, 0:R], self.C("Ms" + sfx, R)[:, 0:R], ALU.mult),
                         reads=[expd, self.cst], writes=[decS])
                    k.op("pool", lambda e: e.tensor_tensor(decTI.t[0:R, 0:R], expdT.t[0:R, 0:R], self.C("MTi" + sfx, R)[:, 0:R], ALU.mult),
                         reads=[expdT, self.cst], writes=[decTI])
                    k.op("pool", lambda e: e.tensor_tensor(qeT.t[:, 0:R], qkn.t[:, h, 0:R], ECB.t[:, 0:R], ALU.mult), reads=[qkn, ECB], writes=[qeT])
                    if lvl < 5:
                        continue
                    skip = os.environ.get("GDN_SKIP", "").split(",")
                    pk_ = k.psum()
                    if "kk" not in skip:
                        k.op("pe", lambda e: e.matmul(pk_.t[0:R, 0:R], kT, kT, start=True, stop=True), reads=[qkb], writes=[pk_])
                    if "stt" not in skip:
                        k.op("dve", lambda e: e.scalar_tensor_tensor(Lm.t[0:R, 0:R], pk_.t[0:R, 0:R], beta.t[0:R, h:h + 1], decS.t[0:R, 0:R],
                                                                     ALU.mult, ALU.mult), reads=[pk_, beta, decS], writes=[Lm])
                    pA = k.psum()
                    if "tr" not in skip:
                        k.op("pe", lambda e: e.transpose(pA.t[0:R, 0:R], Lm.t[0:R, 0:R], ident[0:R, 0:R]), reads=[Lm, self.cst], writes=[pA])
                    P, PT, Rc = Pa[0], Lm, Rb[0]
                    if "pc" not in skip:
                        k.op("act", lambda e: e.activation(P.t[0:R, 0:R], pA.t[0:R, 0:R], AF.Copy), reads=[pA], writes=[P])
                    if "rc" not in skip:
                        k.op("dve", lambda e: e.tensor_tensor(Rc.t[0:R, 0:R], ident[0:R, 0:R], pA.t[0:R, 0:R], ALU.subtract),
                             reads=[pA, self.cst], writes=[Rc])
                    nlev = 1 if smp else 6
                    nlev = int(os.environ.get('GDN_NLEV', nlev))
                    for lv in range(nlev):
                        last = lv == nlev - 1
                        P2, PT2, R2 = Pa[(lv + 1) % 2], Pt[lv % 2], Rb[(lv + 1) % 2]
                        pb = k.psum()
                        k.op("pe", lambda e: e.matmul(pb.t[0:R, 0:R], P.t[0:R, 0:R], PT.t[0:R, 0:R], start=True, stop=True),
                             reads=[P, PT], writes=[pb])
                        if not last:
                            pa_ = k.psum()
                            k.op("pe", lambda e: e.matmul(pa_.t[0:R, 0:R], PT.t[0:R, 0:R], P.t[0:R, 0:R], start=True, stop=True),
                                 reads=[P, PT], writes=[pa_])
                        k.op("dve", lambda e: e.tensor_copy(PT2.t[0:R, 0:R], pb.t[0:R, 0:R]), reads=[pb], writes=[PT2])
                        if not last:
                            k.op("act", lambda e: e.activation(P2.t[0:R, 0:R], pa_.t[0:R, 0:R], AF.Copy), reads=[pa_], writes=[P2])
                        pc = k.psum()
                        k.op("pe", lambda e: e.matmul(pc.t[0:R, 0:R], PT2.t[0:R, 0:R], Rc.t[0:R, 0:R], start=True, stop=True),
                             reads=[PT2, Rc], writes=[pc])
                        k.op("dve", lambda e: e.tensor_tensor(R2.t[0:R, 0:R], Rc.t[0:R, 0:R], pc.t[0:R, 0:R], ALU.add), reads=[pc, Rc], writes=[R2])
                        P, PT, Rc = P2, PT2, R2
                    Rf = Rc
                    if lvl < 6:
                        continue
                    pq = k.psum()
                    k.op("pe", lambda e: e.matmul(pq.t[0:R, 0:R], kT, qT, start=True, stop=True), reads=[qkb], writes=[pq])
                    k.op("dve", lambda e: e.tensor_tensor(AqkT.t[0:R, 0:R], pq.t[0:R, 0:R], decTI.t[0:R, 0:R], ALU.mult), reads=[pq, decTI], writes=[AqkT])
                    k.op("dve", lambda e: e.tensor_scalar(rhsu.t[0:R, :], vtm.t[0:R, h, :], beta.t[0:R, h:h + 1], None, ALU.mult),
                         reads=[vtm, beta], writes=[rhsu])
                    k.op("dve", lambda e: e.tensor_scalar(rhsw.t[0:R, :], ktm.t[0:R, h, :], becum.t[0:R, h:h + 1], None, ALU.mult),
                         reads=[ktm, becum], writes=[rhsw])
                    k.op("dve", lambda e: e.tensor_scalar(kdec.t[0:R, :], ktm.t[0:R, h, :], ekl.t[0:R, h:h + 1], None, ALU.mult),
                         reads=[ktm, ekl], writes=[kdec])
                    pw = k.psum()
                    k.op("pe", lambda e: e.matmul(pw.t[:, 0:R], rhsw.t[0:R, :], Rf.t[0:R, 0:R], start=True, stop=True), reads=[rhsw, Rf], writes=[pw])
                    k.op("act", lambda e: e.activation(negwT.t[:, 0:R], pw.t[:, 0:R], AF.Copy, scale=-1.0), reads=[pw], writes=[negwT])
                    if smp:
                        k.dma("sp", Sall_v, self.dr["sgdn"][l, :, h].rearrange("s d e -> d s e"), writes=[Sall])
                        bmv = self.C("BM").rearrange("p (s i) -> p s i", s=16)
                        k.op("dve", lambda e: e.tensor_tensor(wbig_v, negwT.t[:, 0:64].unsqueeze(1).to_broadcast([128, 16, 64]), bmv, ALU.mult),
                             reads=[negwT, self.cst], writes=[wbig])
                        k.op("pool", lambda e: e.tensor_tensor(qbig_v, qeT.t[:, 0:64].unsqueeze(1).to_broadcast([128, 16, 64]), bmv, ALU.mult),
                             reads=[qeT, self.cst], writes=[qbig])
                        k.op("dve", lambda e: e.tensor_tensor(kbig_v[0:64, :, :], kdec.t[0:64, :].unsqueeze(1).to_broadcast([64, 16, 128]),
                                                              self.C("SEL_s", 64).unsqueeze(2).to_broadcast([64, 16, 128]), ALU.mult),
                             reads=[kdec, self.cst], writes=[kbig])
                    pv_ = k.psum()
                    k.op("pe", lambda e: e.matmul(pv_.t[0:R, 0:128], Rf.t[0:R, 0:R], rhsu.t[0:R, :], start=True, stop=False), reads=[Rf, rhsu], writes=[pv_])
                    if not smp:
                        k.op("pe", lambda e: e.matmul(pv_.t[0:R, 0:128], negwT.t[:, 0:R], S.t[:, h, :], start=False, stop=True),
                             reads=[negwT, S], writes=[pv_])
                    else:
                        for s in range(16):
                            k.op("pe", lambda e: e.matmul(pv_.t[0:R, 0:128], wbig_v[:, s, :], Sall_v[:, s, :], start=False, stop=(s == 15)),
                                 reads=[wbig, Sall], writes=[pv_])
                    k.op("act", lambda e: e.activation(vnew.t[0:R, :], pv_.t[0:R, 0:128], AF.Copy), reads=[pv_], writes=[vnew])
                    if lvl < 7:
                        continue
                    po = k.psum()
                    k.op("pe", lambda e: e.matmul(po.t[0:R, 0:128], AqkT.t[0:R, 0:R], vnew.t[0:R, :], start=True, stop=False), reads=[AqkT, vnew], writes=[po])
                    if not smp:
                        k.op("pe", lambda e: e.matmul(po.t[0:R, 0:128], qeT.t[:, 0:R], S.t[:, h, :], start=False, stop=True), reads=[qeT, S], writes=[po])
                    else:
                        for s in range(16):
                            k.op("pe", lambda e: e.matmul(po.t[0:R, 0:128], qbig_v[:, s, :], Sall_v[:, s, :], start=False, stop=(s == 15)),
                                 reads=[qbig, Sall], writes=[po])
                    k.op("act", lambda e: e.activation(sqo.t[0:R, :], po.t[0:R, 0:128], AF.Square), reads=[po], writes=[sqo])
                    k.op("dve", lambda e: e.tensor_reduce(ss.t[0:R, h:h + 1], sqo.t[0:R, :], X, ALU.add), reads=[sqo], writes=[ss])
                    k.op("dve", lambda e: e.tensor_scalar(r1.t[0:R, h:h + 1], ss.t[0:R, h:h + 1], 1.0 / 128, EPS, ALU.mult, ALU.add), reads=[ss], writes=[r1])
                    k.op("act", lambda e: e.activation(r2.t[0:R, h:h + 1], r1.t[0:R, h:h + 1], AF.Sqrt), reads=[r1], writes=[r2])
                    k.op("dve", lambda e: e.reciprocal(rstd.t[0:R, h:h + 1], r2.t[0:R, h:h + 1]), reads=[r2], writes=[rstd])
                    k.op("dve", lambda e: e.scalar_tensor_tensor(y.t[0:R, h * 128:(h + 1) * 128], po.t[0:R, 0:128], rstd.t[0:R, h:h + 1],
                                                                 sgt.t[0:R, h * 128:(h + 1) * 128], ALU.mult, ALU.mult),
                         reads=[po, rstd, sgt], writes=[y])
                    if not smp:
                        pS = k.psum()
                        k.op("pe", lambda e: e.matmul(pS.t[:, 0:128], kdec.t[0:R, :], vnew.t[0:R, :], start=True, stop=True), reads=[kdec, vnew], writes=[pS])
                        k.op("dve", lambda e: e.scalar_tensor_tensor(S.t[:, h, :], S.t[:, h, :], ecl.t[:, 0, h:h + 1], pS.t[:, 0:128], ALU.mult, ALU.add),
                             reads=[pS, S, ecl], writes=[S])
                        if tt == 15:
                            k.dma("pool", self.dr["o_pgdn"][l, h], S.t[:, h, :], reads=[S])
                    else:
                        for s in range(16):
                            pS = k.psum()
                            k.op("pe", lambda e: e.matmul(pS.t[:, 0:128], kbig_v[0:64, s, :], vnew.t[0:64, :], start=True, stop=True),
                                 reads=[kbig, vnew], writes=[pS])
                            k.op("dve", lambda e: e.scalar_tensor_tensor(Snew_v[:, s, :], Sall_v[:, s, :], ecl.t[:, s, h:h + 1], pS.t[:, 0:128],
                                                                         ALU.mult, ALU.add), reads=[pS, Sall, ecl], writes=[Snew])
                        k.dma("pool", self.dr["o_sgdn"][l, :, h].rearrange("s d e -> d s e"), Snew_v, reads=[Snew])
                yt = yT[tt % 2]
                self.transposes(lambda c: y.t[0:R, c * 128:(c + 1) * 128], 8, yt, R, [y])
                k.dma("pool", self.dr["YBT"][1024:2048, tt * 128:tt * 128 + R].rearrange("(c p) t -> p c t", p=128),
                      yt.t[:, :, 0:R], reads=[yt])
            if lvl >= 9:
                k.dma("sp", self.dr["o_pconv"][l], PFM[0:3072, TP - 3:TP])
            k.barrier()


_PROG = {}


def _get_prog():
    if "nc" not in _PROG:
        nc = bass.Bass("TRN2", target_bir_lowering=False)
        m = Model(nc)
        m.build()
        _PROG["nc"] = nc
        _PROG["ninst"] = m.k.ninst
    return _PROG["nc"]


def _cossin():
    inv = 1.0 / (10000.0 ** np.linspace(0.0, 1.0, 64, dtype=np.float32)).astype(np.float32)
    pos = np.concatenate([np.arange(TP, dtype=np.float32),
                          np.tile(np.arange(4, dtype=np.float32) + np.float32(16384.0), 16)])
    ang = (pos[:, None] * inv[None, :]).astype(np.float32)
    return np.concatenate([np.cos(ang), np.sin(ang)], axis=1).astype(np.float32)


def kernel(x_prompt, x_sample, state_ret, state_gdn, state_gdn_conv, state_gla, norm_mix, norm_ffn,
           norm_final, w_in, b_merge, gdn_conv_w, gdn_a_log, gdn_dt_bias, gdn_norm, gla_w_up, gla_b_up,
           gla_norm, w_branch, w_o, w_gate_up, w_down):
    f = lambda a: np.ascontiguousarray(np.asarray(a, dtype=np.float32))
    x_prompt, x_sample = f(x_prompt), f(x_sample)
    w_in = np.asarray(w_in, dtype=np.float32)
    wtm = np.zeros((DEPTH, D, TMW), np.float32)
    wtm[:, :, :7200] = w_in[:, :, TM_IDX]
    wfm = np.ascontiguousarray(w_in[:, :, FM_IDX])
    wgu = np.asarray(w_gate_up, dtype=np.float32).reshape(DEPTH, D, 2, DFF // 128, 128)
    wgu = np.ascontiguousarray(wgu.transpose(0, 1, 3, 2, 4)).reshape(DEPTH, D, 2 * DFF)
    pack = np.zeros((DEPTH, 128, NPK), np.float32)
    col = lambda v, n: np.asarray(v, np.float32).reshape(n, 128).T
    for l in range(DEPTH):
        pack[l, :, PK_GMIX:PK_GMIX + 16] = col(norm_mix[l], 16)
        pack[l, :, PK_GFFN:PK_GFFN + 16] = col(norm_ffn[l], 16)
        cwl = np.asarray(gdn_conv_w[l], np.float32)
        pack[l, :, PK_CONV:PK_CONV + 96] = cwl.reshape(4, 24, 128).transpose(2, 1, 0).reshape(128, 96)
        pack[l, :, PK_BM:PK_BM + 48] = col(b_merge[l], 48)
        pack[l, :, PK_ALOG:PK_ALOG + 8] = np.asarray(gdn_a_log[l], np.float32)[None, :]
        pack[l, :, PK_DTB:PK_DTB + 8] = np.asarray(gdn_dt_bias[l], np.float32)[None, :]
        pack[l, :, PK_GDNN:PK_GDNN + 128] = np.asarray(gdn_norm[l], np.float32)[None, :]
        pack[l, :, PK_GLAN:PK_GLAN + 256] = np.asarray(gla_norm[l], np.float32)[None, :]
        pack[l, :, PK_BUP:PK_BUP + 512] = np.asarray(gla_b_up[l], np.float32)[None, :]
        pack[l, 0:16, PK_WUP:PK_WUP + 512] = np.asarray(gla_w_up[l], np.float32)
        pack[l, :, PK_GFIN:PK_GFIN + 16] = col(norm_final, 16)
    cossin = _cossin()
    shared = {"wtm": wtm, "wfm": wfm, "wbr": f(w_branch), "wo": f(w_o), "wgu": wgu, "wdn": f(w_down),
              "pack": pack, "cst": CST_ARR, "cossin": cossin}
    state_ret, state_gdn, state_gla = f(state_ret), f(state_gdn), f(state_gla)
    sconv = f(state_gdn_conv)
    in_maps = []
    for c in range(NCORES):
        xs = x_sample[16 * c:16 * (c + 1)].reshape(64, D)
        xT = np.ascontiguousarray(np.concatenate([x_prompt[c % 4], xs], axis=0).T)
        sc = sconv[:, 16 * c:16 * (c + 1)]
        sc = sc.reshape(DEPTH, 16, 3, 24, 128).transpose(0, 4, 3, 1, 2)
        m = dict(shared)
        m.update({"xT": xT,
                  "sret": np.ascontiguousarray(state_ret[:, 16 * c:16 * (c + 1)]),
                  "sgdn": np.ascontiguousarray(state_gdn[:, 16 * c:16 * (c + 1)]),
                  "sgla": np.ascontiguousarray(state_gla[:, 16 * c:16 * (c + 1)]),
                  "sconv": np.ascontiguousarray(sc).reshape(DEPTH, 128, 24 * 16 * 3)})
        in_maps.append(m)
    nc = _get_prog()
    res = run_bass_kernel_spmd(nc, in_maps, core_ids=list(range(NCORES))).results
    y_prompt = np.stack([res[c]["yT"][:, :TP].T for c in range(4)])
    y_sample = np.concatenate([res[c]["yT"][:, TP:].T.reshape(16, 4, D) for c in range(NCORES)])
    p_ret = np.stack([res[c]["o_pret"] for c in range(4)], axis=1)
    p_gdn = np.stack([res[c]["o_pgdn"] for c in range(4)], axis=1)
    p_conv = np.stack([res[c]["o_pconv"].transpose(0, 2, 1) for c in range(4)], axis=1)
    p_gla = np.stack([res[c]["o_pgla"] for c in range(4)], axis=1)
    s_ret = np.concatenate([res[c]["o_sret"] for c in range(NCORES)], axis=1)
    s_gdn = np.concatenate([res[c]["o_sgdn"] for c in range(NCORES)], axis=1)
    s_gla = np.concatenate([res[c]["o_sgla"] for c in range(NCORES)], axis=1)
    s_conv = np.concatenate([res[c]["o_sconv"].reshape(DEPTH, 128, 24, 16, 3).transpose(0, 3, 4, 2, 1).reshape(DEPTH, 16, 3, 3072)
                             for c in range(NCORES)], axis=1)
    outs = (y_prompt, y_sample, p_ret, p_gdn, p_conv, p_gla, s_ret, s_gdn, s_conv, s_gla)
    return tuple(np.ascontiguousarray(o, dtype=np.float32) for o in outs)
```

```python
import contextlib
import numpy as np
import concourse.bass as bass
import concourse.mybir as mybir
from concourse.bass_utils import run_bass_kernel_spmd

F32 = mybir.dt.float32
BF16 = mybir.dt.bfloat16
AF = mybir.ActivationFunctionType
ALU = mybir.AluOpType

SEM_LIMIT = 28000


class Buf:
    __slots__ = ("name", "t", "ap", "w", "r", "psum")

    def __init__(self, name, t=None):
        self.name = name
        self.psum = False
        self.t = t
        self.ap = t[:] if t is not None else None
        self.w = None
        self.r = {}


class Ctx:
    ENG = ("pe", "act", "dve", "pool", "sp")

    def __init__(self, nc):
        self.nc = nc
        self.es = contextlib.ExitStack()
        self.engs = {"pe": nc.tensor, "act": nc.scalar, "dve": nc.vector, "pool": nc.gpsimd, "sp": nc.sync}
        self.cur = {}
        self.waited = {e: {} for e in self.ENG}
        self.semid = 0
        self.dma_pool = []
        self.dma_rr = 0
        self.all_sems = []
        self.drams = {}
        self.psums = []
        self.ps_rr = 0
        self.ninst = 0

    def __enter__(self):
        self.es.__enter__()
        for e in self.ENG:
            self.cur[e] = self._newsem(e)
        for i in range(24):
            self.dma_pool.append(self._newsem())
        for i in range(8):
            t = self.es.enter_context(self.nc.psum_tensor(f"psb{i}", [128, 512], F32))
            self.psums.append(Buf(f"psb{i}", t))
            self.psums[-1].psum = True
        return self

    def __exit__(self, *a):
        return self.es.__exit__(*a)

    def _newsem(self, owner=None):
        s = self.es.enter_context(self.nc.semaphore(f"s{self.semid}"))
        self.semid += 1
        rec = [s, 0, self.semid, owner]
        self.all_sems.append(rec)
        return rec

    def sb(self, name, shape, dt, stack=None):
        self.nsb = getattr(self, "nsb", 0) + 1
        name = f"{name}_u{self.nsb}"
        stk = stack or self.es
        t = stk.enter_context(self.nc.sbuf_tensor(name, shape, dt))
        nb = int(np.prod(shape[1:])) * (2 if dt == BF16 else 4)
        self.live = getattr(self, "live", 0) + nb
        self.peak = max(getattr(self, "peak", 0), self.live)

        def _free(n=nb):
            self.live -= n
        stk.callback(_free)
        return Buf(name, t)

    def dram(self, name):
        if name not in self.drams:
            self.drams[name] = Buf(name)
        return self.drams[name]

    def psum(self):
        b = self.psums[self.ps_rr % 8]
        self.ps_rr += 1
        return b

    def _wait(self, e, tok):
        rec, val = tok
        key = rec[2]
        if e == "pe" and rec[3] == "pe":
            return
        if self.waited[e].get(key, 0) >= val:
            return
        self.waited[e][key] = val
        self.engs[e].wait_ge(rec[0], val)

    def _deps(self, e, reads, writes):
        toks = []
        for b in reads:
            if b.w is not None:
                toks.append(b.w)
        for b in writes:
            if b.w is not None:
                toks.append(b.w)
            toks.extend(b.r.values())
        for t in toks:
            self._wait(e, t)

    def _record(self, tok, reads, writes):
        for b in reads:
            key = tok[0][2]
            old = b.r.get(key)
            if old is None or old[1] < tok[1]:
                b.r[key] = tok
        for b in writes:
            b.w = tok
            b.r = {}

    def op(self, e, fn, reads=(), writes=()):
        pr = [b for b in reads if b.psum]
        if pr:
            reads = [b for b in reads if not b.psum]
            writes = list(writes) + pr
        self._deps(e, reads, writes)
        rec = self.cur[e]
        if rec[1] >= SEM_LIMIT:
            rec = self.cur[e] = self._newsem(e)
        inst = fn(self.engs[e])
        rec[1] += 1
        inst.then_inc(rec[0], 1)
        tok = (rec, rec[1])
        self._record(tok, reads, writes)
        self.ninst += 1
        return tok

    def dma(self, q, out, in_, reads=(), writes=(), **kw):
        self._deps(q, reads, writes)
        i = self.dma_rr % len(self.dma_pool)
        self.dma_rr += 1
        rec = self.dma_pool[i]
        if rec[1] >= SEM_LIMIT:
            rec = self.dma_pool[i] = self._newsem()
        if rec[1] > 0:
            self._wait(q, (rec, rec[1]))
        inst = self.engs[q].dma_start(out=out, in_=in_, **kw)
        rec[1] += 16
        inst.then_inc(rec[0], 16)
        tok = (rec, rec[1])
        self._record(tok, reads, writes)
        self.ninst += 1
        return tok

    def barrier(self):
        for e in self.ENG:
            for rec in self.all_sems:
                if rec[1] > 0:
                    self._wait(e, (rec, rec[1]))

    def finish(self):
        for rec in self.all_sems:
            if rec[1] > 0:
                self._wait("sp", (rec, rec[1]))
                self._wait("act", (rec, rec[1]))


D = 2048
TP = 2048
TS = 64
T = TP + TS
NT = 17
DEPTH = 4
DFF = 5632
EPS = 1e-6
TMW = 7424
FMW = 9216
NPK = 1616
NCORES = 8

_o = [0, 512, 1024, 2048, 3072, 6144, 6152, 6160, 7184, 7696, 8208, 9232, 9248, 10272, 16416]
TM_IDX = np.concatenate([np.arange(0, 3072), np.arange(6160, 7184), np.arange(7184, 9232),
                         np.arange(9248, 10272), np.arange(6144, 6160), np.arange(9232, 9248)])
FM_IDX = np.concatenate([np.arange(3072, 6144), np.arange(10272, 16416)])
C_RQK, C_RV, C_RG, C_DZ, C_LQK, C_LV, C_LGT, C_DAB, C_LLR = 0, 1024, 2048, 3072, 4096, 5120, 6144, 7168, 7184
PK_GMIX, PK_GFFN, PK_CONV, PK_BM, PK_ALOG, PK_DTB, PK_GDNN, PK_GLAN, PK_BUP, PK_WUP, PK_GFIN = \
    0, 16, 32, 128, 176, 184, 192, 320, 576, 1088, 1600


def _build_consts():
    items = {}
    i = np.arange(128)
    ident = np.eye(128, dtype=np.float64)
    items["ident"] = ident
    items["ones"] = np.ones((128, 128))
    mti = (i[:, None] <= i[None, :]).astype(np.float64)
    mts = (i[:, None] < i[None, :]).astype(np.float64)
    items["MTi_p"] = mti
    items["MTs_p"] = mts
    items["Mi_p"] = mti.T.copy()
    items["Ms_p"] = mts.T.copy()
    items["ALL_p"] = np.ones((128, 128))
    same = np.zeros((128, 128))
    same[:64, :64] = (i[:64, None] // 4 == i[None, :64] // 4)
    items["MTi_s"] = mti * same
    items["MTs_s"] = mts * same
    items["Mi_s"] = mti.T * same
    items["Ms_s"] = mts.T * same
    items["ALL_s"] = same
    ssp = np.zeros((128, 16)); ssp[:, 0] = 1.0
    items["SEL_p"] = ssp
    sss = np.zeros((128, 16)); sss[np.arange(64), np.arange(64) // 4] = 1.0
    items["SEL_s"] = sss
    bm = np.zeros((128, 16, 64))
    bm[:, np.arange(64) // 4, np.arange(64)] = 1.0
    items["BM"] = bm.reshape(128, 1024)
    gam = 1.0 - 2.0 ** (-5.0 - np.arange(4, dtype=np.float64))
    rdp = np.zeros((128, 4, 128)); rds = np.zeros((128, 4, 128))
    dqkp = np.zeros((128, 8)); dqks = np.zeros((128, 8))
    for h in range(4):
        rdp[:, h, :] = mti * gam[h] ** (-128.0)
        rds[:, h, :] = mti * same * gam[h] ** (-4.0)
        dqkp[:, h] = gam[h] ** (i + 1.0)
        dqkp[:, 4 + h] = gam[h] ** (127.0 - i) * 128 ** -0.5
        t = (i % 4).astype(np.float64)
        dqks[:, h] = gam[h] ** (t + 1.0)
        dqks[:, 4 + h] = gam[h] ** (3.0 - t) * 128 ** -0.5
    items["RD_p"] = rdp.reshape(128, 512)
    items["RD_s"] = rds.reshape(128, 512)
    items["DQK_p"] = dqkp
    items["DQK_s"] = dqks
    off = {}
    cols = []
    o = 0
    for name, a in items.items():
        off[name] = (o, a.shape[1])
        o += a.shape[1]
        cols.append(a)
    return off, o, np.concatenate(cols, axis=1).astype(np.float32), gam


CST_OFF, NCST, CST_ARR, GAMMA = _build_consts()


def blocks(total, step, t0=0):
    out = []
    t = 0
    while t < total:
        n = min(step, total - t)
        out.append((t0 + t, n))
        t += n
    return out


TOKB = blocks(TP, 512) + [(TP, TS)]


class Model:
    def __init__(self, nc, debug=False, phases=None, nl=DEPTH):
        self.nc = nc
        self.k = Ctx(nc)
        self.debug = debug
        self.phases = phases
        self.nl = nl
        d = self.dr = {}

        def di(name, shape, dt=F32):
            d[name] = nc.dram_tensor(name, list(shape), dt, kind="ExternalInput").ap()

        def do(name, shape, dt=F32):
            d[name] = nc.dram_tensor(name, list(shape), dt, kind="ExternalOutput").ap()

        def ds(name, shape, dt=F32):
            d[name] = nc.dram_tensor(name, list(shape), dt, kind="ExternalOutput" if debug else "Internal").ap()

        NLD = self.nl if debug else DEPTH
        di("xT", [D, T])
        di("wtm", [NLD, D, TMW]); di("wfm", [NLD, D, FMW])
        di("wbr", [NLD, 3, 1024, D]); di("wo", [NLD, D, D])
        di("wgu", [NLD, D, 2 * DFF]); di("wdn", [NLD, DFF, D])
        di("pack", [NLD, 128, NPK])
        di("cst", [128, NCST])
        di("cossin", [T, 128])
        di("sret", [NLD, 16, 4, 128, 256]); di("sgdn", [NLD, 16, 8, 128, 128])
        di("sgla", [NLD, 16, 4, 128, 256]); di("sconv", [NLD, 128, 24 * 16 * 3])
        do("yT", [D, T])
        do("o_pret", [NLD, 4, 128, 256]); do("o_pgdn", [NLD, 8, 128, 128])
        do("o_pconv", [NLD, 3072, 3]); do("o_pgla", [NLD, 4, 128, 256])
        do("o_sret", [NLD, 16, 4, 128, 256]); do("o_sgdn", [NLD, 16, 8, 128, 128])
        do("o_sconv", [NLD, 128, 24 * 16 * 3]); do("o_sgla", [NLD, 16, 4, 128, 256])
        ds("XR", [D, T]); ds("PTM", [T, TMW]); ds("PFM", [FMW, T])
        ds("YBT", [3072, T], BF16); ds("MT", [D, T], BF16); ds("FT", [DFF, T], BF16)

    def build(self):
        k = self.k
        with k:
            self.cst = k.sb("cst", [128, NCST], F32)
            k.dma("sp", self.cst.ap, self.dr["cst"], writes=[self.cst])
            self.pk = k.sb("pk", [128, NPK], F32)
            k.dma("sp", self.dr["XR"], self.dr["xT"])
            k.barrier()
            ph = self.phases or ("win", "ret", "gla", "gdn", "branch", "wo", "ffn", "final")
            for l in range(self.nl):
                k.dma("sp", self.pk.ap, self.dr["pack"][l], writes=[self.pk])
                k.barrier()
                for p in ("win", "ret", "gla", "gdn", "branch", "wo", "ffn"):
                    if p in ph:
                        getattr(self, "phase_" + p)(l)
            if "final" in ph:
                self.phase_final()
            k.finish()

    def C(self, name, rows=128):
        o, n = CST_OFF[name]
        return self.cst.t[0:rows, o:o + n]

    def rmsnorm_fm(self, st, gain_off, hT=None, dst=None):
        k = self.k
        src = self.dr["XR"].rearrange("(kc p) t -> p kc t", p=128)
        xs = [k.sb(f"nx{i}", [128, 16, 256], F32, st) for i in range(2)]
        sq = k.sb("nsq", [128, 16, 256], F32, st)
        r1 = k.sb("nr1", [128, 256], F32, st)
        r2 = k.sb("nr2", [128, 256], F32, st)
        r3 = k.sb("nr3", [128, 256], F32, st)
        oo = [k.sb(f"no{i}", [128, 16, 256], F32, st) for i in range(2)] if dst is not None else None
        ones = self.C("ones")
        for bi, (t0, n) in enumerate(blocks(T, 256)):
            xb = xs[bi % 2]
            k.dma("sp", xb.t[:, :, 0:n], src[:, :, t0:t0 + n], writes=[xb])
            k.op("act", lambda e: e.activation(sq.t[:, :, 0:n], xb.t[:, :, 0:n], AF.Square), reads=[xb], writes=[sq])
            ps = k.psum()
            for kc in range(16):
                k.op("pe", lambda e: e.matmul(ps.t[:, 0:n], ones, sq.t[:, kc, 0:n], start=(kc == 0), stop=(kc == 15)),
                     reads=[sq, self.cst], writes=[ps])
            k.op("dve", lambda e: e.tensor_scalar(r1.t[:, 0:n], ps.t[:, 0:n], 1.0 / D, EPS, ALU.mult, ALU.add),
                 reads=[ps], writes=[r1])
            k.op("act", lambda e: e.activation(r2.t[:, 0:n], r1.t[:, 0:n], AF.Sqrt), reads=[r1], writes=[r2])
            k.op("dve", lambda e: e.reciprocal(r3.t[:, 0:n], r2.t[:, 0:n]), reads=[r2], writes=[r3])
            for kc in range(16):
                g = self.pk.t[:, gain_off + kc:gain_off + kc + 1]
                if dst is None:
                    k.op("dve", lambda e: e.scalar_tensor_tensor(hT.t[:, kc, t0:t0 + n], xb.t[:, kc, 0:n], g,
                                                                 r3.t[:, 0:n], ALU.mult, ALU.mult),
                         reads=[xb, r3, self.pk], writes=[hT])
                else:
                    ob = oo[bi % 2]
                    k.op("dve", lambda e: e.scalar_tensor_tensor(ob.t[:, kc, 0:n], xb.t[:, kc, 0:n], g,
                                                                 r3.t[:, 0:n], ALU.mult, ALU.mult),
                         reads=[xb, r3, self.pk], writes=[ob])
            if dst is not None:
                ob = oo[bi % 2]
                k.dma("pool", dst.rearrange("(kc p) t -> p kc t", p=128)[:, :, t0:t0 + n], ob.t[:, :, 0:n], reads=[ob])

    def load_w_dma(self, wraw, src):
        self.k.dma("sp", wraw.ap, src, writes=[wraw])

    def load_w_cast(self, wraw, wbf, KCi):
        k = self.k
        h = KCi // 2
        k.op("act", lambda e: e.activation(wbf.t[:, 0:h, :], wraw.t[:, 0:h, :], AF.Copy), reads=[wraw], writes=[wbf])
        k.op("pool", lambda e: e.tensor_copy(wbf.t[:, h:KCi, :], wraw.t[:, h:KCi, :]), reads=[wraw], writes=[wbf])

    def load_w(self, wraw, wbf, src, KCi):
        k = self.k
        k.dma("sp", wraw.ap, src, writes=[wraw])
        h = KCi // 2
        k.op("act", lambda e: e.activation(wbf.t[:, 0:h, :], wraw.t[:, 0:h, :], AF.Copy), reads=[wraw], writes=[wbf])
        k.op("pool", lambda e: e.tensor_copy(wbf.t[:, h:KCi, :], wraw.t[:, h:KCi, :]), reads=[wraw], writes=[wbf])

    def dense_fm(self, st, inT, KCi, tokb, wsrc_fn, ncols, WN, epi, pre=None, tag="w", tbase=0):
        k = self.k
        wraw = [k.sb(f"{tag}raw{i}", [128, KCi, WN], F32, st) for i in range(2)]
        wbf = [k.sb(f"{tag}bf{i}", [128, KCi, WN], BF16, st) for i in range(2)]
        nw = ncols // WN
        self.load_w(wraw[0], wbf[0], wsrc_fn(0, WN), KCi)
        for wi in range(nw):
            wr, wb = wraw[wi % 2], wbf[wi % 2]
            if wi + 1 < nw:
                self.load_w_dma(wraw[(wi + 1) % 2], wsrc_fn((wi + 1) * WN, WN))
            for bi_, (t0, n) in enumerate(tokb):
                if wi + 1 < nw and bi_ == max(0, len(tokb) - 2):
                    self.load_w_cast(wraw[(wi + 1) % 2], wbf[(wi + 1) % 2], KCi)
                for sub in range(WN // 128):
                    c0 = wi * WN + sub * 128
                    if pre is not None:
                        pre(c0, t0, n)
                    ps = k.psum()
                    for kc in range(KCi):
                        k.op("pe", lambda e: e.matmul(ps.t[:, 0:n], wb.t[:, kc, sub * 128:(sub + 1) * 128],
                                                      inT.t[:, kc, t0 - tbase:t0 - tbase + n],
                                                      start=(kc == 0), stop=(kc == KCi - 1)),
                             reads=[wb, inT], writes=[ps])
                    epi(c0, t0, n, ps)

    def phase_win(self, l):
        k = self.k
        with contextlib.ExitStack() as st:
            hT = k.sb("hT", [128, 16, T], BF16, st)
            with contextlib.ExitStack() as st2:
                self.rmsnorm_fm(st2, PK_GMIX, hT=hT)
                k.barrier()
            with contextlib.ExitStack() as st2:
                wraw = [k.sb(f"tmraw{i}", [128, 16, 256], F32, st2) for i in range(2)]
                wbf = [k.sb(f"tmbf{i}", [128, 16, 256], BF16, st2) for i in range(2)]
                ost = [k.sb(f"tmo{i}", [128, 256], F32, st2) for i in range(4)]
                wsrc = self.dr["wtm"][l].rearrange("(kc p) n -> p kc n", p=128)
                cnt = 0
                self.load_w(wraw[0], wbf[0], wsrc[:, :, 0:256], 16)
                for ci in range(TMW // 256):
                    c0 = ci * 256
                    ncol = min(256, 7200 - c0)
                    wr, wb = wraw[ci % 2], wbf[ci % 2]
                    if ci + 1 < TMW // 256:
                        self.load_w_dma(wraw[(ci + 1) % 2], wsrc[:, :, c0 + 256:c0 + 512])
                    for tt in range(NT):
                        if ci + 1 < TMW // 256 and tt == 13:
                            self.load_w_cast(wraw[(ci + 1) % 2], wbf[(ci + 1) % 2], 16)
                        R = 128 if tt < 16 else 64
                        ps = k.psum()
                        for kc in range(16):
                            k.op("pe", lambda e: e.matmul(ps.t[0:R, 0:ncol], hT.t[:, kc, tt * 128:tt * 128 + R],
                                                          wb.t[:, kc, 0:ncol], start=(kc == 0), stop=(kc == 15)),
                                 reads=[wb, hT], writes=[ps])
                        ob = ost[cnt % 4]
                        if cnt % 2 == 0:
                            k.op("act", lambda e: e.activation(ob.t[0:R, 0:ncol], ps.t[0:R, 0:ncol], AF.Copy),
                                 reads=[ps], writes=[ob])
                        else:
                            k.op("dve", lambda e: e.tensor_copy(ob.t[0:R, 0:ncol], ps.t[0:R, 0:ncol]),
                                 reads=[ps], writes=[ob])
                        k.dma("pool" if cnt % 2 else "sp", self.dr["PTM"][tt * 128:tt * 128 + R, c0:c0 + ncol],
                              ob.t[0:R, 0:ncol], reads=[ob])
                        cnt += 1
                k.barrier()
            with contextlib.ExitStack() as st2:
                ost = [k.sb(f"fmo{i}", [128, 512], F32, st2) for i in range(4)]
                wsrc = self.dr["wfm"][l].rearrange("(kc p) n -> p kc n", p=128)
                cnt = [0]

                def epi(c0, t0, n, ps):
                    ob = ost[cnt[0] % 4]
                    if c0 < 3072:
                        if cnt[0] % 2 == 0:
                            k.op("act", lambda e: e.activation(ob.t[:, 0:n], ps.t[:, 0:n], AF.Copy), reads=[ps], writes=[ob])
                        else:
                            k.op("dve", lambda e: e.tensor_copy(ob.t[:, 0:n], ps.t[:, 0:n]), reads=[ps], writes=[ob])
                    else:
                        ch = (c0 - 3072) // 128
                        b = self.pk.t[:, PK_BM + ch:PK_BM + ch + 1]
                        k.op("act", lambda e: e.activation(ob.t[:, 0:n], ps.t[:, 0:n], AF.Sigmoid, bias=b),
                             reads=[ps, self.pk], writes=[ob])
                    k.dma("pool" if cnt[0] % 2 else "sp", self.dr["PFM"][c0:c0 + 128, t0:t0 + n], ob.t[:, 0:n], reads=[ob])
                    cnt[0] += 1

                self.dense_fm(st2, hT, 16, TOKB, lambda c, w: wsrc[:, :, c:c + w], FMW, 256, epi, tag="fm")
                k.barrier()

    def load_act_bf(self, dst, src_fm, KCi, t0, n):
        k = self.k
        v = src_fm.rearrange("(kc p) t -> p kc t", p=128)
        for (a, m) in blocks(KCi, 8):
            k.dma("sp", dst.t[:, a:a + m, 0:n], v[:, a:a + m, t0:t0 + n], writes=[dst])

    def phase_branch(self, l):
        k = self.k
        for (s0, sn) in blocks(T, 1056):
            with contextlib.ExitStack() as st:
                ybt = k.sb("ybt", [128, 24, 1056], BF16, st)
                self.load_act_bf(ybt, self.dr["YBT"], 24, s0, sn)
                wraw = [k.sb(f"brraw{i}", [128, 8, 256], F32, st) for i in range(3)]
                wbf = [[k.sb(f"brbf{j}_{i}", [128, 8, 256], BF16, st) for i in range(3)] for j in range(2)]
                gts = [k.sb(f"brg{i}", [128, 512], F32, st) for i in range(6)]
                tmp = [k.sb(f"brt{i}", [128, 512], F32, st) for i in range(6)]
                mo = [k.sb(f"brm{i}", [128, 512], BF16, st) for i in range(2)]
                cnt = 0
                for ci in range(D // 256):
                    for b in range(3):
                        src = self.dr["wbr"][l, b].rearrange("(kc p) n -> p kc n", p=128)[:, :, ci * 256:(ci + 1) * 256]
                        self.load_w(wraw[b], wbf[ci % 2][b], src, 8)
                    for (t0, n) in blocks(sn, 512, s0):
                        for sub in range(2):
                            c0 = ci * 256 + sub * 128
                            tl = []
                            for b in range(3):
                                wb = wbf[ci % 2][b]
                                gt = gts[(cnt * 3 + b) % 6]
                                r0 = 3072 + b * 2048 + c0
                                k.dma("pool", gt.t[:, 0:n], self.dr["PFM"][r0:r0 + 128, t0:t0 + n], writes=[gt])
                                ps = k.psum()
                                for kc in range(8):
                                    k.op("pe", lambda e: e.matmul(ps.t[:, 0:n], wb.t[:, kc, sub * 128:(sub + 1) * 128],
                                                                  ybt.t[:, b * 8 + kc, t0 - s0:t0 - s0 + n],
                                                                  start=(kc == 0), stop=(kc == 7)),
                                         reads=[wb, ybt], writes=[ps])
                                tb = tmp[(cnt * 3 + b) % 6]
                                k.op("dve", lambda e: e.tensor_tensor(tb.t[:, 0:n], ps.t[:, 0:n], gt.t[:, 0:n], ALU.mult),
                                     reads=[ps, gt], writes=[tb])
                                tl.append(tb)
                            k.op("pool", lambda e: e.tensor_tensor(tl[0].t[:, 0:n], tl[0].t[:, 0:n], tl[1].t[:, 0:n], ALU.add),
                                 reads=[tl[1], tl[0]], writes=[tl[0]])
                            m = mo[cnt % 2]
                            k.op("pool", lambda e: e.tensor_tensor(m.t[:, 0:n], tl[0].t[:, 0:n], tl[2].t[:, 0:n], ALU.add),
                                 reads=[tl[0], tl[2]], writes=[m])
                            k.dma("sp", self.dr["MT"][c0:c0 + 128, t0:t0 + n], m.t[:, 0:n], reads=[m])
                            cnt += 1
                k.barrier()

    def resid_epi(self, st, tag):
        k = self.k
        xin = [k.sb(f"{tag}xi{i}", [128, 512], F32, st) for i in range(4)]
        cnt = [0]
        cur = {}

        def pre(c0, t0, n):
            xb = xin[cnt[0] % 4]
            k.dma("pool", xb.t[:, 0:n], self.dr["XR"][c0:c0 + 128, t0:t0 + n], writes=[xb])
            cur["xb"] = xb

        def epi(c0, t0, n, ps):
            xb = cur["xb"]
            k.op("dve", lambda e: e.tensor_tensor(xb.t[:, 0:n], xb.t[:, 0:n], ps.t[:, 0:n], ALU.add), reads=[ps, xb], writes=[xb])
            k.dma("pool", self.dr["XR"][c0:c0 + 128, t0:t0 + n], xb.t[:, 0:n], reads=[xb])
            cnt[0] += 1

        return pre, epi

    def phase_wo(self, l):
        k = self.k
        with contextlib.ExitStack() as st:
            mt = k.sb("mt", [128, 16, T], BF16, st)
            self.load_act_bf(mt, self.dr["MT"], 16, 0, T)
            pre, epi = self.resid_epi(st, "wo")
            wsrc = self.dr["wo"][l].rearrange("(kc p) n -> p kc n", p=128)
            self.dense_fm(st, mt, 16, TOKB, lambda c, w: wsrc[:, :, c:c + w], D, 256, epi, pre=pre, tag="wo")
            k.barrier()

    def phase_ffn(self, l):
        k = self.k
        with contextlib.ExitStack() as st:
            hT = k.sb("h2T", [128, 16, T], BF16, st)
            with contextlib.ExitStack() as st2:
                self.rmsnorm_fm(st2, PK_GFFN, hT=hT)
                k.barrier()
            sg = [k.sb(f"sg{i}", [128, 512], F32, st) for i in range(2)]
            fo = [k.sb(f"fo{i}", [128, 512], BF16, st) for i in range(2)]
            cnt = [0]

            def epi(c0, t0, n, ps):
                ft = c0 // 256
                if (c0 // 128) % 2 == 0:
                    s_ = sg[cnt[0] % 2]
                    k.op("act", lambda e: e.activation(s_.t[:, 0:n], ps.t[:, 0:n], AF.Silu), reads=[ps], writes=[s_])
                else:
                    s_ = sg[cnt[0] % 2]
                    o_ = fo[cnt[0] % 2]
                    k.op("dve", lambda e: e.tensor_tensor(o_.t[:, 0:n], s_.t[:, 0:n], ps.t[:, 0:n], ALU.mult),
                         reads=[ps, s_], writes=[o_])
                    k.dma("pool", self.dr["FT"][ft * 128:(ft + 1) * 128, t0:t0 + n], o_.t[:, 0:n], reads=[o_])
                    cnt[0] += 1

            wsrc = self.dr["wgu"][l].rearrange("(kc p) n -> p kc n", p=128)
            self.dense_fm(st, hT, 16, TOKB, lambda c, w: wsrc[:, :, c:c + w], 2 * DFF, 256, epi, tag="gu")
            k.barrier()
        for (s0, sn) in blocks(T, 704):
            with contextlib.ExitStack() as st:
                ft = k.sb("ftT", [128, 44, 704], BF16, st)
                self.load_act_bf(ft, self.dr["FT"], 44, s0, sn)
                pre, epi = self.resid_epi(st, "dn")
                wsrc = self.dr["wdn"][l].rearrange("(kc p) n -> p kc n", p=128)
                self.dense_fm(st, ft, 44, blocks(sn, 512, s0), lambda c, w: wsrc[:, :, c:c + w], D, 128, epi, pre=pre,
                              tag="dn", tbase=s0)
                k.barrier()

    def phase_final(self):
        k = self.k
        with contextlib.ExitStack() as st:
            self.rmsnorm_fm(st, PK_GFIN, dst=self.dr["yT"])
            k.barrier()

    def transposes(self, src_fn, nch, dst, R, reads, evac_scale=None):
        k = self.k
        ident = self.C("ident")
        for g0 in range(0, nch, 4):
            m = min(4, nch - g0)
            ps = k.psum()
            for c in range(m):
                k.op("pe", lambda e: e.transpose(ps.t[:, c * 128:c * 128 + R], src_fn(g0 + c), ident[0:R, 0:R]),
                     reads=list(reads) + [self.cst], writes=[ps])
            pv = ps.t[:, :].rearrange("p (a b) -> p a b", a=4)[:, 0:m, 0:R]
            if (g0 // 4) % 2 == 0:
                k.op("act", lambda e: e.activation(dst.t[:, g0:g0 + m, 0:R], pv, AF.Copy), reads=[ps], writes=[dst])
            else:
                k.op("dve", lambda e: e.tensor_copy(dst.t[:, g0:g0 + m, 0:R], pv), reads=[ps], writes=[dst])

    def rstd_cols(self, ss, r1, r2, rstd, R, n, inv_n):
        k = self.k
        k.op("dve", lambda e: e.tensor_scalar(r1.t[0:R, 0:n], ss.t[0:R, 0:n], inv_n, EPS, ALU.mult, ALU.add), reads=[ss], writes=[r1])
        k.op("act", lambda e: e.activation(r2.t[0:R, 0:n], r1.t[0:R, 0:n], AF.Sqrt), reads=[r1], writes=[r2])
        k.op("dve", lambda e: e.reciprocal(rstd.t[0:R, 0:n], r2.t[0:R, 0:n]), reads=[r2], writes=[rstd])

    def phase_la(self, l, kind):
        k = self.k
        X = mybir.AxisListType.X
        ret = (kind == "ret")
        cqk, cv, cg = (C_RQK, C_RV, C_RG) if ret else (C_LQK, C_LV, C_LGT)
        s_in = self.dr["sret" if ret else "sgla"]
        s_out = self.dr["o_sret" if ret else "o_sgla"]
        p_out = self.dr["o_pret" if ret else "o_pgla"]
        ybase = 0 if ret else 2048
        PTM = self.dr["PTM"]
        with contextlib.ExitStack() as st:
            sb = lambda n, shp, dt=F32: k.sb(f"{kind}_{n}", shp, dt, st)
            qk = [sb(f"qk{i}", [128, 1024]) for i in range(2)]
            vv = [sb(f"v{i}", [128, 1024]) for i in range(2)]
            gg = [sb(f"g{i}", [128, 1024]) for i in range(2)]
            cs = [sb(f"cs{i}", [128, 128]) for i in range(2)]
            lr = [sb(f"lr{i}", [128, 16]) for i in range(2)]
            t1, t2, t3, t4 = [sb(f"t{i}", [128, 512]) for i in range(4)]
            qkr = sb("qkr", [128, 1024])
            qkd = sb("qkd", [128, 1024])
            qkT = sb("qkT", [128, 8, 128], BF16)
            kdb = sb("kdb", [128, 4, 128], BF16)
            vbf = sb("vbf", [128, 1024], BF16)
            sgt = sb("sgt", [128, 1024])
            scb = [sb(f"sc{i}", [128, 128], BF16) for i in range(2)]
            sqo = sb("sqo", [128, 256])
            y = sb("y", [128, 1024])
            yT = [sb(f"yT{i}", [128, 8, 128], BF16) for i in range(2)]
            ss = sb("ss", [128, 4]); r1 = sb("r1", [128, 4]); r2 = sb("r2", [128, 4]); rstd = sb("rstd", [128, 4])
            S = sb("S", [128, 4, 256]); Sbf = sb("Sbf", [128, 4, 256], BF16)
            Sall = sb("Sall", [128, 16, 256]); Sallbf = sb("Sallbf", [128, 16, 256], BF16)
            Snew = sb("Snew", [128, 16, 256])
            qbig = sb("qbig", [128, 16, 64], BF16); kbig = sb("kbig", [128, 16, 128], BF16)
            if not ret:
                z = sb("z", [128, 512]); ez = sb("ez", [128, 512]); lz = sb("lz", [128, 512])
                Bsb = sb("Bsb", [128, 512]); eb = sb("eb", [128, 512]); enb = sb("enb", [128, 512])
                dfb = sb("dfb", [128, 512]); ekd = sb("ekd", [128, 512])
                llrT = sb("llrT", [16, 128]); ebl = sb("ebl", [128, 4, 16])
            k.op("dve", lambda e: e.memset(S.ap, 0.0), writes=[S])
            k.op("pool", lambda e: e.memset(Sbf.ap, 0.0), writes=[Sbf])

            def load(tt):
                R = 128 if tt < 16 else 64
                r0 = tt * 128
                i = tt % 2
                k.dma("sp", qk[i].t[0:R, :], PTM[r0:r0 + R, cqk:cqk + 1024], writes=[qk[i]])
                k.dma("sp", vv[i].t[0:R, :], PTM[r0:r0 + R, cv:cv + 1024], writes=[vv[i]])
                k.dma("sp", gg[i].t[0:R, :], PTM[r0:r0 + R, cg:cg + 1024], writes=[gg[i]])
                if ret:
                    k.dma("sp", cs[i].t[0:R, :], self.dr["cossin"][r0:r0 + R, :], writes=[cs[i]])
                else:
                    k.dma("sp", lr[i].t[0:R, :], PTM[r0:r0 + R, C_LLR:C_LLR + 16], writes=[lr[i]])

            load(0)
            for tt in range(NT):
                if tt + 1 < NT:
                    load(tt + 1)
                R = 128 if tt < 16 else 64
                smp = tt == 16
                sfx = "_s" if smp else "_p"
                i = tt % 2
                qkb, vb, gb = qk[i], vv[i], gg[i]
                if ret:
                    x4 = qkb.t[0:R, :].rearrange("p (g d two) -> p g d two", g=8, two=2)
                    o4 = qkr.t[0:R, :].rearrange("p (g d two) -> p g d two", g=8, two=2)
                    x1, x2 = x4[:, :, :, 0], x4[:, :, :, 1]
                    cosb = cs[i].t[0:R, 0:64].unsqueeze(1).to_broadcast([R, 8, 64])
                    sinb = cs[i].t[0:R, 64:128].unsqueeze(1).to_broadcast([R, 8, 64])
                    v3 = lambda b: b.t[0:R, :].rearrange("p (g d) -> p g d", g=8)
                    k.op("dve", lambda e: e.tensor_tensor(v3(t1), x1, cosb, ALU.mult), reads=[qkb, cs[i]], writes=[t1])
                    k.op("pool", lambda e: e.tensor_tensor(v3(t2), x2, sinb, ALU.mult), reads=[qkb, cs[i]], writes=[t2])
                    k.op("dve", lambda e: e.tensor_tensor(o4[:, :, :, 0], v3(t1), v3(t2), ALU.subtract), reads=[t1, t2], writes=[qkr])
                    k.op("pool", lambda e: e.tensor_tensor(v3(t3), x1, sinb, ALU.mult), reads=[qkb, cs[i]], writes=[t3])
                    k.op("dve", lambda e: e.tensor_tensor(v3(t4), x2, cosb, ALU.mult), reads=[qkb, cs[i]], writes=[t4])
                    k.op("pool", lambda e: e.tensor_tensor(o4[:, :, :, 1], v3(t3), v3(t4), ALU.add), reads=[t3, t4], writes=[qkr])
                    tab = self.C("DQK" + sfx, R).unsqueeze(2).to_broadcast([R, 8, 128])
                    k.op("dve", lambda e: e.tensor_tensor(qkd.t[0:R, :].rearrange("p (g d) -> p g d", g=8),
                                                          qkr.t[0:R, :].rearrange("p (g d) -> p g d", g=8), tab, ALU.mult),
                         reads=[qkr, self.cst], writes=[qkd])
                else:
                    ps = k.psum()
                    k.op("pe", lambda e: e.transpose(ps.t[0:16, 0:R], lr[i].t[0:R, 0:16], self.C("ident")[0:R, 0:R]),
                         reads=[lr[i], self.cst], writes=[ps])
                    k.op("act", lambda e: e.activation(llrT.t[0:16, 0:R], ps.t[0:16, 0:R], AF.Copy), reads=[ps], writes=[llrT])
                    ps = k.psum()
                    k.op("pe", lambda e: e.matmul(ps.t[0:R, 0:512], llrT.t[0:16, 0:R], self.pk.t[0:16, PK_WUP:PK_WUP + 512],
                                                  start=True, stop=True), reads=[llrT, self.pk], writes=[ps])
                    k.op("dve", lambda e: e.tensor_tensor(z.t[0:R, :], ps.t[0:R, 0:512], self.pk.t[0:R, PK_BUP:PK_BUP + 512], ALU.add),
                         reads=[ps, self.pk], writes=[z])
                    k.op("act", lambda e: e.activation(ez.t[0:R, :], z.t[0:R, :], AF.Exp, scale=-1.0), reads=[z], writes=[ez])
                    k.op("dve", lambda e: e.tensor_scalar(ez.t[0:R, :], ez.t[0:R, :], 1.0, None, ALU.add), reads=[ez], writes=[ez])
                    k.op("act", lambda e: e.activation(lz.t[0:R, :], ez.t[0:R, :], AF.Ln), reads=[ez], writes=[lz])
                    psB = k.psum()
                    k.op("pe", lambda e: e.matmul(psB.t[0:R, 0:512], self.C("MTi" + sfx, R)[:, 0:R], lz.t[0:R, :], start=True, stop=True),
                         reads=[lz, self.cst], writes=[psB])
                    psL = k.psum()
                    k.op("pe", lambda e: e.matmul(psL.t[0:R, 0:512], self.C("ALL" + sfx, R)[:, 0:R], lz.t[0:R, :], start=True, stop=True),
                         reads=[lz, self.cst], writes=[psL])
                    k.op("act", lambda e: e.activation(Bsb.t[0:R, :], psB.t[0:R, 0:512], AF.Copy), reads=[psB], writes=[Bsb])
                    k.op("act", lambda e: e.activation(eb.t[0:R, :], Bsb.t[0:R, :], AF.Exp, scale=-1.0 / 16), reads=[Bsb], writes=[eb])
                    k.op("act", lambda e: e.activation(enb.t[0:R, :], Bsb.t[0:R, :], AF.Exp, scale=1.0 / 16), reads=[Bsb], writes=[enb])
                    k.op("dve", lambda e: e.tensor_tensor(dfb.t[0:R, :], Bsb.t[0:R, :], psL.t[0:R, 0:512], ALU.subtract),
                         reads=[Bsb, psL], writes=[dfb])
                    k.op("act", lambda e: e.activation(ekd.t[0:R, :], dfb.t[0:R, :], AF.Exp, scale=1.0 / 16), reads=[dfb], writes=[ekd])
                    k.op("dve", lambda e: e.scalar_tensor_tensor(qkd.t[0:R, 0:512], qkb.t[0:R, 0:512], 128 ** -0.5, eb.t[0:R, :],
                                                                 ALU.mult, ALU.mult), reads=[qkb, eb], writes=[qkd])
                    k.op("pool", lambda e: e.tensor_tensor(qkd.t[0:R, 512:1024], qkb.t[0:R, 512:1024], enb.t[0:R, :], ALU.mult),
                         reads=[qkb, enb], writes=[qkd])
                    for h in range(4):
                        ps = k.psum()
                        k.op("pe", lambda e: e.matmul(ps.t[:, 0:16], lz.t[0:R, h * 128:(h + 1) * 128], self.C("SEL" + sfx, R),
                                                      start=True, stop=True), reads=[lz, self.cst], writes=[ps])
                        k.op("act", lambda e: e.activation(ebl.t[:, h, :], ps.t[:, 0:16], AF.Exp, scale=-1.0 / 16), reads=[ps], writes=[ebl])
                self.transposes(lambda c: qkd.t[0:R, c * 128:(c + 1) * 128], 8, qkT, R, [qkd])
                if ret:
                    k.op("act", lambda e: e.activation(kdb.t[0:R, :, :], qkd.t[0:R, 512:1024].rearrange("p (g d) -> p g d", g=4), AF.Copy),
                         reads=[qkd], writes=[kdb])
                else:
                    k.op("dve", lambda e: e.tensor_tensor(kdb.t[0:R, :, :], qkb.t[0:R, 512:1024].rearrange("p (g d) -> p g d", g=4),
                                                          ekd.t[0:R, :].rearrange("p (g d) -> p g d", g=4), ALU.mult),
                         reads=[qkb, ekd], writes=[kdb])
                k.op("pool", lambda e: e.tensor_copy(vbf.t[0:R, :], vb.t[0:R, :]), reads=[vb], writes=[vbf])
                k.op("act", lambda e: e.activation(sgt.t[0:R, :], gb.t[0:R, :], AF.Silu), reads=[gb], writes=[sgt])
                if not ret:
                    gn = self.pk.t[0:R, PK_GLAN:PK_GLAN + 256].unsqueeze(1).to_broadcast([R, 4, 256])
                    s3 = sgt.t[0:R, :].rearrange("p (g d) -> p g d", g=4)
                    k.op("pool", lambda e: e.tensor_tensor(s3, s3, gn, ALU.mult), reads=[sgt, self.pk], writes=[sgt])
                for h in range(4):
                    ps_sc = k.psum()
                    k.op("pe", lambda e: e.matmul(ps_sc.t[0:R, 0:R], qkT.t[:, 4 + h, 0:R], qkT.t[:, h, 0:R], start=True, stop=True),
                         reads=[qkT], writes=[ps_sc])
                    sc = scb[h % 2]
                    if ret:
                        mtab = self.C("RD" + sfx, R)[:, h * 128:h * 128 + R]
                    else:
                        mtab = self.C("MTi" + sfx, R)[:, 0:R]
                    k.op("dve", lambda e: e.tensor_tensor(sc.t[0:R, 0:R], ps_sc.t[0:R, 0:R], mtab, ALU.mult),
                         reads=[ps_sc, self.cst], writes=[sc])
                    if smp:
                        k.dma("sp", Sall.ap, s_in[l, :, h].rearrange("s d e -> d s e"), writes=[Sall])
                        k.op("act", lambda e: e.activation(Sallbf.t[:, 0:8, :], Sall.t[:, 0:8, :], AF.Copy), reads=[Sall], writes=[Sallbf])
                        k.op("pool", lambda e: e.tensor_copy(Sallbf.t[:, 8:16, :], Sall.t[:, 8:16, :]), reads=[Sall], writes=[Sallbf])
                        k.op("dve", lambda e: e.tensor_tensor(qbig.ap, qkT.t[:, h, 0:64].unsqueeze(1).to_broadcast([128, 16, 64]),
                                                              self.C("BM").rearrange("p (s i) -> p s i", s=16), ALU.mult),
                             reads=[qkT, self.cst], writes=[qbig])
                    ps_o = k.psum()
                    k.op("pe", lambda e: e.matmul(ps_o.t[0:R, 0:256], sc.t[0:R, 0:R], vbf.t[0:R, h * 256:(h + 1) * 256], start=True, stop=False),
                         reads=[sc, vbf], writes=[ps_o])
                    if not smp:
                        k.op("pe", lambda e: e.matmul(ps_o.t[0:R, 0:256], qkT.t[:, h, 0:R], Sbf.t[:, h, :], start=False, stop=True),
                             reads=[qkT, Sbf], writes=[ps_o])
                    else:
                        for s in range(16):
                            k.op("pe", lambda e: e.matmul(ps_o.t[0:R, 0:256], qbig.t[:, s, :], Sallbf.t[:, s, :], start=False, stop=(s == 15)),
                                 reads=[qbig, Sallbf], writes=[ps_o])
                    k.op("act", lambda e: e.activation(sqo.t[0:R, :], ps_o.t[0:R, 0:256], AF.Square), reads=[ps_o], writes=[sqo])
                    k.op("dve", lambda e: e.tensor_reduce(ss.t[0:R, h:h + 1], sqo.t[0:R, :], X, ALU.add), reads=[sqo], writes=[ss])
                    k.op("dve", lambda e: e.tensor_scalar(r1.t[0:R, h:h + 1], ss.t[0:R, h:h + 1], 1.0 / 256, EPS, ALU.mult, ALU.add), reads=[ss], writes=[r1])
                    k.op("act", lambda e: e.activation(r2.t[0:R, h:h + 1], r1.t[0:R, h:h + 1], AF.Sqrt), reads=[r1], writes=[r2])
                    k.op("dve", lambda e: e.reciprocal(rstd.t[0:R, h:h + 1], r2.t[0:R, h:h + 1]), reads=[r2], writes=[rstd])
                    k.op("dve", lambda e: e.scalar_tensor_tensor(y.t[0:R, h * 256:(h + 1) * 256], ps_o.t[0:R, 0:256], rstd.t[0:R, h:h + 1],
                                                                 sgt.t[0:R, h * 256:(h + 1) * 256], ALU.mult, ALU.mult),
                         reads=[ps_o, rstd, sgt], writes=[y])
                    if not smp:
                        ps_s = k.psum()
                        k.op("pe", lambda e: e.matmul(ps_s.t[:, 0:256], kdb.t[0:R, h, :], vbf.t[0:R, h * 256:(h + 1) * 256], start=True, stop=True),
                             reads=[kdb, vbf], writes=[ps_s])
                        dec = float(GAMMA[h] ** 128.0) if ret else ebl.t[:, h, 0:1]
                        k.op("dve", lambda e: e.scalar_tensor_tensor(S.t[:, h, :], S.t[:, h, :], dec, ps_s.t[:, 0:256], ALU.mult, ALU.add),
                             reads=[ps_s, S] + ([] if ret else [ebl]), writes=[S])
                        k.op("act", lambda e: e.activation(Sbf.t[:, h, :], S.t[:, h, :], AF.Copy), reads=[S], writes=[Sbf])
                        if tt == 15:
                            k.dma("pool", p_out[l, h], S.t[:, h, :], reads=[S])
                    else:
                        k.op("dve", lambda e: e.tensor_tensor(kbig.t[0:64, :, :], kdb.t[0:64, h, :].unsqueeze(1).to_broadcast([64, 16, 128]),
                                                              self.C("SEL_s", 64).unsqueeze(2).to_broadcast([64, 16, 128]), ALU.mult),
                             reads=[kdb, self.cst], writes=[kbig])
                        for s in range(16):
                            ps_s = k.psum()
                            k.op("pe", lambda e: e.matmul(ps_s.t[:, 0:256], kbig.t[0:64, s, :], vbf.t[0:64, h * 256:(h + 1) * 256], start=True, stop=True),
                                 reads=[kbig, vbf], writes=[ps_s])
                            dec = float(GAMMA[h] ** 4.0) if ret else ebl.t[:, h, s:s + 1]
                            k.op("dve", lambda e: e.scalar_tensor_tensor(Snew.t[:, s, :], Sall.t[:, s, :], dec, ps_s.t[:, 0:256], ALU.mult, ALU.add),
                                 reads=[ps_s, Sall] + ([] if ret else [ebl]), writes=[Snew])
                        k.dma("pool", s_out[l, :, h].rearrange("s d e -> d s e"), Snew.ap, reads=[Snew])
                yt = yT[tt % 2]
                self.transposes(lambda c: y.t[0:R, c * 128:(c + 1) * 128], 8, yt, R, [y])
                k.dma("pool", self.dr["YBT"][ybase:ybase + 1024, tt * 128:tt * 128 + R].rearrange("(c p) t -> p c t", p=128),
                      yt.t[:, :, 0:R], reads=[yt])
            k.barrier()

    def phase_ret(self, l):
        self.phase_la(l, "ret")

    def phase_gla(self, l):
        self.phase_la(l, "gla")

    def phase_gdn(self, l):
        k = self.k
        X = mybir.AxisListType.X
        PTM, PFM = self.dr["PTM"], self.dr["PFM"]
        ident = self.C("ident")
        ones = self.C("ones")
        with contextlib.ExitStack() as st:
            sb = lambda n, shp, dt=F32: k.sb(f"gd_{n}", shp, dt, st)
            xc = [sb(f"xc{i}", [128, 24, 131]) for i in range(2)]
            xs = sb("xs", [128, 24, 16, 7])
            xst = sb("xst", [128, 24, 64])
            cst_in = sb("cstin", [128, 24, 16, 3])
            ca = sb("ca", [128, 24, 128]); cb = sb("cb", [128, 24, 128]); cc = cb
            sq = sb("sq", [128, 16, 128]); rr = sb("rr", [128, 16, 128]); rr2 = sq
            qkn = sb("qkn", [128, 16, 128]); qkb = sb("qkb", [128, 16, 128], BF16)
            ktm = sb("ktm", [128, 8, 128]); vtm = sb("vtm", [128, 8, 128])
            dab = [sb(f"dab{i}", [128, 16]) for i in range(2)]
            dzb = [sb(f"dz{i}", [128, 1024]) for i in range(2)]
            sgt = sb("sgt", [128, 1024])
            tg = sb("tg", [128, 8]); eg = sb("eg", [128, 8]); lg = sb("lg", [128, 8]); ea = sb("ea", [128, 8])
            g = sb("g", [128, 8]); beta = sb("beta", [128, 8]); cum = sb("cum", [128, 8]); ecum = sb("ecum", [128, 8])
            dcl = sb("dcl", [128, 8]); ekl = sb("ekl", [128, 8]); becum = sb("becum", [128, 8])
            gsel = sb("gsel", [128, 16, 8]); ecl = sb("ecl", [128, 16, 8])
            NSLOT = 2
            Gh_s = [sb("Gh%d" % i, [128, 128]) for i in range(NSLOT)]
            n1_s = [sb("n1%d" % i, [128, 128]) for i in range(NSLOT)]
            n2_s = [sb("n2%d" % i, [128, 128]) for i in range(NSLOT)]
            expd_s = [sb("expd%d" % i, [128, 128]) for i in range(NSLOT)]
            expdT_s = [sb("expdT%d" % i, [128, 128]) for i in range(NSLOT)]
            ECB_s = [sb("ECB%d" % i, [128, 128]) for i in range(NSLOT)]
            decS_s = [sb("decS%d" % i, [128, 128]) for i in range(NSLOT)]
            decTI_s = [sb("decTI%d" % i, [128, 128]) for i in range(NSLOT)]
            qeT_s = [sb("qeT%d" % i, [128, 128]) for i in range(NSLOT)]
            Lm_s = [sb("Lm%d" % i, [128, 128]) for i in range(NSLOT)]
            AqkT_s = [sb("AqkT%d" % i, [128, 128]) for i in range(NSLOT)]
            rhsu_s = [sb("rhsu%d" % i, [128, 128]) for i in range(NSLOT)]
            rhsw_s = [sb("rhsw%d" % i, [128, 128]) for i in range(NSLOT)]
            kdec_s = [sb("kdec%d" % i, [128, 128]) for i in range(NSLOT)]
            negwT_s = [sb("negwT%d" % i, [128, 128]) for i in range(NSLOT)]
            vnew_s = [sb("vnew%d" % i, [128, 128]) for i in range(NSLOT)]
            Pa_s = [[sb("Pa%d_%d" % (j, i), [128, 128]) for i in range(2)] for j in range(NSLOT)]
            Pt_s = [[sb("Pt%d_%d" % (j, i), [128, 128]) for i in range(2)] for j in range(NSLOT)]
            Rb_s = [[sb("Rb%d_%d" % (j, i), [128, 128]) for i in range(2)] for j in range(NSLOT)]
            sqo_s = [sb("sqo%d" % i, [128, 128]) for i in range(NSLOT)]
            ss_s = [sb("ss%d" % i, [128, 8]) for i in range(NSLOT)]
            r1_s = [sb("r1%d" % i, [128, 8]) for i in range(NSLOT)]
            r2_s = [sb("r2%d" % i, [128, 8]) for i in range(NSLOT)]
            rstd_s = [sb("rstd%d" % i, [128, 8]) for i in range(NSLOT)]
            y = sb("y", [128, 1024]); yT = [sb(f"yT{i}", [128, 8, 128], BF16) for i in range(2)]
            S = sb("S", [128, 8, 128])
            Sall, Snew = sq, rr
            Sall_v = sq.t[:, :, :]; Snew_v = rr.t[:, :, :]
            kbig_v = ca.t[:, 0:16, :]
            wbig_v = ca.t[:, 16:24, :].rearrange("p a b -> p (a b)").rearrange("p (s i) -> p s i", s=16)
            qbig_v = cb.t[:, 0:8, :].rearrange("p a b -> p (a b)").rearrange("p (s i) -> p s i", s=16)
            wbig = kbig = ca
            qbig = cb
            k.op("dve", lambda e: e.memset(S.ap, 0.0), writes=[S])
            k.op("act", lambda e: e.activation(ea.ap, self.pk.t[:, PK_ALOG:PK_ALOG + 8], AF.Exp), reads=[self.pk], writes=[ea])
            cw = lambda w: self.pk.t[:, PK_CONV:PK_CONV + 96].rearrange("p (c w) -> p c w", w=4)[:, :, w:w + 1]
            pfm3 = PFM[0:3072, :].rearrange("(c p) t -> p c t", p=128)

            def load(tt):
                R = 128 if tt < 16 else 64
                r0 = tt * 128
                i = tt % 2
                if tt == 0:
                    k.op("pool", lambda e: e.memset(xc[i].t[:, :, 0:3], 0.0), writes=[xc[i]])
                    k.dma("sp", xc[i].t[:, :, 3:131], pfm3[:, :, 0:128], writes=[xc[i]])
                elif tt < 16:
                    k.dma("sp", xc[i].t[:, :, 0:131], pfm3[:, :, r0 - 3:r0 + 128], writes=[xc[i]])
                else:
                    k.dma("sp", xst.ap, pfm3[:, :, TP:TP + 64], writes=[xst])
                    k.dma("sp", cst_in.ap, self.dr["sconv"][l].rearrange("p (c s w) -> p c s w", c=24, s=16), writes=[cst_in])
                k.dma("sp", dab[i].t[0:R, :], PTM[r0:r0 + R, C_DAB:C_DAB + 16], writes=[dab[i]])
                k.dma("sp", dzb[i].t[0:R, :], PTM[r0:r0 + R, C_DZ:C_DZ + 1024], writes=[dzb[i]])

            import os
            tiles = [int(x) for x in os.environ.get("GDN_TILES", ",".join(map(str, range(NT)))).split(",")]
            lvl = int(os.environ.get("GDN_LEVEL", "9"))
            load(tiles[0])
            for ti, tt in enumerate(tiles):
                if ti + 1 < len(tiles):
                    load(tiles[ti + 1])
                R = 128 if tt < 16 else 64
                smp = tt == 16
                sfx = "_s" if smp else "_p"
                nseq = 16 if smp else 1
                i = tt % 2
                if not smp:
                    src = lambda w: xc[i].t[:, :, w:w + 128]
                    shp = [128, 24, 128]
                    va = lambda b: b.t[:, :, :]
                    rd = [xc[i]]
                else:
                    k.op("pool", lambda e: e.tensor_copy(xs.t[:, :, :, 0:3], cst_in.ap), reads=[cst_in], writes=[xs])
                    k.op("pool", lambda e: e.tensor_copy(xs.t[:, :, :, 3:7], xst.t[:, :, :].rearrange("p c (s t) -> p c s t", s=16)),
                         reads=[xst], writes=[xs])
                    src = lambda w: xs.t[:, :, :, w:w + 4]
                    shp = [128, 24, 16, 4]
                    va = lambda b: b.t[:, :, 0:64].rearrange("p c (s t) -> p c s t", s=16)
                    rd = [xs]
                    k.op("pool", lambda e: e.tensor_copy(cst_in.ap, xs.t[:, :, :, 4:7]), reads=[xs], writes=[cst_in])
                    k.dma("pool", self.dr["o_sconv"][l].rearrange("p (c s w) -> p c s w", c=24, s=16), cst_in.ap, reads=[cst_in])
                cwb = lambda w: (cw(w).to_broadcast(shp) if not smp else cw(w).unsqueeze(3).to_broadcast(shp))
                k.op("dve", lambda e: e.tensor_tensor(va(ca), src(0), cwb(0), ALU.mult), reads=rd + [self.pk], writes=[ca])
                k.op("pool", lambda e: e.tensor_tensor(va(cb), src(1), cwb(1), ALU.mult), reads=rd + [self.pk], writes=[cb])
                k.op("dve", lambda e: e.tensor_tensor(va(ca), va(ca), va(cb), ALU.add), reads=[cb, ca], writes=[ca])
                k.op("pool", lambda e: e.tensor_tensor(va(cb), src(2), cwb(2), ALU.mult), reads=rd + [self.pk], writes=[cb])
                k.op("dve", lambda e: e.tensor_tensor(va(ca), va(ca), va(cb), ALU.add), reads=[cb, ca], writes=[ca])
                k.op("pool", lambda e: e.tensor_tensor(va(cb), src(3), cwb(3), ALU.mult), reads=rd + [self.pk], writes=[cb])
                k.op("dve", lambda e: e.tensor_tensor(va(ca), va(ca), va(cb), ALU.add), reads=[cb, ca], writes=[ca])
                k.op("act", lambda e: e.activation(cc.t[:, :, 0:R], ca.t[:, :, 0:R], AF.Silu), reads=[ca], writes=[cc])
                if lvl < 2:
                    continue
                k.op("act", lambda e: e.activation(sq.t[:, :, 0:R], cc.t[:, 0:16, 0:R], AF.Square), reads=[cc], writes=[sq])
                for g0 in range(0, 16, 4):
                    ps = k.psum()
                    for c in range(4):
                        k.op("pe", lambda e: e.matmul(ps.t[:, c * 128:c * 128 + R], ones, sq.t[:, g0 + c, 0:R], start=True, stop=True),
                             reads=[sq, self.cst], writes=[ps])
                    pv = ps.t[:, :].rearrange("p (a b) -> p a b", a=4)[:, :, 0:R]
                    k.op("dve", lambda e: e.tensor_scalar(rr.t[:, g0:g0 + 4, 0:R], pv, EPS, None, ALU.add), reads=[ps], writes=[rr])
                k.op("act", lambda e: e.activation(rr2.t[:, :, 0:R], rr.t[:, :, 0:R], AF.Sqrt), reads=[rr], writes=[rr2])
                k.op("dve", lambda e: e.reciprocal(rr.t[:, :, 0:R], rr2.t[:, :, 0:R]), reads=[rr2], writes=[rr])
                k.op("dve", lambda e: e.scalar_tensor_tensor(qkn.t[:, 0:8, 0:R], cc.t[:, 0:8, 0:R], 128 ** -0.5, rr.t[:, 0:8, 0:R], ALU.mult, ALU.mult),
                     reads=[cc, rr], writes=[qkn])
                k.op("pool", lambda e: e.tensor_tensor(qkn.t[:, 8:16, 0:R], cc.t[:, 8:16, 0:R], rr.t[:, 8:16, 0:R], ALU.mult),
                     reads=[cc, rr], writes=[qkn])
                k.op("act", lambda e: e.activation(qkb.t[:, :, 0:R], qkn.t[:, :, 0:R], AF.Copy), reads=[qkn], writes=[qkb])
                for h0 in range(0, 8, 4):
                    for (srcb, c0, dst) in ((qkn, 8, ktm), (cc, 16, vtm)):
                        ps = k.psum()
                        for c in range(4):
                            k.op("pe", lambda e: e.transpose(ps.t[0:R, c * 128:(c + 1) * 128], srcb.t[:, c0 + h0 + c, 0:R], ident),
                                 reads=[srcb, self.cst], writes=[ps])
                        k.op("act", lambda e: e.activation(dst.t[0:R, h0:h0 + 4, :], ps.t[0:R, :].rearrange("p (a b) -> p a b", a=4), AF.Copy),
                             reads=[ps], writes=[dst])
                if lvl < 3:
                    continue
                da = dab[i]
                k.op("dve", lambda e: e.tensor_tensor(tg.t[0:R, :], da.t[0:R, 0:8], self.pk.t[0:R, PK_DTB:PK_DTB + 8], ALU.add),
                     reads=[da, self.pk], writes=[tg])
                k.op("act", lambda e: e.activation(eg.t[0:R, :], tg.t[0:R, :], AF.Exp), reads=[tg], writes=[eg])
                k.op("dve", lambda e: e.tensor_scalar(eg.t[0:R, :], eg.t[0:R, :], 1.0, None, ALU.add), reads=[eg], writes=[eg])
                k.op("act", lambda e: e.activation(lg.t[0:R, :], eg.t[0:R, :], AF.Ln), reads=[eg], writes=[lg])
                k.op("dve", lambda e: e.scalar_tensor_tensor(g.t[0:R, :], lg.t[0:R, :], -1.0, ea.t[0:R, :], ALU.mult, ALU.mult),
                     reads=[lg, ea], writes=[g])
                k.op("act", lambda e: e.activation(beta.t[0:R, :], da.t[0:R, 8:16], AF.Sigmoid), reads=[da], writes=[beta])
                ps = k.psum()
                k.op("pe", lambda e: e.matmul(ps.t[0:R, 0:8], self.C("MTi" + sfx, R)[:, 0:R], g.t[0:R, :], start=True, stop=True),
                     reads=[g, self.cst], writes=[ps])
                k.op("pe", lambda e: e.matmul(ps.t[0:R, 8:16], self.C("ALL" + sfx, R)[:, 0:R], g.t[0:R, :], start=True, stop=True),
                     reads=[g, self.cst], writes=[ps])
                k.op("dve", lambda e: e.tensor_copy(cum.t[0:R, :], ps.t[0:R, 0:8]), reads=[ps], writes=[cum])
                k.op("dve", lambda e: e.tensor_tensor(dcl.t[0:R, :], ps.t[0:R, 8:16], cum.t[0:R, :], ALU.subtract), reads=[ps, cum], writes=[dcl])
                k.op("act", lambda e: e.activation(ecum.t[0:R, :], cum.t[0:R, :], AF.Exp), reads=[cum], writes=[ecum])
                k.op("act", lambda e: e.activation(ekl.t[0:R, :], dcl.t[0:R, :], AF.Exp), reads=[dcl], writes=[ekl])
                k.op("dve", lambda e: e.tensor_tensor(becum.t[0:R, :], beta.t[0:R, :], ecum.t[0:R, :], ALU.mult), reads=[beta, ecum], writes=[becum])
                k.op("dve", lambda e: e.tensor_tensor(gsel.t[0:R, :, :], g.t[0:R, :].unsqueeze(1).to_broadcast([R, 16, 8]),
                                                      self.C("SEL" + sfx, R).unsqueeze(2).to_broadcast([R, 16, 8]), ALU.mult),
                     reads=[g, self.cst], writes=[gsel])
                ps = k.psum()
                k.op("pe", lambda e: e.matmul(ps.t[:, 0:128], ones[0:R, :], gsel.t[0:R, :, :].rearrange("p s h -> p (s h)"), start=True, stop=True),
                     reads=[gsel, self.cst], writes=[ps])
                k.op("act", lambda e: e.activation(ecl.t[:, :, :].rearrange("p s h -> p (s h)"), ps.t[:, 0:128], AF.Exp), reads=[ps], writes=[ecl])
                k.op("act", lambda e: e.activation(sgt.t[0:R, :], dzb[i].t[0:R, :], AF.Silu), reads=[dzb[i]], writes=[sgt])
                gn = self.pk.t[0:R, PK_GDNN:PK_GDNN + 128].unsqueeze(1).to_broadcast([R, 8, 128])
                s3 = sgt.t[0:R, :].rearrange("p (g d) -> p g d", g=8)
                k.op("pool", lambda e: e.tensor_tensor(s3, s3, gn, ALU.mult), reads=[sgt, self.pk], writes=[sgt])
                if lvl < 4:
                    continue
                def head(h, slot):
                    Gh, n1, n2, expd, expdT, ECB, decS, decTI, qeT, Lm, AqkT, rhsu, rhsw, kdec, negwT, vnew, sqo, ss, r1, r2, rstd = Gh_s[slot], n1_s[slot], n2_s[slot], expd_s[slot], expdT_s[slot], ECB_s[slot], decS_s[slot], decTI_s[slot], qeT_s[slot], Lm_s[slot], AqkT_s[slot], rhsu_s[slot], rhsw_s[slot], kdec_s[slot], negwT_s[slot], vnew_s[slot], sqo_s[slot], ss_s[slot], r1_s[slot], r2_s[slot], rstd_s[slot]
                    Pa, Pt, Rb = Pa_s[slot], Pt_s[slot], Rb_s[slot]
                    kT = qkb.t[:, 8 + h, 0:R]
                    qT = qkb.t[:, h, 0:R]
                    yield
                    k.op("dve", lambda e: e.tensor_scalar(Gh.t[0:R, :], ones[0:R, :], g.t[0:R, h:h + 1], None, ALU.mult),
                         reads=[g, self.cst], writes=[Gh])
                    psc = k.psum()
                    yield
                    k.op("pe", lambda e: e.matmul(psc.t[:, 0:R], Gh.t[0:R, :], self.C("MTi" + sfx, R)[:, 0:R], start=True, stop=True),
                         reads=[Gh, self.cst], writes=[psc])
                    cumc = cum.t[0:R, h:h + 1]
                    yield
                    k.op("dve", lambda e: e.tensor_scalar(n1.t[0:R, 0:R], psc.t[0:R, 0:R], cumc, 0.0, ALU.subtract, ALU.max),
                         reads=[psc, cum], writes=[n1])
                    yield
                    k.op("act", lambda e: e.activation(expd.t[0:R, 0:R], n1.t[0:R, 0:R], AF.Exp, scale=-1.0), reads=[n1], writes=[expd])
                    yield
                    k.op("dve", lambda e: e.tensor_scalar(n2.t[0:R, 0:R], psc.t[0:R, 0:R], cumc, 0.0, ALU.subtract, ALU.min),
                         reads=[psc, cum], writes=[n2])
                    yield
                    k.op("act", lambda e: e.activation(expdT.t[0:R, 0:R], n2.t[0:R, 0:R], AF.Exp), reads=[n2], writes=[expdT])
                    yield
                    k.op("act", lambda e: e.activation(ECB.t[:, 0:R], psc.t[:, 0:R], AF.Exp), reads=[psc], writes=[ECB])
                    yield
                    k.op("pool", lambda e: e.tensor_tensor(decS.t[0:R, 0:R], expd.t[0:R, 0:R], self.C("Ms" + sfx, R)[:, 0:R], ALU.mult),
                         reads=[expd, self.cst], writes=[decS])
                    yield
                    k.op("pool", lambda e: e.tensor_tensor(decTI.t[0:R, 0:R], expdT.t[0:R, 0:R], self.C("MTi" + sfx, R)[:, 0:R], ALU.mult),
                         reads=[expdT, self.cst], writes=[decTI])
                    yield
                    k.op("pool", lambda e: e.tensor_tensor(qeT.t[:, 0:R], qkn.t[:, h, 0:R], ECB.t[:, 0:R], ALU.mult), reads=[qkn, ECB], writes=[qeT])
                    if lvl < 5:
                        return
                    skip = os.environ.get("GDN_SKIP", "").split(",")
                    pk_ = k.psum()
                    if "kk" not in skip:
                        yield
                        k.op("pe", lambda e: e.matmul(pk_.t[0:R, 0:R], kT, kT, start=True, stop=True), reads=[qkb], writes=[pk_])
                    if "stt" not in skip:
                        yield
                        k.op("dve", lambda e: e.scalar_tensor_tensor(Lm.t[0:R, 0:R], pk_.t[0:R, 0:R], beta.t[0:R, h:h + 1], decS.t[0:R, 0:R],
                                                                     ALU.mult, ALU.mult), reads=[pk_, beta, decS], writes=[Lm])
                    pA = k.psum()
                    if "tr" not in skip:
                        yield
                        k.op("pe", lambda e: e.transpose(pA.t[0:R, 0:R], Lm.t[0:R, 0:R], ident[0:R, 0:R]), reads=[Lm, self.cst], writes=[pA])
                    P, PT, Rc = Pa[0], Lm, Rb[0]
                    if "pc" not in skip:
                        yield
                        k.op("act", lambda e: e.activation(P.t[0:R, 0:R], pA.t[0:R, 0:R], AF.Copy), reads=[pA], writes=[P])
                    if "rc" not in skip:
                        yield
                        k.op("dve", lambda e: e.tensor_tensor(Rc.t[0:R, 0:R], ident[0:R, 0:R], pA.t[0:R, 0:R], ALU.subtract),
                             reads=[pA, self.cst], writes=[Rc])
                    nlev = 1 if smp else 6
                    nlev = int(os.environ.get('GDN_NLEV', nlev))
                    for lv in range(nlev):
                        last = lv == nlev - 1
                        P2, PT2, R2 = Pa[(lv + 1) % 2], Pt[lv % 2], Rb[(lv + 1) % 2]
                        pb = k.psum()
                        yield
                        k.op("pe", lambda e: e.matmul(pb.t[0:R, 0:R], P.t[0:R, 0:R], PT.t[0:R, 0:R], start=True, stop=True),
                             reads=[P, PT], writes=[pb])
                        if not last:
                            pa_ = k.psum()
                            yield
                            k.op("pe", lambda e: e.matmul(pa_.t[0:R, 0:R], PT.t[0:R, 0:R], P.t[0:R, 0:R], start=True, stop=True),
                                 reads=[P, PT], writes=[pa_])
                        yield
                        k.op("dve", lambda e: e.tensor_copy(PT2.t[0:R, 0:R], pb.t[0:R, 0:R]), reads=[pb], writes=[PT2])
                        if not last:
                            yield
                            k.op("act", lambda e: e.activation(P2.t[0:R, 0:R], pa_.t[0:R, 0:R], AF.Copy), reads=[pa_], writes=[P2])
                        pc = k.psum()
                        yield
                        k.op("pe", lambda e: e.matmul(pc.t[0:R, 0:R], PT2.t[0:R, 0:R], Rc.t[0:R, 0:R], start=True, stop=True),
                             reads=[PT2, Rc], writes=[pc])
                        yield
                        k.op("dve", lambda e: e.tensor_tensor(R2.t[0:R, 0:R], Rc.t[0:R, 0:R], pc.t[0:R, 0:R], ALU.add), reads=[pc, Rc], writes=[R2])
                        P, PT, Rc = P2, PT2, R2
                    Rf = Rc
                    if lvl < 6:
                        return
                    pq = k.psum()
                    yield
                    k.op("pe", lambda e: e.matmul(pq.t[0:R, 0:R], kT, qT, start=True, stop=True), reads=[qkb], writes=[pq])
                    yield
                    k.op("dve", lambda e: e.tensor_tensor(AqkT.t[0:R, 0:R], pq.t[0:R, 0:R], decTI.t[0:R, 0:R], ALU.mult), reads=[pq, decTI], writes=[AqkT])
                    yield
                    k.op("dve", lambda e: e.tensor_scalar(rhsu.t[0:R, :], vtm.t[0:R, h, :], beta.t[0:R, h:h + 1], None, ALU.mult),
                         reads=[vtm, beta], writes=[rhsu])
                    yield
                    k.op("dve", lambda e: e.tensor_scalar(rhsw.t[0:R, :], ktm.t[0:R, h, :], becum.t[0:R, h:h + 1], None, ALU.mult),
                         reads=[ktm, becum], writes=[rhsw])
                    yield
                    k.op("dve", lambda e: e.tensor_scalar(kdec.t[0:R, :], ktm.t[0:R, h, :], ekl.t[0:R, h:h + 1], None, ALU.mult),
                         reads=[ktm, ekl], writes=[kdec])
                    pw = k.psum()
                    yield
                    k.op("pe", lambda e: e.matmul(pw.t[:, 0:R], rhsw.t[0:R, :], Rf.t[0:R, 0:R], start=True, stop=True), reads=[rhsw, Rf], writes=[pw])
                    yield
                    k.op("act", lambda e: e.activation(negwT.t[:, 0:R], pw.t[:, 0:R], AF.Copy, scale=-1.0), reads=[pw], writes=[negwT])
                    if smp:
                        yield
                        k.dma("sp", Sall_v, self.dr["sgdn"][l, :, h].rearrange("s d e -> d s e"), writes=[Sall])
                        bmv = self.C("BM").rearrange("p (s i) -> p s i", s=16)
                        yield
                        k.op("dve", lambda e: e.tensor_tensor(wbig_v, negwT.t[:, 0:64].unsqueeze(1).to_broadcast([128, 16, 64]), bmv, ALU.mult),
                             reads=[negwT, self.cst], writes=[wbig])
                        yield
                        k.op("pool", lambda e: e.tensor_tensor(qbig_v, qeT.t[:, 0:64].unsqueeze(1).to_broadcast([128, 16, 64]), bmv, ALU.mult),
                             reads=[qeT, self.cst], writes=[qbig])
                        yield
                        k.op("dve", lambda e: e.tensor_tensor(kbig_v[0:64, :, :], kdec.t[0:64, :].unsqueeze(1).to_broadcast([64, 16, 128]),
                                                              self.C("SEL_s", 64).unsqueeze(2).to_broadcast([64, 16, 128]), ALU.mult),
                             reads=[kdec, self.cst], writes=[kbig])
                    pv_ = k.psum()
                    yield
                    k.op("pe", lambda e: e.matmul(pv_.t[0:R, 0:128], Rf.t[0:R, 0:R], rhsu.t[0:R, :], start=True, stop=False), reads=[Rf, rhsu], writes=[pv_])
                    if not smp:
                        yield
                        k.op("pe", lambda e: e.matmul(pv_.t[0:R, 0:128], negwT.t[:, 0:R], S.t[:, h, :], start=False, stop=True),
                             reads=[negwT, S], writes=[pv_])
                    else:
                        for s in range(16):
                            yield
                            k.op("pe", lambda e: e.matmul(pv_.t[0:R, 0:128], wbig_v[:, s, :], Sall_v[:, s, :], start=False, stop=(s == 15)),
                                 reads=[wbig, Sall], writes=[pv_])
                    yield
                    k.op("act", lambda e: e.activation(vnew.t[0:R, :], pv_.t[0:R, 0:128], AF.Copy), reads=[pv_], writes=[vnew])
                    if lvl < 7:
                        return
                    po = k.psum()
                    yield
                    k.op("pe", lambda e: e.matmul(po.t[0:R, 0:128], AqkT.t[0:R, 0:R], vnew.t[0:R, :], start=True, stop=False), reads=[AqkT, vnew], writes=[po])
                    if not smp:
                        yield
                        k.op("pe", lambda e: e.matmul(po.t[0:R, 0:128], qeT.t[:, 0:R], S.t[:, h, :], start=False, stop=True), reads=[qeT, S], writes=[po])
                    else:
                        for s in range(16):
                            yield
                            k.op("pe", lambda e: e.matmul(po.t[0:R, 0:128], qbig_v[:, s, :], Sall_v[:, s, :], start=False, stop=(s == 15)),
                                 reads=[qbig, Sall], writes=[po])
                    yield
                    k.op("act", lambda e: e.activation(sqo.t[0:R, :], po.t[0:R, 0:128], AF.Square), reads=[po], writes=[sqo])
                    yield
                    k.op("dve", lambda e: e.tensor_reduce(ss.t[0:R, h:h + 1], sqo.t[0:R, :], X, ALU.add), reads=[sqo], writes=[ss])
                    yield
                    k.op("dve", lambda e: e.tensor_scalar(r1.t[0:R, h:h + 1], ss.t[0:R, h:h + 1], 1.0 / 128, EPS, ALU.mult, ALU.add), reads=[ss], writes=[r1])
                    yield
                    k.op("act", lambda e: e.activation(r2.t[0:R, h:h + 1], r1.t[0:R, h:h + 1], AF.Sqrt), reads=[r1], writes=[r2])
                    yield
                    k.op("dve", lambda e: e.reciprocal(rstd.t[0:R, h:h + 1], r2.t[0:R, h:h + 1]), reads=[r2], writes=[rstd])
                    yield
                    k.op("dve", lambda e: e.scalar_tensor_tensor(y.t[0:R, h * 128:(h + 1) * 128], po.t[0:R, 0:128], rstd.t[0:R, h:h + 1],
                                                                 sgt.t[0:R, h * 128:(h + 1) * 128], ALU.mult, ALU.mult),
                         reads=[po, rstd, sgt], writes=[y])
                    if not smp:
                        pS = k.psum()
                        yield
                        k.op("pe", lambda e: e.matmul(pS.t[:, 0:128], kdec.t[0:R, :], vnew.t[0:R, :], start=True, stop=True), reads=[kdec, vnew], writes=[pS])
                        yield
                        k.op("dve", lambda e: e.scalar_tensor_tensor(S.t[:, h, :], S.t[:, h, :], ecl.t[:, 0, h:h + 1], pS.t[:, 0:128], ALU.mult, ALU.add),
                             reads=[pS, S, ecl], writes=[S])
                        if tt == 15:
                            yield
                            k.dma("pool", self.dr["o_pgdn"][l, h], S.t[:, h, :], reads=[S])
                    else:
                        for s in range(16):
                            pS = k.psum()
                            yield
                            k.op("pe", lambda e: e.matmul(pS.t[:, 0:128], kbig_v[0:64, s, :], vnew.t[0:64, :], start=True, stop=True),
                                 reads=[kbig, vnew], writes=[pS])
                            yield
                            k.op("dve", lambda e: e.scalar_tensor_tensor(Snew_v[:, s, :], Sall_v[:, s, :], ecl.t[:, s, h:h + 1], pS.t[:, 0:128],
                                                                         ALU.mult, ALU.add), reads=[pS, Sall, ecl], writes=[Snew])
                        yield
                        k.dma("pool", self.dr["o_sgdn"][l, :, h].rearrange("s d e -> d s e"), Snew_v, reads=[Snew])

                    yield
                nh = int(os.environ.get('GDN_HEADS', '8'))
                if smp or nh < 2 or os.environ.get("GDN_SEQ"):
                    for h in range(nh):
                        for _ in head(h, 0):
                            pass
                else:
                    for h0 in range(0, nh, NSLOT):
                        gens = [head(h0 + j, j) for j in range(NSLOT)]
                        alive = list(gens)
                        while alive:
                            for gq in list(alive):
                                try:
                                    next(gq)
                                except StopIteration:
                                    alive.remove(gq)
                yt = yT[tt % 2]
                self.transposes(lambda c: y.t[0:R, c * 128:(c + 1) * 128], 8, yt, R, [y])
                k.dma("pool", self.dr["YBT"][1024:2048, tt * 128:tt * 128 + R].rearrange("(c p) t -> p c t", p=128),
                      yt.t[:, :, 0:R], reads=[yt])
            if lvl >= 9:
                k.dma("sp", self.dr["o_pconv"][l], PFM[0:3072, TP - 3:TP])
            k.barrier()


_PROG = {}


def _get_prog():
    if "nc" not in _PROG:
        nc = bass.Bass("TRN2", target_bir_lowering=False)
        m = Model(nc)
        m.build()
        _PROG["nc"] = nc
        _PROG["ninst"] = m.k.ninst
    return _PROG["nc"]


def _cossin():
    inv = 1.0 / (10000.0 ** np.linspace(0.0, 1.0, 64, dtype=np.float32)).astype(np.float32)
    pos = np.concatenate([np.arange(TP, dtype=np.float32),
                          np.tile(np.arange(4, dtype=np.float32) + np.float32(16384.0), 16)])
    ang = (pos[:, None] * inv[None, :]).astype(np.float32)
    return np.concatenate([np.cos(ang), np.sin(ang)], axis=1).astype(np.float32)


def kernel(x_prompt, x_sample, state_ret, state_gdn, state_gdn_conv, state_gla, norm_mix, norm_ffn,
           norm_final, w_in, b_merge, gdn_conv_w, gdn_a_log, gdn_dt_bias, gdn_norm, gla_w_up, gla_b_up,
           gla_norm, w_branch, w_o, w_gate_up, w_down):
    f = lambda a: np.ascontiguousarray(np.asarray(a, dtype=np.float32))
    x_prompt, x_sample = f(x_prompt), f(x_sample)
    w_in = np.asarray(w_in, dtype=np.float32)
    wtm = np.zeros((DEPTH, D, TMW), np.float32)
    wtm[:, :, :7200] = w_in[:, :, TM_IDX]
    wfm = np.ascontiguousarray(w_in[:, :, FM_IDX])
    wgu = np.asarray(w_gate_up, dtype=np.float32).reshape(DEPTH, D, 2, DFF // 128, 128)
    wgu = np.ascontiguousarray(wgu.transpose(0, 1, 3, 2, 4)).reshape(DEPTH, D, 2 * DFF)
    pack = np.zeros((DEPTH, 128, NPK), np.float32)
    col = lambda v, n: np.asarray(v, np.float32).reshape(n, 128).T
    for l in range(DEPTH):
        pack[l, :, PK_GMIX:PK_GMIX + 16] = col(norm_mix[l], 16)
        pack[l, :, PK_GFFN:PK_GFFN + 16] = col(norm_ffn[l], 16)
        cwl = np.asarray(gdn_conv_w[l], np.float32)
        pack[l, :, PK_CONV:PK_CONV + 96] = cwl.reshape(4, 24, 128).transpose(2, 1, 0).reshape(128, 96)
        pack[l, :, PK_BM:PK_BM + 48] = col(b_merge[l], 48)
        pack[l, :, PK_ALOG:PK_ALOG + 8] = np.asarray(gdn_a_log[l], np.float32)[None, :]
        pack[l, :, PK_DTB:PK_DTB + 8] = np.asarray(gdn_dt_bias[l], np.float32)[None, :]
        pack[l, :, PK_GDNN:PK_GDNN + 128] = np.asarray(gdn_norm[l], np.float32)[None, :]
        pack[l, :, PK_GLAN:PK_GLAN + 256] = np.asarray(gla_norm[l], np.float32)[None, :]
        pack[l, :, PK_BUP:PK_BUP + 512] = np.asarray(gla_b_up[l], np.float32)[None, :]
        pack[l, 0:16, PK_WUP:PK_WUP + 512] = np.asarray(gla_w_up[l], np.float32)
        pack[l, :, PK_GFIN:PK_GFIN + 16] = col(norm_final, 16)
    cossin = _cossin()
    shared = {"wtm": wtm, "wfm": wfm, "wbr": f(w_branch), "wo": f(w_o), "wgu": wgu, "wdn": f(w_down),
              "pack": pack, "cst": CST_ARR, "cossin": cossin}
    state_ret, state_gdn, state_gla = f(state_ret), f(state_gdn), f(state_gla)
    sconv = f(state_gdn_conv)
    in_maps = []
    for c in range(NCORES):
        xs = x_sample[16 * c:16 * (c + 1)].reshape(64, D)
        xT = np.ascontiguousarray(np.concatenate([x_prompt[c % 4], xs], axis=0).T)
        sc = sconv[:, 16 * c:16 * (c + 1)]
        sc = sc.reshape(DEPTH, 16, 3, 24, 128).transpose(0, 4, 3, 1, 2)
        m = dict(shared)
        m.update({"xT": xT,
                  "sret": np.ascontiguousarray(state_ret[:, 16 * c:16 * (c + 1)]),
                  "sgdn": np.ascontiguousarray(state_gdn[:, 16 * c:16 * (c + 1)]),
                  "sgla": np.ascontiguousarray(state_gla[:, 16 * c:16 * (c + 1)]),
                  "sconv": np.ascontiguousarray(sc).reshape(DEPTH, 128, 24 * 16 * 3)})
        in_maps.append(m)
    nc = _get_prog()
    res = run_bass_kernel_spmd(nc, in_maps, core_ids=list(range(NCORES))).results
    y_prompt = np.stack([res[c]["yT"][:, :TP].T for c in range(4)])
    y_sample = np.concatenate([res[c]["yT"][:, TP:].T.reshape(16, 4, D) for c in range(NCORES)])
    p_ret = np.stack([res[c]["o_pret"] for c in range(4)], axis=1)
    p_gdn = np.stack([res[c]["o_pgdn"] for c in range(4)], axis=1)
    p_conv = np.stack([res[c]["o_pconv"].transpose(0, 2, 1) for c in range(4)], axis=1)
    p_gla = np.stack([res[c]["o_pgla"] for c in range(4)], axis=1)
    s_ret = np.concatenate([res[c]["o_sret"] for c in range(NCORES)], axis=1)
    s_gdn = np.concatenate([res[c]["o_sgdn"] for c in range(NCORES)], axis=1)
    s_gla = np.concatenate([res[c]["o_sgla"] for c in range(NCORES)], axis=1)
    s_conv = np.concatenate([res[c]["o_sconv"].reshape(DEPTH, 128, 24, 16, 3).transpose(0, 3, 4, 2, 1).reshape(DEPTH, 16, 3, 3072)
                             for c in range(NCORES)], axis=1)
    outs = (y_prompt, y_sample, p_ret, p_gdn, p_conv, p_gla, s_ret, s_gdn, s_conv, s_gla)
    return tuple(np.ascontiguousarray(o, dtype=np.float32) for o in outs)
```

```python
import contextlib
import numpy as np
import concourse.bass as bass
import concourse.mybir as mybir
from concourse.bass_utils import run_bass_kernel_spmd

F32 = mybir.dt.float32
BF16 = mybir.dt.bfloat16
AF = mybir.ActivationFunctionType
ALU = mybir.AluOpType

SEM_LIMIT = 28000


class Buf:
    __slots__ = ("name", "t", "ap", "w", "r", "psum")

    def __init__(self, name, t=None):
        self.name = name
        self.psum = False
        self.t = t
        self.ap = t[:] if t is not None else None
        self.w = None
        self.r = {}


class Ctx:
    ENG = ("pe", "act", "dve", "pool", "sp")

    def __init__(self, nc):
        self.nc = nc
        self.es = contextlib.ExitStack()
        self.engs = {"pe": nc.tensor, "act": nc.scalar, "dve": nc.vector, "pool": nc.gpsimd, "sp": nc.sync}
        self.cur = {}
        self.waited = {e: {} for e in self.ENG}
        self.semid = 0
        self.dma_pool = []
        self.dma_rr = 0
        self.all_sems = []
        self.drams = {}
        self.psums = []
        self.ps_rr = 0
        self.ninst = 0

    def __enter__(self):
        self.es.__enter__()
        for e in self.ENG:
            self.cur[e] = self._newsem(e)
        for i in range(24):
            self.dma_pool.append(self._newsem())
        for i in range(8):
            t = self.es.enter_context(self.nc.psum_tensor(f"psb{i}", [128, 512], F32))
            self.psums.append(Buf(f"psb{i}", t))
            self.psums[-1].psum = True
        return self

    def __exit__(self, *a):
        return self.es.__exit__(*a)

    def _newsem(self, owner=None):
        s = self.es.enter_context(self.nc.semaphore(f"s{self.semid}"))
        self.semid += 1
        rec = [s, 0, self.semid, owner]
        self.all_sems.append(rec)
        return rec

    def sb(self, name, shape, dt, stack=None):
        self.nsb = getattr(self, "nsb", 0) + 1
        name = f"{name}_u{self.nsb}"
        stk = stack or self.es
        t = stk.enter_context(self.nc.sbuf_tensor(name, shape, dt))
        nb = int(np.prod(shape[1:])) * (2 if dt == BF16 else 4)
        self.live = getattr(self, "live", 0) + nb
        self.peak = max(getattr(self, "peak", 0), self.live)

        def _free(n=nb):
            self.live -= n
        stk.callback(_free)
        return Buf(name, t)

    def dram(self, name):
        if name not in self.drams:
            self.drams[name] = Buf(name)
        return self.drams[name]

    def psum(self):
        b = self.psums[self.ps_rr % 8]
        self.ps_rr += 1
        return b

    def _wait(self, e, tok):
        rec, val = tok
        key = rec[2]
        if e == "pe" and rec[3] == "pe":
            return
        if self.waited[e].get(key, 0) >= val:
            return
        self.waited[e][key] = val
        self.engs[e].wait_ge(rec[0], val)

    def _deps(self, e, reads, writes):
        toks = []
        for b in reads:
            if b.w is not None:
                toks.append(b.w)
        for b in writes:
            if b.w is not None:
                toks.append(b.w)
            toks.extend(b.r.values())
        for t in toks:
            self._wait(e, t)

    def _record(self, tok, reads, writes):
        for b in reads:
            key = tok[0][2]
            old = b.r.get(key)
            if old is None or old[1] < tok[1]:
                b.r[key] = tok
        for b in writes:
            b.w = tok
            b.r = {}

    def op(self, e, fn, reads=(), writes=()):
        pr = [b for b in reads if b.psum]
        if pr:
            reads = [b for b in reads if not b.psum]
            writes = list(writes) + pr
        self._deps(e, reads, writes)
        rec = self.cur[e]
        if rec[1] >= SEM_LIMIT:
            rec = self.cur[e] = self._newsem(e)
        inst = fn(self.engs[e])
        rec[1] += 1
        inst.then_inc(rec[0], 1)
        tok = (rec, rec[1])
        self._record(tok, reads, writes)
        self.ninst += 1
        return tok

    def dma(self, q, out, in_, reads=(), writes=(), **kw):
        self._deps(q, reads, writes)
        i = self.dma_rr % len(self.dma_pool)
        self.dma_rr += 1
        rec = self.dma_pool[i]
        if rec[1] >= SEM_LIMIT:
            rec = self.dma_pool[i] = self._newsem()
        if rec[1] > 0:
            self._wait(q, (rec, rec[1]))
        inst = self.engs[q].dma_start(out=out, in_=in_, **kw)
        rec[1] += 16
        inst.then_inc(rec[0], 16)
        tok = (rec, rec[1])
        self._record(tok, reads, writes)
        self.ninst += 1
        return tok

    def barrier(self):
        for e in self.ENG:
            for rec in self.all_sems:
                if rec[1] > 0:
                    self._wait(e, (rec, rec[1]))

    def finish(self):
        for rec in self.all_sems:
            if rec[1] > 0:
                self._wait("sp", (rec, rec[1]))
                self._wait("act", (rec, rec[1]))


D = 2048
TP = 2048
TS = 64
T = TP + TS
NT = 17
DEPTH = 4
DFF = 5632
EPS = 1e-6
TMW = 7424
FMW = 9216
NPK = 1616
NCORES = 8

_o = [0, 512, 1024, 2048, 3072, 6144, 6152, 6160, 7184, 7696, 8208, 9232, 9248, 10272, 16416]
TM_IDX = np.concatenate([np.arange(0, 3072), np.arange(6160, 7184), np.arange(7184, 9232),
                         np.arange(9248, 10272), np.arange(6144, 6160), np.arange(9232, 9248)])
FM_IDX = np.concatenate([np.arange(3072, 6144), np.arange(10272, 16416)])
C_RQK, C_RV, C_RG, C_DZ, C_LQK, C_LV, C_LGT, C_DAB, C_LLR = 0, 1024, 2048, 3072, 4096, 5120, 6144, 7168, 7184
PK_GMIX, PK_GFFN, PK_CONV, PK_BM, PK_ALOG, PK_DTB, PK_GDNN, PK_GLAN, PK_BUP, PK_WUP, PK_GFIN = \
    0, 16, 32, 128, 176, 184, 192, 320, 576, 1088, 1600


def _build_consts():
    items = {}
    i = np.arange(128)
    ident = np.eye(128, dtype=np.float64)
    items["ident"] = ident
    items["ones"] = np.ones((128, 128))
    mti = (i[:, None] <= i[None, :]).astype(np.float64)
    mts = (i[:, None] < i[None, :]).astype(np.float64)
    items["MTi_p"] = mti
    items["MTs_p"] = mts
    items["Mi_p"] = mti.T.copy()
    items["Ms_p"] = mts.T.copy()
    items["ALL_p"] = np.ones((128, 128))
    same = np.zeros((128, 128))
    same[:64, :64] = (i[:64, None] // 4 == i[None, :64] // 4)
    items["MTi_s"] = mti * same
    items["MTs_s"] = mts * same
    items["Mi_s"] = mti.T * same
    items["Ms_s"] = mts.T * same
    items["ALL_s"] = same
    ssp = np.zeros((128, 16)); ssp[:, 0] = 1.0
    items["SEL_p"] = ssp
    sss = np.zeros((128, 16)); sss[np.arange(64), np.arange(64) // 4] = 1.0
    items["SEL_s"] = sss
    bm = np.zeros((128, 16, 64))
    bm[:, np.arange(64) // 4, np.arange(64)] = 1.0
    items["BM"] = bm.reshape(128, 1024)
    gam = 1.0 - 2.0 ** (-5.0 - np.arange(4, dtype=np.float64))
    rdp = np.zeros((128, 4, 128)); rds = np.zeros((128, 4, 128))
    dqkp = np.zeros((128, 8)); dqks = np.zeros((128, 8))
    for h in range(4):
        rdp[:, h, :] = mti * gam[h] ** (-128.0)
        rds[:, h, :] = mti * same * gam[h] ** (-4.0)
        dqkp[:, h] = gam[h] ** (i + 1.0)
        dqkp[:, 4 + h] = gam[h] ** (127.0 - i) * 128 ** -0.5
        t = (i % 4).astype(np.float64)
        dqks[:, h] = gam[h] ** (t + 1.0)
        dqks[:, 4 + h] = gam[h] ** (3.0 - t) * 128 ** -0.5
    items["RD_p"] = rdp.reshape(128, 512)
    items["RD_s"] = rds.reshape(128, 512)
    items["DQK_p"] = dqkp
    items["DQK_s"] = dqks
    off = {}
    cols = []
    o = 0
    for name, a in items.items():
        off[name] = (o, a.shape[1])
        o += a.shape[1]
        cols.append(a)
    return off, o, np.concatenate(cols, axis=1).astype(np.float32), gam


CST_OFF, NCST, CST_ARR, GAMMA = _build_consts()


def blocks(total, step, t0=0):
    out = []
    t = 0
    while t < total:
        n = min(step, total - t)
        out.append((t0 + t, n))
        t += n
    return out


TOKB = blocks(TP, 512) + [(TP, TS)]


class Model:
    def __init__(self, nc, debug=False, phases=None, nl=DEPTH):
        self.nc = nc
        self.k = Ctx(nc)
        self.debug = debug
        self.phases = phases
        self.nl = nl
        d = self.dr = {}

        def di(name, shape, dt=F32):
            d[name] = nc.dram_tensor(name, list(shape), dt, kind="ExternalInput").ap()

        def do(name, shape, dt=F32):
            d[name] = nc.dram_tensor(name, list(shape), dt, kind="ExternalOutput").ap()

        def ds(name, shape, dt=F32):
            d[name] = nc.dram_tensor(name, list(shape), dt, kind="ExternalOutput" if debug else "Internal").ap()

        NLD = self.nl if debug else DEPTH
        di("xT", [D, T])
        di("wtm", [NLD, D, TMW]); di("wfm", [NLD, D, FMW])
        di("wbr", [NLD, 3, 1024, D]); di("wo", [NLD, D, D])
        di("wgu", [NLD, D, 2 * DFF]); di("wdn", [NLD, DFF, D])
        di("pack", [NLD, 128, NPK])
        di("cst", [128, NCST])
        di("cossin", [T, 128])
        di("sret", [NLD, 16, 4, 128, 256]); di("sgdn", [NLD, 16, 8, 128, 128])
        di("sgla", [NLD, 16, 4, 128, 256]); di("sconv", [NLD, 128, 24 * 16 * 3])
        do("yT", [D, T])
        do("o_pret", [NLD, 4, 128, 256]); do("o_pgdn", [NLD, 8, 128, 128])
        do("o_pconv", [NLD, 3072, 3]); do("o_pgla", [NLD, 4, 128, 256])
        do("o_sret", [NLD, 16, 4, 128, 256]); do("o_sgdn", [NLD, 16, 8, 128, 128])
        do("o_sconv", [NLD, 128, 24 * 16 * 3]); do("o_sgla", [NLD, 16, 4, 128, 256])
        ds("XR", [D, T]); ds("PTM", [T, TMW]); ds("PFM", [FMW, T])
        ds("YBT", [3072, T], BF16); ds("MT", [D, T], BF16); ds("FT", [DFF, T], BF16)

    def build(self):
        k = self.k
        with k:
            self.cst = k.sb("cst", [128, NCST], F32)
            k.dma("sp", self.cst.ap, self.dr["cst"], writes=[self.cst])
            self.pk = k.sb("pk", [128, NPK], F32)
            k.dma("sp", self.dr["XR"], self.dr["xT"])
            k.barrier()
            ph = self.phases or ("win", "ret", "gla", "gdn", "branch", "wo", "ffn", "final")
            for l in range(self.nl):
                k.dma("sp", self.pk.ap, self.dr["pack"][l], writes=[self.pk])
                k.barrier()
                for p in ("win", "ret", "gla", "gdn", "branch", "wo", "ffn"):
                    if p in ph:
                        getattr(self, "phase_" + p)(l)
            if "final" in ph:
                self.phase_final()
            k.finish()

    def C(self, name, rows=128):
        o, n = CST_OFF[name]
        return self.cst.t[0:rows, o:o + n]

    def rmsnorm_fm(self, st, gain_off, hT=None, dst=None):
        k = self.k
        src = self.dr["XR"].rearrange("(kc p) t -> p kc t", p=128)
        xs = [k.sb(f"nx{i}", [128, 16, 256], F32, st) for i in range(2)]
        sq = k.sb("nsq", [128, 16, 256], F32, st)
        r1 = k.sb("nr1", [128, 256], F32, st)
        r2 = k.sb("nr2", [128, 256], F32, st)
        r3 = k.sb("nr3", [128, 256], F32, st)
        oo = [k.sb(f"no{i}", [128, 16, 256], F32, st) for i in range(2)] if dst is not None else None
        ones = self.C("ones")
        for bi, (t0, n) in enumerate(blocks(T, 256)):
            xb = xs[bi % 2]
            k.dma("sp", xb.t[:, :, 0:n], src[:, :, t0:t0 + n], writes=[xb])
            k.op("act", lambda e: e.activation(sq.t[:, :, 0:n], xb.t[:, :, 0:n], AF.Square), reads=[xb], writes=[sq])
            ps = k.psum()
            for kc in range(16):
                k.op("pe", lambda e: e.matmul(ps.t[:, 0:n], ones, sq.t[:, kc, 0:n], start=(kc == 0), stop=(kc == 15)),
                     reads=[sq, self.cst], writes=[ps])
            k.op("dve", lambda e: e.tensor_scalar(r1.t[:, 0:n], ps.t[:, 0:n], 1.0 / D, EPS, ALU.mult, ALU.add),
                 reads=[ps], writes=[r1])
            k.op("act", lambda e: e.activation(r2.t[:, 0:n], r1.t[:, 0:n], AF.Sqrt), reads=[r1], writes=[r2])
            k.op("dve", lambda e: e.reciprocal(r3.t[:, 0:n], r2.t[:, 0:n]), reads=[r2], writes=[r3])
            for kc in range(16):
                g = self.pk.t[:, gain_off + kc:gain_off + kc + 1]
                if dst is None:
                    k.op("dve", lambda e: e.scalar_tensor_tensor(hT.t[:, kc, t0:t0 + n], xb.t[:, kc, 0:n], g,
                                                                 r3.t[:, 0:n], ALU.mult, ALU.mult),
                         reads=[xb, r3, self.pk], writes=[hT])
                else:
                    ob = oo[bi % 2]
                    k.op("dve", lambda e: e.scalar_tensor_tensor(ob.t[:, kc, 0:n], xb.t[:, kc, 0:n], g,
                                                                 r3.t[:, 0:n], ALU.mult, ALU.mult),
                         reads=[xb, r3, self.pk], writes=[ob])
            if dst is not None:
                ob = oo[bi % 2]
                k.dma("pool", dst.rearrange("(kc p) t -> p kc t", p=128)[:, :, t0:t0 + n], ob.t[:, :, 0:n], reads=[ob])

    def load_w_dma(self, wraw, src):
        self.k.dma("sp", wraw.ap, src, writes=[wraw])

    def load_w_cast(self, wraw, wbf, KCi):
        k = self.k
        h = KCi // 2
        k.op("act", lambda e: e.activation(wbf.t[:, 0:h, :], wraw.t[:, 0:h, :], AF.Copy), reads=[wraw], writes=[wbf])
        k.op("pool", lambda e: e.tensor_copy(wbf.t[:, h:KCi, :], wraw.t[:, h:KCi, :]), reads=[wraw], writes=[wbf])

    def load_w(self, wraw, wbf, src, KCi):
        k = self.k
        k.dma("sp", wraw.ap, src, writes=[wraw])
        h = KCi // 2
        k.op("act", lambda e: e.activation(wbf.t[:, 0:h, :], wraw.t[:, 0:h, :], AF.Copy), reads=[wraw], writes=[wbf])
        k.op("pool", lambda e: e.tensor_copy(wbf.t[:, h:KCi, :], wraw.t[:, h:KCi, :]), reads=[wraw], writes=[wbf])

    def dense_fm(self, st, inT, KCi, tokb, wsrc_fn, ncols, WN, epi, pre=None, tag="w", tbase=0):
        k = self.k
        wraw = [k.sb(f"{tag}raw{i}", [128, KCi, WN], F32, st) for i in range(2)]
        wbf = [k.sb(f"{tag}bf{i}", [128, KCi, WN], BF16, st) for i in range(2)]
        nw = ncols // WN
        self.load_w(wraw[0], wbf[0], wsrc_fn(0, WN), KCi)
        for wi in range(nw):
            wr, wb = wraw[wi % 2], wbf[wi % 2]
            if wi + 1 < nw:
                self.load_w_dma(wraw[(wi + 1) % 2], wsrc_fn((wi + 1) * WN, WN))
            for bi_, (t0, n) in enumerate(tokb):
                if wi + 1 < nw and bi_ == max(0, len(tokb) - 2):
                    self.load_w_cast(wraw[(wi + 1) % 2], wbf[(wi + 1) % 2], KCi)
                for sub in range(WN // 128):
                    c0 = wi * WN + sub * 128
                    if pre is not None:
                        pre(c0, t0, n)
                    ps = k.psum()
                    for kc in range(KCi):
                        k.op("pe", lambda e: e.matmul(ps.t[:, 0:n], wb.t[:, kc, sub * 128:(sub + 1) * 128],
                                                      inT.t[:, kc, t0 - tbase:t0 - tbase + n],
                                                      start=(kc == 0), stop=(kc == KCi - 1)),
                             reads=[wb, inT], writes=[ps])
                    epi(c0, t0, n, ps)

    def phase_win(self, l):
        k = self.k
        with contextlib.ExitStack() as st:
            hT = k.sb("hT", [128, 16, T], BF16, st)
            with contextlib.ExitStack() as st2:
                self.rmsnorm_fm(st2, PK_GMIX, hT=hT)
                k.barrier()
            with contextlib.ExitStack() as st2:
                wraw = [k.sb(f"tmraw{i}", [128, 16, 256], F32, st2) for i in range(2)]
                wbf = [k.sb(f"tmbf{i}", [128, 16, 256], BF16, st2) for i in range(2)]
                ost = [k.sb(f"tmo{i}", [128, 256], F32, st2) for i in range(4)]
                wsrc = self.dr["wtm"][l].rearrange("(kc p) n -> p kc n", p=128)
                cnt = 0
                self.load_w(wraw[0], wbf[0], wsrc[:, :, 0:256], 16)
                for ci in range(TMW // 256):
                    c0 = ci * 256
                    ncol = min(256, 7200 - c0)
                    wr, wb = wraw[ci % 2], wbf[ci % 2]
                    if ci + 1 < TMW // 256:
                        self.load_w_dma(wraw[(ci + 1) % 2], wsrc[:, :, c0 + 256:c0 + 512])
                    for tt in range(NT):
                        if ci + 1 < TMW // 256 and tt == 13:
                            self.load_w_cast(wraw[(ci + 1) % 2], wbf[(ci + 1) % 2], 16)
                        R = 128 if tt < 16 else 64
                        ps = k.psum()
                        for kc in range(16):
                            k.op("pe", lambda e: e.matmul(ps.t[0:R, 0:ncol], hT.t[:, kc, tt * 128:tt * 128 + R],
                                                          wb.t[:, kc, 0:ncol], start=(kc == 0), stop=(kc == 15)),
                                 reads=[wb, hT], writes=[ps])
                        ob = ost[cnt % 4]
                        if cnt % 2 == 0:
                            k.op("act", lambda e: e.activation(ob.t[0:R, 0:ncol], ps.t[0:R, 0:ncol], AF.Copy),
                                 reads=[ps], writes=[ob])
                        else:
                            k.op("dve", lambda e: e.tensor_copy(ob.t[0:R, 0:ncol], ps.t[0:R, 0:ncol]),
                                 reads=[ps], writes=[ob])
                        k.dma("pool" if cnt % 2 else "sp", self.dr["PTM"][tt * 128:tt * 128 + R, c0:c0 + ncol],
                              ob.t[0:R, 0:ncol], reads=[ob])
                        cnt += 1
                k.barrier()
            with contextlib.ExitStack() as st2:
                ost = [k.sb(f"fmo{i}", [128, 512], F32, st2) for i in range(4)]
                wsrc = self.dr["wfm"][l].rearrange("(kc p) n -> p kc n", p=128)
                cnt = [0]

                def epi(c0, t0, n, ps):
                    ob = ost[cnt[0] % 4]
                    if c0 < 3072:
                        if cnt[0] % 2 == 0:
                            k.op("act", lambda e: e.activation(ob.t[:, 0:n], ps.t[:, 0:n], AF.Copy), reads=[ps], writes=[ob])
                        else:
                            k.op("dve", lambda e: e.tensor_copy(ob.t[:, 0:n], ps.t[:, 0:n]), reads=[ps], writes=[ob])
                    else:
                        ch = (c0 - 3072) // 128
                        b = self.pk.t[:, PK_BM + ch:PK_BM + ch + 1]
                        k.op("act", lambda e: e.activation(ob.t[:, 0:n], ps.t[:, 0:n], AF.Sigmoid, bias=b),
                             reads=[ps, self.pk], writes=[ob])
                    k.dma("pool" if cnt[0] % 2 else "sp", self.dr["PFM"][c0:c0 + 128, t0:t0 + n], ob.t[:, 0:n], reads=[ob])
                    cnt[0] += 1

                self.dense_fm(st2, hT, 16, TOKB, lambda c, w: wsrc[:, :, c:c + w], FMW, 256, epi, tag="fm")
                k.barrier()

    def load_act_bf(self, dst, src_fm, KCi, t0, n):
        k = self.k
        v = src_fm.rearrange("(kc p) t -> p kc t", p=128)
        for (a, m) in blocks(KCi, 8):
            k.dma("sp", dst.t[:, a:a + m, 0:n], v[:, a:a + m, t0:t0 + n], writes=[dst])

    def phase_branch(self, l):
        k = self.k
        for (s0, sn) in blocks(T, 1056):
            with contextlib.ExitStack() as st:
                ybt = k.sb("ybt", [128, 24, 1056], BF16, st)
                self.load_act_bf(ybt, self.dr["YBT"], 24, s0, sn)
                wraw = [k.sb(f"brraw{i}", [128, 8, 256], F32, st) for i in range(3)]
                wbf = [[k.sb(f"brbf{j}_{i}", [128, 8, 256], BF16, st) for i in range(3)] for j in range(2)]
                gts = [k.sb(f"brg{i}", [128, 512], F32, st) for i in range(6)]
                tmp = [k.sb(f"brt{i}", [128, 512], F32, st) for i in range(6)]
                mo = [k.sb(f"brm{i}", [128, 512], BF16, st) for i in range(2)]
                cnt = 0
                for ci in range(D // 256):
                    for b in range(3):
                        src = self.dr["wbr"][l, b].rearrange("(kc p) n -> p kc n", p=128)[:, :, ci * 256:(ci + 1) * 256]
                        self.load_w(wraw[b], wbf[ci % 2][b], src, 8)
                    for (t0, n) in blocks(sn, 512, s0):
                        for sub in range(2):
                            c0 = ci * 256 + sub * 128
                            tl = []
                            for b in range(3):
                                wb = wbf[ci % 2][b]
                                gt = gts[(cnt * 3 + b) % 6]
                                r0 = 3072 + b * 2048 + c0
                                k.dma("pool", gt.t[:, 0:n], self.dr["PFM"][r0:r0 + 128, t0:t0 + n], writes=[gt])
                                ps = k.psum()
                                for kc in range(8):
                                    k.op("pe", lambda e: e.matmul(ps.t[:, 0:n], wb.t[:, kc, sub * 128:(sub + 1) * 128],
                                                                  ybt.t[:, b * 8 + kc, t0 - s0:t0 - s0 + n],
                                                                  start=(kc == 0), stop=(kc == 7)),
                                         reads=[wb, ybt], writes=[ps])
                                tb = tmp[(cnt * 3 + b) % 6]
                                k.op("dve", lambda e: e.tensor_tensor(tb.t[:, 0:n], ps.t[:, 0:n], gt.t[:, 0:n], ALU.mult),
                                     reads=[ps, gt], writes=[tb])
                                tl.append(tb)
                            k.op("pool", lambda e: e.tensor_tensor(tl[0].t[:, 0:n], tl[0].t[:, 0:n], tl[1].t[:, 0:n], ALU.add),
                                 reads=[tl[1], tl[0]], writes=[tl[0]])
                            m = mo[cnt % 2]
                            k.op("pool", lambda e: e.tensor_tensor(m.t[:, 0:n], tl[0].t[:, 0:n], tl[2].t[:, 0:n], ALU.add),
                                 reads=[tl[0], tl[2]], writes=[m])
                            k.dma("sp", self.dr["MT"][c0:c0 + 128, t0:t0 + n], m.t[:, 0:n], reads=[m])
                            cnt += 1
                k.barrier()

    def resid_epi(self, st, tag):
        k = self.k
        xin = [k.sb(f"{tag}xi{i}", [128, 512], F32, st) for i in range(4)]
        cnt = [0]
        cur = {}

        def pre(c0, t0, n):
            xb = xin[cnt[0] % 4]
            k.dma("pool", xb.t[:, 0:n], self.dr["XR"][c0:c0 + 128, t0:t0 + n], writes=[xb])
            cur["xb"] = xb

        def epi(c0, t0, n, ps):
            xb = cur["xb"]
            k.op("dve", lambda e: e.tensor_tensor(xb.t[:, 0:n], xb.t[:, 0:n], ps.t[:, 0:n], ALU.add), reads=[ps, xb], writes=[xb])
            k.dma("pool", self.dr["XR"][c0:c0 + 128, t0:t0 + n], xb.t[:, 0:n], reads=[xb])
            cnt[0] += 1

        return pre, epi

    def phase_wo(self, l):
        k = self.k
        with contextlib.ExitStack() as st:
            mt = k.sb("mt", [128, 16, T], BF16, st)
            self.load_act_bf(mt, self.dr["MT"], 16, 0, T)
            pre, epi = self.resid_epi(st, "wo")
            wsrc = self.dr["wo"][l].rearrange("(kc p) n -> p kc n", p=128)
            self.dense_fm(st, mt, 16, TOKB, lambda c, w: wsrc[:, :, c:c + w], D, 256, epi, pre=pre, tag="wo")
            k.barrier()

    def phase_ffn(self, l):
        k = self.k
        with contextlib.ExitStack() as st:
            hT = k.sb("h2T", [128, 16, T], BF16, st)
            with contextlib.ExitStack() as st2:
                self.rmsnorm_fm(st2, PK_GFFN, hT=hT)
                k.barrier()
            sg = [k.sb(f"sg{i}", [128, 512], F32, st) for i in range(2)]
            fo = [k.sb(f"fo{i}", [128, 512], BF16, st) for i in range(2)]
            cnt = [0]

            def epi(c0, t0, n, ps):
                ft = c0 // 256
                if (c0 // 128) % 2 == 0:
                    s_ = sg[cnt[0] % 2]
                    k.op("act", lambda e: e.activation(s_.t[:, 0:n], ps.t[:, 0:n], AF.Silu), reads=[ps], writes=[s_])
                else:
                    s_ = sg[cnt[0] % 2]
                    o_ = fo[cnt[0] % 2]
                    k.op("dve", lambda e: e.tensor_tensor(o_.t[:, 0:n], s_.t[:, 0:n], ps.t[:, 0:n], ALU.mult),
                         reads=[ps, s_], writes=[o_])
                    k.dma("pool", self.dr["FT"][ft * 128:(ft + 1) * 128, t0:t0 + n], o_.t[:, 0:n], reads=[o_])
                    cnt[0] += 1

            wsrc = self.dr["wgu"][l].rearrange("(kc p) n -> p kc n", p=128)
            self.dense_fm(st, hT, 16, TOKB, lambda c, w: wsrc[:, :, c:c + w], 2 * DFF, 256, epi, tag="gu")
            k.barrier()
        for (s0, sn) in blocks(T, 1056):
            with contextlib.ExitStack() as st:
                ft = k.sb("ftT", [128, 44, 1056], BF16, st)
                self.load_act_bf(ft, self.dr["FT"], 44, s0, sn)
                pre, epi = self.resid_epi(st, "dn")
                wsrc = self.dr["wdn"][l].rearrange("(kc p) n -> p kc n", p=128)
                self.dense_fm(st, ft, 44, blocks(sn, 512, s0), lambda c, w: wsrc[:, :, c:c + w], D, 128, epi, pre=pre,
                              tag="dn", tbase=s0)
                k.barrier()

    def phase_final(self):
        k = self.k
        with contextlib.ExitStack() as st:
            self.rmsnorm_fm(st, PK_GFIN, dst=self.dr["yT"])
            k.barrier()

    def transposes(self, src_fn, nch, dst, R, reads, evac_scale=None):
        k = self.k
        ident = self.C("ident")
        for g0 in range(0, nch, 4):
            m = min(4, nch - g0)
            ps = k.psum()
            for c in range(m):
                k.op("pe", lambda e: e.transpose(ps.t[:, c * 128:c * 128 + R], src_fn(g0 + c), ident[0:R, 0:R]),
                     reads=list(reads) + [self.cst], writes=[ps])
            pv = ps.t[:, :].rearrange("p (a b) -> p a b", a=4)[:, 0:m, 0:R]
            if (g0 // 4) % 2 == 0:
                k.op("act", lambda e: e.activation(dst.t[:, g0:g0 + m, 0:R], pv, AF.Copy), reads=[ps], writes=[dst])
            else:
                k.op("dve", lambda e: e.tensor_copy(dst.t[:, g0:g0 + m, 0:R], pv), reads=[ps], writes=[dst])

    def rstd_cols(self, ss, r1, r2, rstd, R, n, inv_n):
        k = self.k
        k.op("dve", lambda e: e.tensor_scalar(r1.t[0:R, 0:n], ss.t[0:R, 0:n], inv_n, EPS, ALU.mult, ALU.add), reads=[ss], writes=[r1])
        k.op("act", lambda e: e.activation(r2.t[0:R, 0:n], r1.t[0:R, 0:n], AF.Sqrt), reads=[r1], writes=[r2])
        k.op("dve", lambda e: e.reciprocal(rstd.t[0:R, 0:n], r2.t[0:R, 0:n]), reads=[r2], writes=[rstd])

    def phase_la(self, l, kind):
        k = self.k
        X = mybir.AxisListType.X
        ret = (kind == "ret")
        cqk, cv, cg = (C_RQK, C_RV, C_RG) if ret else (C_LQK, C_LV, C_LGT)
        s_in = self.dr["sret" if ret else "sgla"]
        s_out = self.dr["o_sret" if ret else "o_sgla"]
        p_out = self.dr["o_pret" if ret else "o_pgla"]
        ybase = 0 if ret else 2048
        PTM = self.dr["PTM"]
        with contextlib.ExitStack() as st:
            sb = lambda n, shp, dt=F32: k.sb(f"{kind}_{n}", shp, dt, st)
            qk = [sb(f"qk{i}", [128, 1024]) for i in range(2)]
            vv = [sb(f"v{i}", [128, 1024]) for i in range(2)]
            gg = [sb(f"g{i}", [128, 1024]) for i in range(2)]
            cs = [sb(f"cs{i}", [128, 128]) for i in range(2)]
            lr = [sb(f"lr{i}", [128, 16]) for i in range(2)]
            t1, t2, t3, t4 = [sb(f"t{i}", [128, 512]) for i in range(4)]
            qkr = sb("qkr", [128, 1024])
            qkd = sb("qkd", [128, 1024])
            qkT = sb("qkT", [128, 8, 128], BF16)
            kdb = sb("kdb", [128, 4, 128], BF16)
            vbf = sb("vbf", [128, 1024], BF16)
            sgt = sb("sgt", [128, 1024])
            scb = [sb(f"sc{i}", [128, 128], BF16) for i in range(2)]
            sqo = sb("sqo", [128, 256])
            y = sb("y", [128, 1024])
            yT = [sb(f"yT{i}", [128, 8, 128], BF16) for i in range(2)]
            ss = sb("ss", [128, 4]); r1 = sb("r1", [128, 4]); r2 = sb("r2", [128, 4]); rstd = sb("rstd", [128, 4])
            S = sb("S", [128, 4, 256]); Sbf = sb("Sbf", [128, 4, 256], BF16)
            Sall = sb("Sall", [128, 16, 256]); Sallbf = sb("Sallbf", [128, 16, 256], BF16)
            Snew = sb("Snew", [128, 16, 256])
            qbig = sb("qbig", [128, 16, 64], BF16); kbig = sb("kbig", [128, 16, 128], BF16)
            if not ret:
                z = sb("z", [128, 512]); ez = sb("ez", [128, 512]); lz = sb("lz", [128, 512])
                Bsb = sb("Bsb", [128, 512]); eb = sb("eb", [128, 512]); enb = sb("enb", [128, 512])
                dfb = sb("dfb", [128, 512]); ekd = sb("ekd", [128, 512])
                llrT = sb("llrT", [16, 128]); ebl = sb("ebl", [128, 4, 16])
            k.op("dve", lambda e: e.memset(S.ap, 0.0), writes=[S])
            k.op("pool", lambda e: e.memset(Sbf.ap, 0.0), writes=[Sbf])

            def load(tt):
                R = 128 if tt < 16 else 64
                r0 = tt * 128
                i = tt % 2
                k.dma("sp", qk[i].t[0:R, :], PTM[r0:r0 + R, cqk:cqk + 1024], writes=[qk[i]])
                k.dma("sp", vv[i].t[0:R, :], PTM[r0:r0 + R, cv:cv + 1024], writes=[vv[i]])
                k.dma("sp", gg[i].t[0:R, :], PTM[r0:r0 + R, cg:cg + 1024], writes=[gg[i]])
                if ret:
                    k.dma("sp", cs[i].t[0:R, :], self.dr["cossin"][r0:r0 + R, :], writes=[cs[i]])
                else:
                    k.dma("sp", lr[i].t[0:R, :], PTM[r0:r0 + R, C_LLR:C_LLR + 16], writes=[lr[i]])

            load(0)
            for tt in range(NT):
                if tt + 1 < NT:
                    load(tt + 1)
                R = 128 if tt < 16 else 64
                smp = tt == 16
                sfx = "_s" if smp else "_p"
                i = tt % 2
                qkb, vb, gb = qk[i], vv[i], gg[i]
                if ret:
                    x4 = qkb.t[0:R, :].rearrange("p (g d two) -> p g d two", g=8, two=2)
                    o4 = qkr.t[0:R, :].rearrange("p (g d two) -> p g d two", g=8, two=2)
                    x1, x2 = x4[:, :, :, 0], x4[:, :, :, 1]
                    cosb = cs[i].t[0:R, 0:64].unsqueeze(1).to_broadcast([R, 8, 64])
                    sinb = cs[i].t[0:R, 64:128].unsqueeze(1).to_broadcast([R, 8, 64])
                    v3 = lambda b: b.t[0:R, :].rearrange("p (g d) -> p g d", g=8)
                    k.op("dve", lambda e: e.tensor_tensor(v3(t1), x1, cosb, ALU.mult), reads=[qkb, cs[i]], writes=[t1])
                    k.op("pool", lambda e: e.tensor_tensor(v3(t2), x2, sinb, ALU.mult), reads=[qkb, cs[i]], writes=[t2])
                    k.op("dve", lambda e: e.tensor_tensor(o4[:, :, :, 0], v3(t1), v3(t2), ALU.subtract), reads=[t1, t2], writes=[qkr])
                    k.op("pool", lambda e: e.tensor_tensor(v3(t3), x1, sinb, ALU.mult), reads=[qkb, cs[i]], writes=[t3])
                    k.op("dve", lambda e: e.tensor_tensor(v3(t4), x2, cosb, ALU.mult), reads=[qkb, cs[i]], writes=[t4])
                    k.op("pool", lambda e: e.tensor_tensor(o4[:, :, :, 1], v3(t3), v3(t4), ALU.add), reads=[t3, t4], writes=[qkr])
                    tab = self.C("DQK" + sfx, R).unsqueeze(2).to_broadcast([R, 8, 128])
                    k.op("dve", lambda e: e.tensor_tensor(qkd.t[0:R, :].rearrange("p (g d) -> p g d", g=8),
                                                          qkr.t[0:R, :].rearrange("p (g d) -> p g d", g=8), tab, ALU.mult),
                         reads=[qkr, self.cst], writes=[qkd])
                else:
                    ps = k.psum()
                    k.op("pe", lambda e: e.transpose(ps.t[0:16, 0:R], lr[i].t[0:R, 0:16], self.C("ident")[0:R, 0:R]),
                         reads=[lr[i], self.cst], writes=[ps])
                    k.op("act", lambda e: e.activation(llrT.t[0:16, 0:R], ps.t[0:16, 0:R], AF.Copy), reads=[ps], writes=[llrT])
                    ps = k.psum()
                    k.op("pe", lambda e: e.matmul(ps.t[0:R, 0:512], llrT.t[0:16, 0:R], self.pk.t[0:16, PK_WUP:PK_WUP + 512],
                                                  start=True, stop=True), reads=[llrT, self.pk], writes=[ps])
                    k.op("dve", lambda e: e.tensor_tensor(z.t[0:R, :], ps.t[0:R, 0:512], self.pk.t[0:R, PK_BUP:PK_BUP + 512], ALU.add),
                         reads=[ps, self.pk], writes=[z])
                    k.op("act", lambda e: e.activation(ez.t[0:R, :], z.t[0:R, :], AF.Exp, scale=-1.0), reads=[z], writes=[ez])
                    k.op("dve", lambda e: e.tensor_scalar(ez.t[0:R, :], ez.t[0:R, :], 1.0, None, ALU.add), reads=[ez], writes=[ez])
                    k.op("act", lambda e: e.activation(lz.t[0:R, :], ez.t[0:R, :], AF.Ln), reads=[ez], writes=[lz])
                    psB = k.psum()
                    k.op("pe", lambda e: e.matmul(psB.t[0:R, 0:512], self.C("MTi" + sfx, R)[:, 0:R], lz.t[0:R, :], start=True, stop=True),
                         reads=[lz, self.cst], writes=[psB])
                    psL = k.psum()
                    k.op("pe", lambda e: e.matmul(psL.t[0:R, 0:512], self.C("ALL" + sfx, R)[:, 0:R], lz.t[0:R, :], start=True, stop=True),
                         reads=[lz, self.cst], writes=[psL])
                    k.op("act", lambda e: e.activation(Bsb.t[0:R, :], psB.t[0:R, 0:512], AF.Copy), reads=[psB], writes=[Bsb])
                    k.op("act", lambda e: e.activation(eb.t[0:R, :], Bsb.t[0:R, :], AF.Exp, scale=-1.0 / 16), reads=[Bsb], writes=[eb])
                    k.op("act", lambda e: e.activation(enb.t[0:R, :], Bsb.t[0:R, :], AF.Exp, scale=1.0 / 16), reads=[Bsb], writes=[enb])
                    k.op("dve", lambda e: e.tensor_tensor(dfb.t[0:R, :], Bsb.t[0:R, :], psL.t[0:R, 0:512], ALU.subtract),
                         reads=[Bsb, psL], writes=[dfb])
                    k.op("act", lambda e: e.activation(ekd.t[0:R, :], dfb.t[0:R, :], AF.Exp, scale=1.0 / 16), reads=[dfb], writes=[ekd])
                    k.op("dve", lambda e: e.scalar_tensor_tensor(qkd.t[0:R, 0:512], qkb.t[0:R, 0:512], 128 ** -0.5, eb.t[0:R, :],
                                                                 ALU.mult, ALU.mult), reads=[qkb, eb], writes=[qkd])
                    k.op("pool", lambda e: e.tensor_tensor(qkd.t[0:R, 512:1024], qkb.t[0:R, 512:1024], enb.t[0:R, :], ALU.mult),
                         reads=[qkb, enb], writes=[qkd])
                    for h in range(4):
                        ps = k.psum()
                        k.op("pe", lambda e: e.matmul(ps.t[:, 0:16], lz.t[0:R, h * 128:(h + 1) * 128], self.C("SEL" + sfx, R),
                                                      start=True, stop=True), reads=[lz, self.cst], writes=[ps])
                        k.op("act", lambda e: e.activation(ebl.t[:, h, :], ps.t[:, 0:16], AF.Exp, scale=-1.0 / 16), reads=[ps], writes=[ebl])
                self.transposes(lambda c: qkd.t[0:R, c * 128:(c + 1) * 128], 8, qkT, R, [qkd])
                if ret:
                    k.op("act", lambda e: e.activation(kdb.t[0:R, :, :], qkd.t[0:R, 512:1024].rearrange("p (g d) -> p g d", g=4), AF.Copy),
                         reads=[qkd], writes=[kdb])
                else:
                    k.op("dve", lambda e: e.tensor_tensor(kdb.t[0:R, :, :], qkb.t[0:R, 512:1024].rearrange("p (g d) -> p g d", g=4),
                                                          ekd.t[0:R, :].rearrange("p (g d) -> p g d", g=4), ALU.mult),
                         reads=[qkb, ekd], writes=[kdb])
                k.op("pool", lambda e: e.tensor_copy(vbf.t[0:R, :], vb.t[0:R, :]), reads=[vb], writes=[vbf])
                k.op("act", lambda e: e.activation(sgt.t[0:R, :], gb.t[0:R, :], AF.Silu), reads=[gb], writes=[sgt])
                if not ret:
                    gn = self.pk.t[0:R, PK_GLAN:PK_GLAN + 256].unsqueeze(1).to_broadcast([R, 4, 256])
                    s3 = sgt.t[0:R, :].rearrange("p (g d) -> p g d", g=4)
                    k.op("pool", lambda e: e.tensor_tensor(s3, s3, gn, ALU.mult), reads=[sgt, self.pk], writes=[sgt])
                for h in range(4):
                    ps_sc = k.psum()
                    k.op("pe", lambda e: e.matmul(ps_sc.t[0:R, 0:R], qkT.t[:, 4 + h, 0:R], qkT.t[:, h, 0:R], start=True, stop=True),
                         reads=[qkT], writes=[ps_sc])
                    sc = scb[h % 2]
                    if ret:
                        mtab = self.C("RD" + sfx, R)[:, h * 128:h * 128 + R]
                    else:
                        mtab = self.C("MTi" + sfx, R)[:, 0:R]
                    k.op("dve", lambda e: e.tensor_tensor(sc.t[0:R, 0:R], ps_sc.t[0:R, 0:R], mtab, ALU.mult),
                         reads=[ps_sc, self.cst], writes=[sc])
                    if smp:
                        k.dma("sp", Sall.ap, s_in[l, :, h].rearrange("s d e -> d s e"), writes=[Sall])
                        k.op("act", lambda e: e.activation(Sallbf.t[:, 0:8, :], Sall.t[:, 0:8, :], AF.Copy), reads=[Sall], writes=[Sallbf])
                        k.op("pool", lambda e: e.tensor_copy(Sallbf.t[:, 8:16, :], Sall.t[:, 8:16, :]), reads=[Sall], writes=[Sallbf])
                        k.op("dve", lambda e: e.tensor_tensor(qbig.ap, qkT.t[:, h, 0:64].unsqueeze(1).to_broadcast([128, 16, 64]),
                                                              self.C("BM").rearrange("p (s i) -> p s i", s=16), ALU.mult),
                             reads=[qkT, self.cst], writes=[qbig])
                    ps_o = k.psum()
                    k.op("pe", lambda e: e.matmul(ps_o.t[0:R, 0:256], sc.t[0:R, 0:R], vbf.t[0:R, h * 256:(h + 1) * 256], start=True, stop=False),
                         reads=[sc, vbf], writes=[ps_o])
                    if not smp:
                        k.op("pe", lambda e: e.matmul(ps_o.t[0:R, 0:256], qkT.t[:, h, 0:R], Sbf.t[:, h, :], start=False, stop=True),
                             reads=[qkT, Sbf], writes=[ps_o])
                    else:
                        for s in range(16):
                            k.op("pe", lambda e: e.matmul(ps_o.t[0:R, 0:256], qbig.t[:, s, :], Sallbf.t[:, s, :], start=False, stop=(s == 15)),
                                 reads=[qbig, Sallbf], writes=[ps_o])
                    k.op("act", lambda e: e.activation(sqo.t[0:R, :], ps_o.t[0:R, 0:256], AF.Square), reads=[ps_o], writes=[sqo])
                    k.op("dve", lambda e: e.tensor_reduce(ss.t[0:R, h:h + 1], sqo.t[0:R, :], X, ALU.add), reads=[sqo], writes=[ss])
                    k.op("dve", lambda e: e.tensor_scalar(r1.t[0:R, h:h + 1], ss.t[0:R, h:h + 1], 1.0 / 256, EPS, ALU.mult, ALU.add), reads=[ss], writes=[r1])
                    k.op("act", lambda e: e.activation(r2.t[0:R, h:h + 1], r1.t[0:R, h:h + 1], AF.Sqrt), reads=[r1], writes=[r2])
                    k.op("dve", lambda e: e.reciprocal(rstd.t[0:R, h:h + 1], r2.t[0:R, h:h + 1]), reads=[r2], writes=[rstd])
                    k.op("dve", lambda e: e.scalar_tensor_tensor(y.t[0:R, h * 256:(h + 1) * 256], ps_o.t[0:R, 0:256], rstd.t[0:R, h:h + 1],
                                                                 sgt.t[0:R, h * 256:(h + 1) * 256], ALU.mult, ALU.mult),
                         reads=[ps_o, rstd, sgt], writes=[y])
                    if not smp:
                        ps_s = k.psum()
                        k.op("pe", lambda e: e.matmul(ps_s.t[:, 0:256], kdb.t[0:R, h, :], vbf.t[0:R, h * 256:(h + 1) * 256], start=True, stop=True),
                             reads=[kdb, vbf], writes=[ps_s])
                        dec = float(GAMMA[h] ** 128.0) if ret else ebl.t[:, h, 0:1]
                        k.op("dve", lambda e: e.scalar_tensor_tensor(S.t[:, h, :], S.t[:, h, :], dec, ps_s.t[:, 0:256], ALU.mult, ALU.add),
                             reads=[ps_s, S] + ([] if ret else [ebl]), writes=[S])
                        k.op("act", lambda e: e.activation(Sbf.t[:, h, :], S.t[:, h, :], AF.Copy), reads=[S], writes=[Sbf])
                        if tt == 15:
                            k.dma("pool", p_out[l, h], S.t[:, h, :], reads=[S])
                    else:
                        k.op("dve", lambda e: e.tensor_tensor(kbig.t[0:64, :, :], kdb.t[0:64, h, :].unsqueeze(1).to_broadcast([64, 16, 128]),
                                                              self.C("SEL_s", 64).unsqueeze(2).to_broadcast([64, 16, 128]), ALU.mult),
                             reads=[kdb, self.cst], writes=[kbig])
                        for s in range(16):
                            ps_s = k.psum()
                            k.op("pe", lambda e: e.matmul(ps_s.t[:, 0:256], kbig.t[0:64, s, :], vbf.t[0:64, h * 256:(h + 1) * 256], start=True, stop=True),
                                 reads=[kbig, vbf], writes=[ps_s])
                            dec = float(GAMMA[h] ** 4.0) if ret else ebl.t[:, h, s:s + 1]
                            k.op("dve", lambda e: e.scalar_tensor_tensor(Snew.t[:, s, :], Sall.t[:, s, :], dec, ps_s.t[:, 0:256], ALU.mult, ALU.add),
                                 reads=[ps_s, Sall] + ([] if ret else [ebl]), writes=[Snew])
                        k.dma("pool", s_out[l, :, h].rearrange("s d e -> d s e"), Snew.ap, reads=[Snew])
                yt = yT[tt % 2]
                self.transposes(lambda c: y.t[0:R, c * 128:(c + 1) * 128], 8, yt, R, [y])
                k.dma("pool", self.dr["YBT"][ybase:ybase + 1024, tt * 128:tt * 128 + R].rearrange("(c p) t -> p c t", p=128),
                      yt.t[:, :, 0:R], reads=[yt])
            k.barrier()

    def phase_ret(self, l):
        self.phase_la(l, "ret")

    def phase_gla(self, l):
        self.phase_la(l, "gla")

    def phase_gdn(self, l):
        k = self.k
        X = mybir.AxisListType.X
        PTM, PFM = self.dr["PTM"], self.dr["PFM"]
        ident = self.C("ident")
        ones = self.C("ones")
        with contextlib.ExitStack() as st:
            sb = lambda n, shp, dt=F32: k.sb(f"gd_{n}", shp, dt, st)
            xc = [sb(f"xc{i}", [128, 24, 131]) for i in range(2)]
            xs = sb("xs", [128, 24, 16, 7])
            xst = sb("xst", [128, 24, 64])
            cst_in = sb("cstin", [128, 24, 16, 3])
            ca = sb("ca", [128, 24, 128]); cb = sb("cb", [128, 24, 128]); cc = cb
            sq = sb("sq", [128, 16, 128]); rr = sb("rr", [128, 16, 128]); rr2 = sq
            qkn = sb("qkn", [128, 16, 128]); qkb = sb("qkb", [128, 16, 128], BF16)
            ktm = sb("ktm", [128, 8, 128]); vtm = sb("vtm", [128, 8, 128])
            dab = [sb(f"dab{i}", [128, 16]) for i in range(2)]
            dzb = [sb(f"dz{i}", [128, 1024]) for i in range(2)]
            sgt = sb("sgt", [128, 1024])
            tg = sb("tg", [128, 8]); eg = sb("eg", [128, 8]); lg = sb("lg", [128, 8]); ea = sb("ea", [128, 8])
            g = sb("g", [128, 8]); beta = sb("beta", [128, 8]); cum = sb("cum", [128, 8]); ecum = sb("ecum", [128, 8])
            dcl = sb("dcl", [128, 8]); ekl = sb("ekl", [128, 8]); becum = sb("becum", [128, 8])
            gsel = sb("gsel", [128, 16, 8]); ecl = sb("ecl", [128, 16, 8])
            NSLOT = 4
            Gh_s = [sb("Gh%d" % i, [128, 128]) for i in range(NSLOT)]
            n1_s = [sb("n1%d" % i, [128, 128]) for i in range(NSLOT)]
            n2_s = [sb("n2%d" % i, [128, 128]) for i in range(NSLOT)]
            expd_s = [sb("expd%d" % i, [128, 128]) for i in range(NSLOT)]
            expdT_s = [sb("expdT%d" % i, [128, 128]) for i in range(NSLOT)]
            ECB_s = [sb("ECB%d" % i, [128, 128]) for i in range(NSLOT)]
            decS_s = [sb("decS%d" % i, [128, 128]) for i in range(NSLOT)]
            decTI_s = [sb("decTI%d" % i, [128, 128]) for i in range(NSLOT)]
            qeT_s = [sb("qeT%d" % i, [128, 128]) for i in range(NSLOT)]
            Lm_s = [sb("Lm%d" % i, [128, 128]) for i in range(NSLOT)]
            AqkT_s = [sb("AqkT%d" % i, [128, 128]) for i in range(NSLOT)]
            rhsu_s = [sb("rhsu%d" % i, [128, 128]) for i in range(NSLOT)]
            rhsw_s = [sb("rhsw%d" % i, [128, 128]) for i in range(NSLOT)]
            kdec_s = [sb("kdec%d" % i, [128, 128]) for i in range(NSLOT)]
            negwT_s = [sb("negwT%d" % i, [128, 128]) for i in range(NSLOT)]
            vnew_s = [sb("vnew%d" % i, [128, 128]) for i in range(NSLOT)]
            Pa_s = [[sb("Pa%d_%d" % (j, i), [128, 128]) for i in range(2)] for j in range(NSLOT)]
            Pt_s = [[sb("Pt%d_%d" % (j, i), [128, 128]) for i in range(2)] for j in range(NSLOT)]
            Rb_s = [[sb("Rb%d_%d" % (j, i), [128, 128]) for i in range(2)] for j in range(NSLOT)]
            sqo_s = [sb("sqo%d" % i, [128, 128]) for i in range(NSLOT)]
            ss_s = [sb("ss%d" % i, [128, 8]) for i in range(NSLOT)]
            r1_s = [sb("r1%d" % i, [128, 8]) for i in range(NSLOT)]
            r2_s = [sb("r2%d" % i, [128, 8]) for i in range(NSLOT)]
            rstd_s = [sb("rstd%d" % i, [128, 8]) for i in range(NSLOT)]
            y = sb("y", [128, 1024]); yT = [sb(f"yT{i}", [128, 8, 128], BF16) for i in range(2)]
            S = sb("S", [128, 8, 128])
            Sall, Snew = sq, rr
            Sall_v = sq.t[:, :, :]; Snew_v = rr.t[:, :, :]
            kbig_v = ca.t[:, 0:16, :]
            wbig_v = ca.t[:, 16:24, :].rearrange("p a b -> p (a b)").rearrange("p (s i) -> p s i", s=16)
            qbig_v = cb.t[:, 0:8, :].rearrange("p a b -> p (a b)").rearrange("p (s i) -> p s i", s=16)
            wbig = kbig = ca
            qbig = cb
            k.op("dve", lambda e: e.memset(S.ap, 0.0), writes=[S])
            k.op("act", lambda e: e.activation(ea.ap, self.pk.t[:, PK_ALOG:PK_ALOG + 8], AF.Exp), reads=[self.pk], writes=[ea])
            cw = lambda w: self.pk.t[:, PK_CONV:PK_CONV + 96].rearrange("p (c w) -> p c w", w=4)[:, :, w:w + 1]
            pfm3 = PFM[0:3072, :].rearrange("(c p) t -> p c t", p=128)

            def load(tt):
                R = 128 if tt < 16 else 64
                r0 = tt * 128
                i = tt % 2
                if tt == 0:
                    k.op("pool", lambda e: e.memset(xc[i].t[:, :, 0:3], 0.0), writes=[xc[i]])
                    k.dma("sp", xc[i].t[:, :, 3:131], pfm3[:, :, 0:128], writes=[xc[i]])
                elif tt < 16:
                    k.dma("sp", xc[i].t[:, :, 0:131], pfm3[:, :, r0 - 3:r0 + 128], writes=[xc[i]])
                else:
                    k.dma("sp", xst.ap, pfm3[:, :, TP:TP + 64], writes=[xst])
                    k.dma("sp", cst_in.ap, self.dr["sconv"][l].rearrange("p (c s w) -> p c s w", c=24, s=16), writes=[cst_in])
                k.dma("sp", dab[i].t[0:R, :], PTM[r0:r0 + R, C_DAB:C_DAB + 16], writes=[dab[i]])
                k.dma("sp", dzb[i].t[0:R, :], PTM[r0:r0 + R, C_DZ:C_DZ + 1024], writes=[dzb[i]])

            import os
            tiles = [int(x) for x in os.environ.get("GDN_TILES", ",".join(map(str, range(NT)))).split(",")]
            lvl = int(os.environ.get("GDN_LEVEL", "9"))
            load(tiles[0])
            for ti, tt in enumerate(tiles):
                if ti + 1 < len(tiles):
                    load(tiles[ti + 1])
                R = 128 if tt < 16 else 64
                smp = tt == 16
                sfx = "_s" if smp else "_p"
                nseq = 16 if smp else 1
                i = tt % 2
                if not smp:
                    src = lambda w: xc[i].t[:, :, w:w + 128]
                    shp = [128, 24, 128]
                    va = lambda b: b.t[:, :, :]
                    rd = [xc[i]]
                else:
                    k.op("pool", lambda e: e.tensor_copy(xs.t[:, :, :, 0:3], cst_in.ap), reads=[cst_in], writes=[xs])
                    k.op("pool", lambda e: e.tensor_copy(xs.t[:, :, :, 3:7], xst.t[:, :, :].rearrange("p c (s t) -> p c s t", s=16)),
                         reads=[xst], writes=[xs])
                    src = lambda w: xs.t[:, :, :, w:w + 4]
                    shp = [128, 24, 16, 4]
                    va = lambda b: b.t[:, :, 0:64].rearrange("p c (s t) -> p c s t", s=16)
                    rd = [xs]
                    k.op("pool", lambda e: e.tensor_copy(cst_in.ap, xs.t[:, :, :, 4:7]), reads=[xs], writes=[cst_in])
                    k.dma("pool", self.dr["o_sconv"][l].rearrange("p (c s w) -> p c s w", c=24, s=16), cst_in.ap, reads=[cst_in])
                cwb = lambda w: (cw(w).to_broadcast(shp) if not smp else cw(w).unsqueeze(3).to_broadcast(shp))
                k.op("dve", lambda e: e.tensor_tensor(va(ca), src(0), cwb(0), ALU.mult), reads=rd + [self.pk], writes=[ca])
                k.op("pool", lambda e: e.tensor_tensor(va(cb), src(1), cwb(1), ALU.mult), reads=rd + [self.pk], writes=[cb])
                k.op("dve", lambda e: e.tensor_tensor(va(ca), va(ca), va(cb), ALU.add), reads=[cb, ca], writes=[ca])
                k.op("pool", lambda e: e.tensor_tensor(va(cb), src(2), cwb(2), ALU.mult), reads=rd + [self.pk], writes=[cb])
                k.op("dve", lambda e: e.tensor_tensor(va(ca), va(ca), va(cb), ALU.add), reads=[cb, ca], writes=[ca])
                k.op("pool", lambda e: e.tensor_tensor(va(cb), src(3), cwb(3), ALU.mult), reads=rd + [self.pk], writes=[cb])
                k.op("dve", lambda e: e.tensor_tensor(va(ca), va(ca), va(cb), ALU.add), reads=[cb, ca], writes=[ca])
                k.op("act", lambda e: e.activation(cc.t[:, :, 0:R], ca.t[:, :, 0:R], AF.Silu), reads=[ca], writes=[cc])
                if lvl < 2:
                    continue
                k.op("act", lambda e: e.activation(sq.t[:, :, 0:R], cc.t[:, 0:16, 0:R], AF.Square), reads=[cc], writes=[sq])
                for g0 in range(0, 16, 4):
                    ps = k.psum()
                    for c in range(4):
                        k.op("pe", lambda e: e.matmul(ps.t[:, c * 128:c * 128 + R], ones, sq.t[:, g0 + c, 0:R], start=True, stop=True),
                             reads=[sq, self.cst], writes=[ps])
                    pv = ps.t[:, :].rearrange("p (a b) -> p a b", a=4)[:, :, 0:R]
                    k.op("dve", lambda e: e.tensor_scalar(rr.t[:, g0:g0 + 4, 0:R], pv, EPS, None, ALU.add), reads=[ps], writes=[rr])
                k.op("act", lambda e: e.activation(rr2.t[:, :, 0:R], rr.t[:, :, 0:R], AF.Sqrt), reads=[rr], writes=[rr2])
                k.op("dve", lambda e: e.reciprocal(rr.t[:, :, 0:R], rr2.t[:, :, 0:R]), reads=[rr2], writes=[rr])
                k.op("dve", lambda e: e.scalar_tensor_tensor(qkn.t[:, 0:8, 0:R], cc.t[:, 0:8, 0:R], 128 ** -0.5, rr.t[:, 0:8, 0:R], ALU.mult, ALU.mult),
                     reads=[cc, rr], writes=[qkn])
                k.op("pool", lambda e: e.tensor_tensor(qkn.t[:, 8:16, 0:R], cc.t[:, 8:16, 0:R], rr.t[:, 8:16, 0:R], ALU.mult),
                     reads=[cc, rr], writes=[qkn])
                k.op("act", lambda e: e.activation(qkb.t[:, :, 0:R], qkn.t[:, :, 0:R], AF.Copy), reads=[qkn], writes=[qkb])
                for h0 in range(0, 8, 4):
                    for (srcb, c0, dst) in ((qkn, 8, ktm), (cc, 16, vtm)):
                        ps = k.psum()
                        for c in range(4):
                            k.op("pe", lambda e: e.transpose(ps.t[0:R, c * 128:(c + 1) * 128], srcb.t[:, c0 + h0 + c, 0:R], ident),
                                 reads=[srcb, self.cst], writes=[ps])
                        k.op("act", lambda e: e.activation(dst.t[0:R, h0:h0 + 4, :], ps.t[0:R, :].rearrange("p (a b) -> p a b", a=4), AF.Copy),
                             reads=[ps], writes=[dst])
                if lvl < 3:
                    continue
                da = dab[i]
                k.op("dve", lambda e: e.tensor_tensor(tg.t[0:R, :], da.t[0:R, 0:8], self.pk.t[0:R, PK_DTB:PK_DTB + 8], ALU.add),
                     reads=[da, self.pk], writes=[tg])
                k.op("act", lambda e: e.activation(eg.t[0:R, :], tg.t[0:R, :], AF.Exp), reads=[tg], writes=[eg])
                k.op("dve", lambda e: e.tensor_scalar(eg.t[0:R, :], eg.t[0:R, :], 1.0, None, ALU.add), reads=[eg], writes=[eg])
                k.op("act", lambda e: e.activation(lg.t[0:R, :], eg.t[0:R, :], AF.Ln), reads=[eg], writes=[lg])
                k.op("dve", lambda e: e.scalar_tensor_tensor(g.t[0:R, :], lg.t[0:R, :], -1.0, ea.t[0:R, :], ALU.mult, ALU.mult),
                     reads=[lg, ea], writes=[g])
                k.op("act", lambda e: e.activation(beta.t[0:R, :], da.t[0:R, 8:16], AF.Sigmoid), reads=[da], writes=[beta])
                ps = k.psum()
                k.op("pe", lambda e: e.matmul(ps.t[0:R, 0:8], self.C("MTi" + sfx, R)[:, 0:R], g.t[0:R, :], start=True, stop=True),
                     reads=[g, self.cst], writes=[ps])
                k.op("pe", lambda e: e.matmul(ps.t[0:R, 8:16], self.C("ALL" + sfx, R)[:, 0:R], g.t[0:R, :], start=True, stop=True),
                     reads=[g, self.cst], writes=[ps])
                k.op("dve", lambda e: e.tensor_copy(cum.t[0:R, :], ps.t[0:R, 0:8]), reads=[ps], writes=[cum])
                k.op("dve", lambda e: e.tensor_tensor(dcl.t[0:R, :], ps.t[0:R, 8:16], cum.t[0:R, :], ALU.subtract), reads=[ps, cum], writes=[dcl])
                k.op("act", lambda e: e.activation(ecum.t[0:R, :], cum.t[0:R, :], AF.Exp), reads=[cum], writes=[ecum])
                k.op("act", lambda e: e.activation(ekl.t[0:R, :], dcl.t[0:R, :], AF.Exp), reads=[dcl], writes=[ekl])
                k.op("dve", lambda e: e.tensor_tensor(becum.t[0:R, :], beta.t[0:R, :], ecum.t[0:R, :], ALU.mult), reads=[beta, ecum], writes=[becum])
                k.op("dve", lambda e: e.tensor_tensor(gsel.t[0:R, :, :], g.t[0:R, :].unsqueeze(1).to_broadcast([R, 16, 8]),
                                                      self.C("SEL" + sfx, R).unsqueeze(2).to_broadcast([R, 16, 8]), ALU.mult),
                     reads=[g, self.cst], writes=[gsel])
                ps = k.psum()
                k.op("pe", lambda e: e.matmul(ps.t[:, 0:128], ones[0:R, :], gsel.t[0:R, :, :].rearrange("p s h -> p (s h)"), start=True, stop=True),
                     reads=[gsel, self.cst], writes=[ps])
                k.op("act", lambda e: e.activation(ecl.t[:, :, :].rearrange("p s h -> p (s h)"), ps.t[:, 0:128], AF.Exp), reads=[ps], writes=[ecl])
                k.op("act", lambda e: e.activation(sgt.t[0:R, :], dzb[i].t[0:R, :], AF.Silu), reads=[dzb[i]], writes=[sgt])
                gn = self.pk.t[0:R, PK_GDNN:PK_GDNN + 128].unsqueeze(1).to_broadcast([R, 8, 128])
                s3 = sgt.t[0:R, :].rearrange("p (g d) -> p g d", g=8)
                k.op("pool", lambda e: e.tensor_tensor(s3, s3, gn, ALU.mult), reads=[sgt, self.pk], writes=[sgt])
                if lvl < 4:
                    continue
                def head(h, slot):
                    Gh, n1, n2, expd, expdT, ECB, decS, decTI, qeT, Lm, AqkT, rhsu, rhsw, kdec, negwT, vnew, sqo, ss, r1, r2, rstd = Gh_s[slot], n1_s[slot], n2_s[slot], expd_s[slot], expdT_s[slot], ECB_s[slot], decS_s[slot], decTI_s[slot], qeT_s[slot], Lm_s[slot], AqkT_s[slot], rhsu_s[slot], rhsw_s[slot], kdec_s[slot], negwT_s[slot], vnew_s[slot], sqo_s[slot], ss_s[slot], r1_s[slot], r2_s[slot], rstd_s[slot]
                    Pa, Pt, Rb = Pa_s[slot], Pt_s[slot], Rb_s[slot]
                    kT = qkb.t[:, 8 + h, 0:R]
                    qT = qkb.t[:, h, 0:R]
                    yield
                    k.op("dve", lambda e: e.tensor_scalar(Gh.t[0:R, :], ones[0:R, :], g.t[0:R, h:h + 1], None, ALU.mult),
                         reads=[g, self.cst], writes=[Gh])
                    psc = k.psum()
                    yield
                    k.op("pe", lambda e: e.matmul(psc.t[:, 0:R], Gh.t[0:R, :], self.C("MTi" + sfx, R)[:, 0:R], start=True, stop=True),
                         reads=[Gh, self.cst], writes=[psc])
                    cumc = cum.t[0:R, h:h + 1]
                    yield
                    k.op("dve", lambda e: e.tensor_scalar(n1.t[0:R, 0:R], psc.t[0:R, 0:R], cumc, 0.0, ALU.subtract, ALU.max),
                         reads=[psc, cum], writes=[n1])
                    yield
                    k.op("act", lambda e: e.activation(expd.t[0:R, 0:R], n1.t[0:R, 0:R], AF.Exp, scale=-1.0), reads=[n1], writes=[expd])
                    yield
                    k.op("dve", lambda e: e.tensor_scalar(n2.t[0:R, 0:R], psc.t[0:R, 0:R], cumc, 0.0, ALU.subtract, ALU.min),
                         reads=[psc, cum], writes=[n2])
                    yield
                    k.op("act", lambda e: e.activation(expdT.t[0:R, 0:R], n2.t[0:R, 0:R], AF.Exp), reads=[n2], writes=[expdT])
                    yield
                    k.op("act", lambda e: e.activation(ECB.t[:, 0:R], psc.t[:, 0:R], AF.Exp), reads=[psc], writes=[ECB])
                    yield
                    k.op("pool", lambda e: e.tensor_tensor(decS.t[0:R, 0:R], expd.t[0:R, 0:R], self.C("Ms" + sfx, R)[:, 0:R], ALU.mult),
                         reads=[expd, self.cst], writes=[decS])
                    yield
                    k.op("pool", lambda e: e.tensor_tensor(decTI.t[0:R, 0:R], expdT.t[0:R, 0:R], self.C("MTi" + sfx, R)[:, 0:R], ALU.mult),
                         reads=[expdT, self.cst], writes=[decTI])
                    yield
                    k.op("pool", lambda e: e.tensor_tensor(qeT.t[:, 0:R], qkn.t[:, h, 0:R], ECB.t[:, 0:R], ALU.mult), reads=[qkn, ECB], writes=[qeT])
                    if lvl < 5:
                        return
                    skip = os.environ.get("GDN_SKIP", "").split(",")
                    pk_ = k.psum()
                    if "kk" not in skip:
                        yield
                        k.op("pe", lambda e: e.matmul(pk_.t[0:R, 0:R], kT, kT, start=True, stop=True), reads=[qkb], writes=[pk_])
                    if "stt" not in skip:
                        yield
                        k.op("dve", lambda e: e.scalar_tensor_tensor(Lm.t[0:R, 0:R], pk_.t[0:R, 0:R], beta.t[0:R, h:h + 1], decS.t[0:R, 0:R],
                                                                     ALU.mult, ALU.mult), reads=[pk_, beta, decS], writes=[Lm])
                    pA = k.psum()
                    if "tr" not in skip:
                        yield
                        k.op("pe", lambda e: e.transpose(pA.t[0:R, 0:R], Lm.t[0:R, 0:R], ident[0:R, 0:R]), reads=[Lm, self.cst], writes=[pA])
                    P, PT, Rc = Pa[0], Lm, Rb[0]
                    if "pc" not in skip:
                        yield
                        k.op("act", lambda e: e.activation(P.t[0:R, 0:R], pA.t[0:R, 0:R], AF.Copy), reads=[pA], writes=[P])
                    if "rc" not in skip:
                        yield
                        k.op("dve", lambda e: e.tensor_tensor(Rc.t[0:R, 0:R], ident[0:R, 0:R], pA.t[0:R, 0:R], ALU.subtract),
                             reads=[pA, self.cst], writes=[Rc])
                    nlev = 1 if smp else 6
                    nlev = int(os.environ.get('GDN_NLEV', nlev))
                    for lv in range(nlev):
                        last = lv == nlev - 1
                        P2, PT2, R2 = Pa[(lv + 1) % 2], Pt[lv % 2], Rb[(lv + 1) % 2]
                        pb = k.psum()
                        yield
                        k.op("pe", lambda e: e.matmul(pb.t[0:R, 0:R], P.t[0:R, 0:R], PT.t[0:R, 0:R], start=True, stop=True),
                             reads=[P, PT], writes=[pb])
                        if not last:
                            pa_ = k.psum()
                            yield
                            k.op("pe", lambda e: e.matmul(pa_.t[0:R, 0:R], PT.t[0:R, 0:R], P.t[0:R, 0:R], start=True, stop=True),
                                 reads=[P, PT], writes=[pa_])
                        yield
                        k.op("dve", lambda e: e.tensor_copy(PT2.t[0:R, 0:R], pb.t[0:R, 0:R]), reads=[pb], writes=[PT2])
                        if not last:
                            yield
                            k.op("act", lambda e: e.activation(P2.t[0:R, 0:R], pa_.t[0:R, 0:R], AF.Copy), reads=[pa_], writes=[P2])
                        pc = k.psum()
                        yield
                        k.op("pe", lambda e: e.matmul(pc.t[0:R, 0:R], PT2.t[0:R, 0:R], Rc.t[0:R, 0:R], start=True, stop=True),
                             reads=[PT2, Rc], writes=[pc])
                        yield
                        k.op("dve", lambda e: e.tensor_tensor(R2.t[0:R, 0:R], Rc.t[0:R, 0:R], pc.t[0:R, 0:R], ALU.add), reads=[pc, Rc], writes=[R2])
                        P, PT, Rc = P2, PT2, R2
                    Rf = Rc
                    if lvl < 6:
                        return
                    pq = k.psum()
                    yield
                    k.op("pe", lambda e: e.matmul(pq.t[0:R, 0:R], kT, qT, start=True, stop=True), reads=[qkb], writes=[pq])
                    yield
                    k.op("dve", lambda e: e.tensor_tensor(AqkT.t[0:R, 0:R], pq.t[0:R, 0:R], decTI.t[0:R, 0:R], ALU.mult), reads=[pq, decTI], writes=[AqkT])
                    yield
                    k.op("dve", lambda e: e.tensor_scalar(rhsu.t[0:R, :], vtm.t[0:R, h, :], beta.t[0:R, h:h + 1], None, ALU.mult),
                         reads=[vtm, beta], writes=[rhsu])
                    yield
                    k.op("dve", lambda e: e.tensor_scalar(rhsw.t[0:R, :], ktm.t[0:R, h, :], becum.t[0:R, h:h + 1], None, ALU.mult),
                         reads=[ktm, becum], writes=[rhsw])
                    yield
                    k.op("dve", lambda e: e.tensor_scalar(kdec.t[0:R, :], ktm.t[0:R, h, :], ekl.t[0:R, h:h + 1], None, ALU.mult),
                         reads=[ktm, ekl], writes=[kdec])
                    pw = k.psum()
                    yield
                    k.op("pe", lambda e: e.matmul(pw.t[:, 0:R], rhsw.t[0:R, :], Rf.t[0:R, 0:R], start=True, stop=True), reads=[rhsw, Rf], writes=[pw])
                    yield
                    k.op("act", lambda e: e.activation(negwT.t[:, 0:R], pw.t[:, 0:R], AF.Copy, scale=-1.0), reads=[pw], writes=[negwT])
                    if smp:
                        yield
                        k.dma("sp", Sall_v, self.dr["sgdn"][l, :, h].rearrange("s d e -> d s e"), writes=[Sall])
                        bmv = self.C("BM").rearrange("p (s i) -> p s i", s=16)
                        yield
                        k.op("dve", lambda e: e.tensor_tensor(wbig_v, negwT.t[:, 0:64].unsqueeze(1).to_broadcast([128, 16, 64]), bmv, ALU.mult),
                             reads=[negwT, self.cst], writes=[wbig])
                        yield
                        k.op("pool", lambda e: e.tensor_tensor(qbig_v, qeT.t[:, 0:64].unsqueeze(1).to_broadcast([128, 16, 64]), bmv, ALU.mult),
                             reads=[qeT, self.cst], writes=[qbig])
                        yield
                        k.op("dve", lambda e: e.tensor_tensor(kbig_v[0:64, :, :], kdec.t[0:64, :].unsqueeze(1).to_broadcast([64, 16, 128]),
                                                              self.C("SEL_s", 64).unsqueeze(2).to_broadcast([64, 16, 128]), ALU.mult),
                             reads=[kdec, self.cst], writes=[kbig])
                    pv_ = k.psum()
                    yield
                    k.op("pe", lambda e: e.matmul(pv_.t[0:R, 0:128], Rf.t[0:R, 0:R], rhsu.t[0:R, :], start=True, stop=False), reads=[Rf, rhsu], writes=[pv_])
                    if not smp:
                        yield
                        k.op("pe", lambda e: e.matmul(pv_.t[0:R, 0:128], negwT.t[:, 0:R], S.t[:, h, :], start=False, stop=True),
                             reads=[negwT, S], writes=[pv_])
                    else:
                        for s in range(16):
                            yield
                            k.op("pe", lambda e: e.matmul(pv_.t[0:R, 0:128], wbig_v[:, s, :], Sall_v[:, s, :], start=False, stop=(s == 15)),
                                 reads=[wbig, Sall], writes=[pv_])
                    yield
                    k.op("act", lambda e: e.activation(vnew.t[0:R, :], pv_.t[0:R, 0:128], AF.Copy), reads=[pv_], writes=[vnew])
                    if lvl < 7:
                        return
                    po = k.psum()
                    yield
                    k.op("pe", lambda e: e.matmul(po.t[0:R, 0:128], AqkT.t[0:R, 0:R], vnew.t[0:R, :], start=True, stop=False), reads=[AqkT, vnew], writes=[po])
                    if not smp:
                        yield
                        k.op("pe", lambda e: e.matmul(po.t[0:R, 0:128], qeT.t[:, 0:R], S.t[:, h, :], start=False, stop=True), reads=[qeT, S], writes=[po])
                    else:
                        for s in range(16):
                            yield
                            k.op("pe", lambda e: e.matmul(po.t[0:R, 0:128], qbig_v[:, s, :], Sall_v[:, s, :], start=False, stop=(s == 15)),
                                 reads=[qbig, Sall], writes=[po])
                    yield
                    k.op("act", lambda e: e.activation(sqo.t[0:R, :], po.t[0:R, 0:128], AF.Square), reads=[po], writes=[sqo])
                    yield
                    k.op("dve", lambda e: e.tensor_reduce(ss.t[0:R, h:h + 1], sqo.t[0:R, :], X, ALU.add), reads=[sqo], writes=[ss])
                    yield
                    k.op("dve", lambda e: e.tensor_scalar(r1.t[0:R, h:h + 1], ss.t[0:R, h:h + 1], 1.0 / 128, EPS, ALU.mult, ALU.add), reads=[ss], writes=[r1])
                    yield
                    k.op("act", lambda e: e.activation(r2.t[0:R, h:h + 1], r1.t[0:R, h:h + 1], AF.Sqrt), reads=[r1], writes=[r2])
                    yield
                    k.op("dve", lambda e: e.reciprocal(rstd.t[0:R, h:h + 1], r2.t[0:R, h:h + 1]), reads=[r2], writes=[rstd])
                    yield
                    k.op("dve", lambda e: e.scalar_tensor_tensor(y.t[0:R, h * 128:(h + 1) * 128], po.t[0:R, 0:128], rstd.t[0:R, h:h + 1],
                                                                 sgt.t[0:R, h * 128:(h + 1) * 128], ALU.mult, ALU.mult),
                         reads=[po, rstd, sgt], writes=[y])
                    if not smp:
                        pS = k.psum()
                        yield
                        k.op("pe", lambda e: e.matmul(pS.t[:, 0:128], kdec.t[0:R, :], vnew.t[0:R, :], start=True, stop=True), reads=[kdec, vnew], writes=[pS])
                        yield
                        k.op("dve", lambda e: e.scalar_tensor_tensor(S.t[:, h, :], S.t[:, h, :], ecl.t[:, 0, h:h + 1], pS.t[:, 0:128], ALU.mult, ALU.add),
                             reads=[pS, S, ecl], writes=[S])
                        if tt == 15:
                            yield
                            k.dma("pool", self.dr["o_pgdn"][l, h], S.t[:, h, :], reads=[S])
                    else:
                        for s in range(16):
                            pS = k.psum()
                            yield
                            k.op("pe", lambda e: e.matmul(pS.t[:, 0:128], kbig_v[0:64, s, :], vnew.t[0:64, :], start=True, stop=True),
                                 reads=[kbig, vnew], writes=[pS])
                            yield
                            k.op("dve", lambda e: e.scalar_tensor_tensor(Snew_v[:, s, :], Sall_v[:, s, :], ecl.t[:, s, h:h + 1], pS.t[:, 0:128],
                                                                         ALU.mult, ALU.add), reads=[pS, Sall, ecl], writes=[Snew])
                        yield
                        k.dma("pool", self.dr["o_sgdn"][l, :, h].rearrange("s d e -> d s e"), Snew_v, reads=[Snew])

                    yield
                nh = int(os.environ.get('GDN_HEADS', '8'))
                if smp or nh < 2 or os.environ.get("GDN_SEQ"):
                    for h in range(nh):
                        for _ in head(h, 0):
                            pass
                else:
                    for h0 in range(0, nh, NSLOT):
                        gens = [head(h0 + j, j) for j in range(NSLOT)]
                        alive = list(gens)
                        while alive:
                            for gq in list(alive):
                                try:
                                    next(gq)
                                except StopIteration:
                                    alive.remove(gq)
                yt = yT[tt % 2]
                self.transposes(lambda c: y.t[0:R, c * 128:(c + 1) * 128], 8, yt, R, [y])
                k.dma("pool", self.dr["YBT"][1024:2048, tt * 128:tt * 128 + R].rearrange("(c p) t -> p c t", p=128),
                      yt.t[:, :, 0:R], reads=[yt])
            if lvl >= 9:
                k.dma("sp", self.dr["o_pconv"][l], PFM[0:3072, TP - 3:TP])
            k.barrier()


_PROG = {}


def _get_prog():
    if "nc" not in _PROG:
        nc = bass.Bass("TRN2", target_bir_lowering=False)
        m = Model(nc)
        m.build()
        _PROG["nc"] = nc
        _PROG["ninst"] = m.k.ninst
    return _PROG["nc"]


def _cossin():
    inv = 1.0 / (10000.0 ** np.linspace(0.0, 1.0, 64, dtype=np.float32)).astype(np.float32)
    pos = np.concatenate([np.arange(TP, dtype=np.float32),
                          np.tile(np.arange(4, dtype=np.float32) + np.float32(16384.0), 16)])
    ang = (pos[:, None] * inv[None, :]).astype(np.float32)
    return np.concatenate([np.cos(ang), np.sin(ang)], axis=1).astype(np.float32)


def kernel(x_prompt, x_sample, state_ret, state_gdn, state_gdn_conv, state_gla, norm_mix, norm_ffn,
           norm_final, w_in, b_merge, gdn_conv_w, gdn_a_log, gdn_dt_bias, gdn_norm, gla_w_up, gla_b_up,
           gla_norm, w_branch, w_o, w_gate_up, w_down):
    f = lambda a: np.ascontiguousarray(np.asarray(a, dtype=np.float32))
    x_prompt, x_sample = f(x_prompt), f(x_sample)
    w_in = np.asarray(w_in, dtype=np.float32)
    wtm = np.zeros((DEPTH, D, TMW), np.float32)
    wtm[:, :, :7200] = w_in[:, :, TM_IDX]
    wfm = np.ascontiguousarray(w_in[:, :, FM_IDX])
    wgu = np.asarray(w_gate_up, dtype=np.float32).reshape(DEPTH, D, 2, DFF // 128, 128)
    wgu = np.ascontiguousarray(wgu.transpose(0, 1, 3, 2, 4)).reshape(DEPTH, D, 2 * DFF)
    pack = np.zeros((DEPTH, 128, NPK), np.float32)
    col = lambda v, n: np.asarray(v, np.float32).reshape(n, 128).T
    for l in range(DEPTH):
        pack[l, :, PK_GMIX:PK_GMIX + 16] = col(norm_mix[l], 16)
        pack[l, :, PK_GFFN:PK_GFFN + 16] = col(norm_ffn[l], 16)
        cwl = np.asarray(gdn_conv_w[l], np.float32)
        pack[l, :, PK_CONV:PK_CONV + 96] = cwl.reshape(4, 24, 128).transpose(2, 1, 0).reshape(128, 96)
        pack[l, :, PK_BM:PK_BM + 48] = col(b_merge[l], 48)
        pack[l, :, PK_ALOG:PK_ALOG + 8] = np.asarray(gdn_a_log[l], np.float32)[None, :]
        pack[l, :, PK_DTB:PK_DTB + 8] = np.asarray(gdn_dt_bias[l], np.float32)[None, :]
        pack[l, :, PK_GDNN:PK_GDNN + 128] = np.asarray(gdn_norm[l], np.float32)[None, :]
        pack[l, :, PK_GLAN:PK_GLAN + 256] = np.asarray(gla_norm[l], np.float32)[None, :]
        pack[l, :, PK_BUP:PK_BUP + 512] = np.asarray(gla_b_up[l], np.float32)[None, :]
        pack[l, 0:16, PK_WUP:PK_WUP + 512] = np.asarray(gla_w_up[l], np.float32)
        pack[l, :, PK_GFIN:PK_GFIN + 16] = col(norm_final, 16)
    cossin = _cossin()
    shared = {"wtm": wtm, "wfm": wfm, "wbr": f(w_branch), "wo": f(w_o), "wgu": wgu, "wdn": f(w_down),
              "pack": pack, "cst": CST_ARR, "cossin": cossin}
    state_ret, state_gdn, state_gla = f(state_ret), f(state_gdn), f(state_gla)
    sconv = f(state_gdn_conv)
    in_maps = []
    for c in range(NCORES):
        xs = x_sample[16 * c:16 * (c + 1)].reshape(64, D)
        xT = np.ascontiguousarray(np.concatenate([x_prompt[c % 4], xs], axis=0).T)
        sc = sconv[:, 16 * c:16 * (c + 1)]
        sc = sc.reshape(DEPTH, 16, 3, 24, 128).transpose(0, 4, 3, 1, 2)
        m = dict(shared)
        m.update({"xT": xT,
                  "sret": np.ascontiguousarray(state_ret[:, 16 * c:16 * (c + 1)]),
                  "sgdn": np.ascontiguousarray(state_gdn[:, 16 * c:16 * (c + 1)]),
                  "sgla": np.ascontiguousarray(state_gla[:, 16 * c:16 * (c + 1)]),
                  "sconv": np.ascontiguousarray(sc).reshape(DEPTH, 128, 24 * 16 * 3)})
        in_maps.append(m)
    nc = _get_prog()
    res = run_bass_kernel_spmd(nc, in_maps, core_ids=list(range(NCORES))).results
    y_prompt = np.stack([res[c]["yT"][:, :TP].T for c in range(4)])
    y_sample = np.concatenate([res[c]["yT"][:, TP:].T.reshape(16, 4, D) for c in range(NCORES)])
    p_ret = np.stack([res[c]["o_pret"] for c in range(4)], axis=1)
    p_gdn = np.stack([res[c]["o_pgdn"] for c in range(4)], axis=1)
    p_conv = np.stack([res[c]["o_pconv"].transpose(0, 2, 1) for c in range(4)], axis=1)
    p_gla = np.stack([res[c]["o_pgla"] for c in range(4)], axis=1)
    s_ret = np.concatenate([res[c]["o_sret"] for c in range(NCORES)], axis=1)
    s_gdn = np.concatenate([res[c]["o_sgdn"] for c in range(NCORES)], axis=1)
    s_gla = np.concatenate([res[c]["o_sgla"] for c in range(NCORES)], axis=1)
    s_conv = np.concatenate([res[c]["o_sconv"].reshape(DEPTH, 128, 24, 16, 3).transpose(0, 3, 4, 2, 1).reshape(DEPTH, 16, 3, 3072)
                             for c in range(NCORES)], axis=1)
    outs = (y_prompt, y_sample, p_ret, p_gdn, p_conv, p_gla, s_ret, s_gdn, s_conv, s_gla)
    return tuple(np.ascontiguousarray(o, dtype=np.float32) for o in outs)
```
